# Optimizing a Trainium2 kernel written in Bass

```python
import jax, jax.numpy as jnp
from jax import lax
import numpy as np

D_MODEL = 2048
BATCH = 4
SEQ = 4096
DEPTH = 4

MIX_WIDTH = D_MODEL
EPS = 1e-6
BLOCK = 128
NEG = -1e30

SWA_WIDTH = MIX_WIDTH // 2
SWA_HEAD_DIM = 64
SWA_Q_HEADS = SWA_WIDTH // SWA_HEAD_DIM
SWA_KV_HEADS = 2
SWA_GROUP = SWA_Q_HEADS // SWA_KV_HEADS
WINDOW = 128

MLA_WIDTH = MIX_WIDTH - SWA_WIDTH
MLA_V_DIM = 128
MLA_HEADS = MLA_WIDTH // MLA_V_DIM
MLA_NOPE_DIM = 128
MLA_ROPE_DIM = 64
MLA_QK_DIM = MLA_NOPE_DIM + MLA_ROPE_DIM
Q_LORA_RANK = 384
KV_LORA_RANK = 256
ROPE_THETA = 10000.0

A_Q = SWA_Q_HEADS * SWA_HEAD_DIM
A_KV = SWA_KV_HEADS * SWA_HEAD_DIM
A_GATE = SWA_WIDTH
B_GATE = MLA_WIDTH
IN_WIDTH = A_Q + 2 * A_KV + A_GATE + Q_LORA_RANK + KV_LORA_RANK + MLA_ROPE_DIM + B_GATE
SPLIT_POINTS = (
    A_Q,
    A_Q + A_KV,
    A_Q + 2 * A_KV,
    A_Q + 2 * A_KV + A_GATE,
    A_Q + 2 * A_KV + A_GATE + Q_LORA_RANK,
    A_Q + 2 * A_KV + A_GATE + Q_LORA_RANK + KV_LORA_RANK,
    A_Q + 2 * A_KV + A_GATE + Q_LORA_RANK + KV_LORA_RANK + MLA_ROPE_DIM,
)

kernel_name = "hybrid_swa_sink_alibi_mla_gated_trunk"


def rmsnorm(x, g):
    xf = x.astype(jnp.float32)
    y = xf * lax.rsqrt(jnp.mean(xf * xf, axis=-1, keepdims=True) + EPS) * g.astype(jnp.float32)
    return y.astype(x.dtype)


def rope(x, cos, sin):
    half = x.shape[-1] // 2
    x1, x2 = x[..., :half], x[..., half:]
    cos = cos.astype(x.dtype)
    sin = sin.astype(x.dtype)
    return jnp.concatenate([x1 * cos - x2 * sin, x2 * cos + x1 * sin], axis=-1)


def swa_attention(q, k, v, sinks):
    b, s = q.shape[0], q.shape[1]
    nb = s // BLOCK
    qb = q.reshape(b, nb, BLOCK, SWA_KV_HEADS, SWA_GROUP, SWA_HEAD_DIM)
    pad = ((0, 0), (BLOCK, 0), (0, 0), (0, 0))
    kp = jnp.pad(k, pad).reshape(b, nb + 1, BLOCK, SWA_KV_HEADS, SWA_HEAD_DIM)
    vp = jnp.pad(v, pad).reshape(b, nb + 1, BLOCK, SWA_KV_HEADS, SWA_HEAD_DIM)
    kk = jnp.concatenate([kp[:, :-1], kp[:, 1:]], axis=2)
    vv = jnp.concatenate([vp[:, :-1], vp[:, 1:]], axis=2)
    scores = jnp.einsum('bnqhgd,bnkhd->bnhgqk', qb, kk).astype(jnp.float32) * (SWA_HEAD_DIM ** -0.5)
    qi = jnp.arange(BLOCK)[:, None]
    ki = jnp.arange(2 * BLOCK)[None, :]
    delta = BLOCK + qi - ki
    key_pos = (jnp.arange(nb)[:, None] - 1) * BLOCK + jnp.arange(2 * BLOCK)[None, :]
    valid = ((delta >= 0) & (delta < WINDOW))[None] & (key_pos >= 0)[:, None, :]
    slopes = jnp.exp2(-8.0 * jnp.arange(1, SWA_Q_HEADS + 1, dtype=jnp.float32) / SWA_Q_HEADS)
    slopes = slopes.reshape(SWA_KV_HEADS, SWA_GROUP)
    alibi = -slopes[:, :, None, None] * delta.astype(jnp.float32)[None, None]
    scores = jnp.where(valid[None, :, None, None], scores + alibi, NEG)
    sink = sinks.astype(jnp.float32).reshape(SWA_KV_HEADS, SWA_GROUP, 1, 1)
    sink = jnp.broadcast_to(sink, scores.shape[:-1] + (1,))
    probs = jax.nn.softmax(jnp.concatenate([scores, sink], axis=-1), axis=-1)[..., :-1]
    out = jnp.einsum('bnhgqk,bnkhd->bnqhgd', probs.astype(v.dtype), vv)
    return out.reshape(b, s, SWA_WIDTH)


def mla_attention(c_q, c_kv, k_rope, q_a_g, kv_a_g, w_q_b, w_kv_b, cos, sin):
    b, s = c_q.shape[0], c_q.shape[1]
    q = (rmsnorm(c_q, q_a_g) @ w_q_b).reshape(b, s, MLA_HEADS, MLA_QK_DIM)
    q_nope = q[..., :MLA_NOPE_DIM]
    q_rope = rope(q[..., MLA_NOPE_DIM:], cos[:, None, :], sin[:, None, :])
    kv = (rmsnorm(c_kv, kv_a_g) @ w_kv_b).reshape(b, s, MLA_HEADS, MLA_NOPE_DIM + MLA_V_DIM)
    k_nope = kv[..., :MLA_NOPE_DIM]
    v = kv[..., MLA_NOPE_DIM:]
    k_r = rope(k_rope, cos, sin)
    nb = s // BLOCK
    qn_b = q_nope.reshape(b, nb, BLOCK, MLA_HEADS, MLA_NOPE_DIM).transpose(1, 0, 2, 3, 4)
    qr_b = q_rope.reshape(b, nb, BLOCK, MLA_HEADS, MLA_ROPE_DIM).transpose(1, 0, 2, 3, 4)
    key_pos = jnp.arange(s)
    scale = MLA_QK_DIM ** -0.5

    def one_block(args):
        qn, qr, i = args
        sc = (jnp.einsum('bqhd,bkhd->bhqk', qn, k_nope)
              + jnp.einsum('bqhd,bkd->bhqk', qr, k_r)).astype(jnp.float32) * scale
        q_pos = i * BLOCK + jnp.arange(BLOCK)
        mask = key_pos[None, :] <= q_pos[:, None]
        p = jax.nn.softmax(jnp.where(mask, sc, NEG), axis=-1)
        return jnp.einsum('bhqk,bkhd->bqhd', p.astype(v.dtype), v)

    out = lax.map(one_block, (qn_b, qr_b, jnp.arange(nb)))
    return out.transpose(1, 0, 2, 3, 4).reshape(b, s, MLA_WIDTH)


def setup_inputs(seed: int = 0) -> dict:
    key = jax.random.key(seed)
    ks = jax.random.split(key, 11)
    f32 = jnp.float32
    x = jax.random.normal(ks[0], (BATCH, SEQ, D_MODEL), f32)
    attn_norm_g = 1.0 + 0.02 * jax.random.normal(ks[1], (DEPTH, D_MODEL), f32)
    w_in = jax.random.normal(ks[2], (DEPTH, D_MODEL, IN_WIDTH), f32) * D_MODEL ** -0.5
    swa_sinks = 0.5 * jax.random.normal(ks[3], (DEPTH, SWA_Q_HEADS), f32)
    q_a_norm_g = 1.0 + 0.02 * jax.random.normal(ks[4], (DEPTH, Q_LORA_RANK), f32)
    kv_a_norm_g = 1.0 + 0.02 * jax.random.normal(ks[5], (DEPTH, KV_LORA_RANK), f32)
    w_q_b = jax.random.normal(ks[6], (DEPTH, Q_LORA_RANK, MLA_HEADS * MLA_QK_DIM), f32) * Q_LORA_RANK ** -0.5
    w_kv_b = jax.random.normal(ks[7], (DEPTH, KV_LORA_RANK, MLA_HEADS * (MLA_NOPE_DIM + MLA_V_DIM)), f32) * KV_LORA_RANK ** -0.5
    w_out = jax.random.normal(ks[8], (DEPTH, MIX_WIDTH, D_MODEL), f32) * MIX_WIDTH ** -0.5
    final_norm_g = 1.0 + 0.02 * jax.random.normal(ks[9], (D_MODEL,), f32)
    return {"x": x, "attn_norm_g": attn_norm_g, "w_in": w_in, "swa_sinks": swa_sinks,
            "q_a_norm_g": q_a_norm_g, "kv_a_norm_g": kv_a_norm_g, "w_q_b": w_q_b,
            "w_kv_b": w_kv_b, "w_out": w_out, "final_norm_g": final_norm_g}


def reference(x, attn_norm_g, w_in, swa_sinks, q_a_norm_g, kv_a_norm_g, w_q_b, w_kv_b, w_out, final_norm_g):
    b, s = x.shape[0], x.shape[1]
    pos = jnp.arange(s, dtype=jnp.float32)
    inv_freq = ROPE_THETA ** (-jnp.arange(0, MLA_ROPE_DIM, 2, dtype=jnp.float32) / MLA_ROPE_DIM)
    ang = pos[:, None] * inv_freq[None, :]
    cos, sin = jnp.cos(ang), jnp.sin(ang)
    for l in range(DEPTH):
        h = rmsnorm(x, attn_norm_g[l])
        proj = h @ w_in[l]
        qa, ka, va, ga, cq, ckv, kr, gb = jnp.split(proj, SPLIT_POINTS, axis=-1)
        ya = swa_attention(qa.reshape(b, s, SWA_Q_HEADS, SWA_HEAD_DIM),
                           ka.reshape(b, s, SWA_KV_HEADS, SWA_HEAD_DIM),
                           va.reshape(b, s, SWA_KV_HEADS, SWA_HEAD_DIM),
                           swa_sinks[l]) * jax.nn.silu(ga)
        yb = mla_attention(cq, ckv, kr, q_a_norm_g[l], kv_a_norm_g[l], w_q_b[l], w_kv_b[l],
                           cos, sin) * jax.nn.silu(gb)
        x = x + jnp.concatenate([ya, yb], axis=-1) @ w_out[l]
    return rmsnorm(x, final_norm_g)
```

```python
import contextlib
import numpy as np
import ml_dtypes
import concourse.bass as bass
import concourse.mybir as mybir
from concourse.bass_utils import run_bass_kernel_spmd

F32 = mybir.dt.float32
BF16 = mybir.dt.bfloat16
AF = mybir.ActivationFunctionType
ALU = mybir.AluOpType

D = 2048
NCH = 16
EPS = 1e-6
QL = 384
KVL = 256
NPIECE = 32
PAIRS = [[0, 1], [2, 3], [4, 5], [6, 7]]


class Dep:
    __slots__ = ("w", "r", "ds")

    def __init__(self):
        self.w = None
        self.r = {}
        self.ds = None


class KB:
    def __init__(self, nc, es, n_dsem=70):
        self.nc = nc
        self.eng = {"pe": nc.tensor, "act": nc.scalar, "dve": nc.vector,
                    "pool": nc.gpsimd, "sp": nc.sync}
        self.sem = {e: es.enter_context(nc.semaphore("s_" + e)) for e in ("pe", "act", "dve", "pool")}
        self.cnt = {e: 0 for e in self.sem}
        self.known = {e: {} for e in self.eng}
        self.dsems = [[es.enter_context(nc.semaphore("d%d" % i)), 0] for i in range(n_dsem)]
        self.dfree = list(range(n_dsem))
        self.ccsem = es.enter_context(nc.semaphore("ccs"))
        self.cccnt = 0

    def _wait(self, e, ev):
        if ev is None:
            return
        s, v = ev
        k = self.known[e]
        if k.get(id(s), 0) >= v:
            return
        if e == "pe" and s is self.sem["pe"]:
            return
        self.eng[e].wait_ge(s, v)
        k[id(s)] = v

    def _deps(self, e, R, W):
        for d in R:
            self._wait(e, d.w)
        for d in W:
            self._wait(e, d.w)
            for ev in d.r.values():
                self._wait(e, ev)

    def op(self, e, fn, R=(), W=(), nowait=False):
        if not nowait:
            self._deps(e, R, W)
        ins = fn(self.eng[e])
        self.cnt[e] += 1
        ins.then_inc(self.sem[e], 1)
        ev = (self.sem[e], self.cnt[e])
        for d in R:
            d.r[e] = ev
        for d in W:
            d.w = ev
            d.r = {}
        return ev

    def dsem_alloc(self, dep):
        dep.ds = self.dfree.pop()
        return dep

    def dsem_free(self, dep):
        if dep.ds is not None:
            self.dfree.append(dep.ds)
            dep.ds = None

    def dma(self, pairs, R=(), W=(), st=None, q="sp"):
        if st.ds is None:
            self.dsem_alloc(st)
        self._deps(q, R, W)
        slot = self.dsems[st.ds]
        for (o, i) in pairs:
            ins = self.eng[q].dma_start(out=o, in_=i)
            slot[1] += 16
            ins.then_inc(slot[0], 16)
        ev = (slot[0], slot[1])
        for d in R:
            d.r[("d", st.ds)] = ev
        for d in W:
            d.w = ev
            d.r = {}
        return ev

    def allgather(self, src, dst, R=(), W=()):
        self._deps("pool", R, W)
        ins = self.nc.gpsimd.collective_compute("AllGather", ALU.bypass, replica_groups=PAIRS,
                                                ins=[src], outs=[dst])
        self.cccnt += 1
        ins.then_inc(self.ccsem, 1)
        ev = (self.ccsem, self.cccnt)
        for d in R:
            d.r["cc"] = ev
        for d in W:
            d.w = ev
            d.r = {}

    def barrier(self):
        evs = [(self.sem[e], self.cnt[e]) for e in self.sem if self.cnt[e] > 0]
        evs += [(s, c) for (s, c) in self.dsems if c > 0]
        if self.cccnt:
            evs.append((self.ccsem, self.cccnt))
        for e in self.eng:
            for ev in evs:
                self._wait(e, ev)


class Ph:
    uid = 0

    def __init__(self, kb):
        self.kb = kb
        self.es = contextlib.ExitStack()
        self.deps = []

    def sb(self, name, shape, dt):
        Ph.uid += 1
        t = self.es.enter_context(self.kb.nc.sbuf_tensor("sb%d_%s" % (Ph.uid, name), list(shape), dt))
        return t

    def dep(self):
        d = Dep()
        self.deps.append(d)
        return d

    def close(self):
        self.kb.barrier()
        for d in self.deps:
            self.kb.dsem_free(d)
        self.es.close()


def build(NS=16, L=4):
    T = NS * 128
    NG = NS // 4
    nc = bass.Bass("TRN2", target_bir_lowering=False)
    dt = nc.dram_tensor
    x_in = dt("x", [T, D], F32, kind="ExternalInput").ap()
    win = dt("win", [L, NPIECE, 128, NCH, 128], F32, kind="ExternalInput").ap()
    wq_d = dt("wq", [L, 128, 3, 2048], F32, kind="ExternalInput").ap()
    wkv_d = dt("wkv", [L, 128, 2, 2048], F32, kind="ExternalInput").ap()
    wo_d = dt("wo", [L, 16, 128, NCH, 128], F32, kind="ExternalInput").ap()
    gin_d = dt("gin", [L, 128, NCH], F32, kind="ExternalInput").ap()
    gq_d = dt("gq", [L, 128, 3], F32, kind="ExternalInput").ap()
    gkv_d = dt("gkv", [L, 128, 2], F32, kind="ExternalInput").ap()
    sink_d = dt("sinks", [L, 128, 16], F32, kind="ExternalInput").ap()
    gfin_d = dt("gfin", [128, D], F32, kind="ExternalInput").ap()
    cs_d = dt("cs", [2, 64, T], F32, kind="ExternalInput").ap()
    etab_d = dt("etab", [128, 3, 16, 128], BF16, kind="ExternalInput").ap()
    mask_d = dt("mask", [128, 2, 128], BF16, kind="ExternalInput").ap()
    ident_d = dt("ident", [128, 128], BF16, kind="ExternalInput").ap()
    out_d = dt("out", [T, D], F32, kind="ExternalOutput").ap()
    QA = dt("QA", [8, 128, T], BF16, kind="Internal").ap()
    GT = dt("GT", [T, 2048], BF16, kind="Internal").ap()
    GTB = dt("GTB", [8, 128, T], BF16, kind="Internal").ap()
    XS = [dt("X%d" % i, [T, D], F32, kind="Internal").ap() for i in range(2)]
    SND = [dt("SND%d" % k, [128, T], BF16).ap() for k in range(5)]
    RCV = [dt("RCV%d" % k, [256, T], BF16).ap() for k in range(5)]
    O_CK, O_KR, O_KA, O_VA = 0, 2 * T, 3 * T, 4 * T

    es = contextlib.ExitStack()
    kb = KB(nc, es)
    op, dma = kb.op, kb.dma

    PS = [es.enter_context(nc.psum_tensor("ps%d" % i, [128, 512], F32)) for i in range(8)]
    PD = [Dep() for _ in range(8)]
    gp = Ph(kb)
    ident = gp.sb("ident", [128, 128], BF16); ident_dp = gp.dep()
    cs = gp.sb("cs", [64, 2, T], F32); cs_dp = gp.dep()
    mask = gp.sb("mask", [128, 2, 128], BF16); mask_dp = gp.dep()
    cqnT = gp.sb("cqnT", [128, 3, T], BF16); cqnT_dp = [gp.dep() for _ in range(NS)]
    small = gp.sb("small", [128, 64], F32)
    small_dp = gp.dep()
    esink_dp = gp.dep()
    dma([(ident[:], ident_d)], W=[ident_dp], st=ident_dp)
    dma([(cs[:, 0, :], cs_d[0]), (cs[:, 1, :], cs_d[1])], W=[cs_dp], st=cs_dp)
    dma([(mask[:], mask_d)], W=[mask_dp], st=mask_dp)

    x_dp = [[Dep() for _ in range(NS)] for _ in range(3)]
    qa_dp = Dep(); gt_dp = [Dep() for _ in range(NS)]; yt_dp = Dep()
    snd_dp = [Dep() for _ in range(5)]; rcv_dp = [Dep() for _ in range(5)]
    gtb_dp = Dep()

    def bf(ps_ap):
        return ps_ap.bitcast(BF16)

    rr = {"ps": 0, "ev": 0}

    def evac_engine():
        rr["ev"] += 1
        return "act" if rr["ev"] % 2 else "dve"

    def copy(e, out, in_, R, W):
        if e == "act":
            return op("act", lambda g: g.copy(out=out, in_=in_), R=R, W=W)
        return op(e, lambda g: g.tensor_copy(out=out, in_=in_), R=R, W=W)

    for l in range(L):
        x_src = x_in if l == 0 else XS[(l - 1) % 2]
        x_src_dp = x_dp[0] if l == 0 else x_dp[1 + (l - 1) % 2]
        x_dst = XS[l % 2]
        x_dst_dp = x_dp[1 + l % 2]

        dma([(small[:, 0:16], gin_d[l]), (small[:, 16:19], gq_d[l]), (small[:, 19:21], gkv_d[l]),
             (small[:, 24:40], sink_d[l])], W=[small_dp], st=small_dp)
        op("act", lambda g: g.activation(out=small[:, 40:56], in_=small[:, 24:40], func=AF.Exp),
           R=[small_dp], W=[esink_dp])

        pa = Ph(kb)
        hT = pa.sb("hT", [128, NCH, T], BF16)
        hT_dp = [pa.dep() for _ in range(NS)]
        p1 = Ph(kb)
        xs = [p1.sb("xs%d" % i, [128, D], F32) for i in range(2)]; xs_dp = [p1.dep() for _ in range(2)]
        hb = [p1.sb("hb%d" % i, [128, D], BF16) for i in range(2)]; hb_dp = [p1.dep() for _ in range(2)]
        junk = p1.sb("junk", [128, D], BF16); junk_dp = p1.dep()
        st1 = p1.sb("st1", [128, 4 * NS], F32); st1_dp = [p1.dep() for _ in range(NS)]
        for s in range(NS):
            b = s % 2
            dma([(xs[b][:], x_src[s * 128:(s + 1) * 128, :])], R=[x_src_dp[s]], W=[xs_dp[b]], st=xs_dp[b])
            c0 = 4 * s
            op("act", lambda g: g.activation(out=junk[:], in_=xs[b][:], func=AF.Square,
                                             accum_out=st1[:, c0:c0 + 1]),
               R=[xs_dp[b]], W=[junk_dp, st1_dp[s]])
            op("dve", lambda g: g.tensor_scalar(out=st1[:, c0 + 1:c0 + 2], in0=st1[:, c0:c0 + 1],
                                                scalar1=1.0 / D, scalar2=EPS, op0=ALU.mult, op1=ALU.add),
               R=[], W=[st1_dp[s]])
            op("act", lambda g: g.activation(out=st1[:, c0 + 2:c0 + 3], in_=st1[:, c0 + 1:c0 + 2], func=AF.Sqrt),
               W=[st1_dp[s]])
            op("dve", lambda g: g.reciprocal(out=st1[:, c0 + 3:c0 + 4], in_=st1[:, c0 + 2:c0 + 3]),
               W=[st1_dp[s]])
            op("dve", lambda g: g.tensor_scalar(out=hb[b][:], in0=xs[b][:], scalar1=st1[:, c0 + 3:c0 + 4],
                                                scalar2=None, op0=ALU.mult),
               R=[xs_dp[b], st1_dp[s]], W=[hb_dp[b]])
            for cg in range(4):
                pi = rr["ps"] % 4; rr["ps"] += 1
                pv = bf(PS[pi][:])[:, 0:512].rearrange("p (a b) -> p a b", a=4)
                for a in range(4):
                    c = cg * 4 + a
                    op("pe", lambda g: g.transpose(out=pv[:, a, :], in_=hb[b][:, c * 128:(c + 1) * 128],
                                                   identity=ident[:]),
                       R=[hb_dp[b], ident_dp], W=[PD[pi]])
                copy(evac_engine(), hT[:, cg * 4:(cg + 1) * 4, s * 128:(s + 1) * 128], pv,
                     R=[PD[pi]], W=[hT_dp[s]])
        p1.close()

        p2 = Ph(kb)
        wst = [p2.sb("wst%d" % i, [128, NCH, 128], F32) for i in range(3)]; wst_dp = [p2.dep() for _ in range(3)]
        wb = [p2.sb("wb%d" % i, [128, NCH, 128], BF16) for i in range(2)]; wb_dp = [p2.dep() for _ in range(2)]
        wt = [p2.sb("wt%d" % i, [128, NCH, 512], BF16) for i in range(2)]; wt_dp = [p2.dep() for _ in range(2)]
        ofm = [p2.sb("ofm%d" % i, [128, 512], BF16) for i in range(3)]; ofm_dp = [p2.dep() for _ in range(3)]
        otm = [p2.sb("otm%d" % i, [128, 512], BF16) for i in range(3)]; otm_dp = [p2.dep() for _ in range(3)]
        sndt = [p2.sb("sndt%d" % k, [128, T], BF16) for k in range(5)]
        sn_dp = [p2.dep() for _ in range(5)]

        def send(k):
            dma([(SND[k], sndt[k][:])], R=[sn_dp[k]], W=[snd_dp[k]], st=sn_dp[k])
            kb.allgather(SND[k], RCV[k], R=[snd_dp[k]], W=[rcv_dp[k]])
        rp = [p2.sb("rp%d" % i, [64, 512], F32) for i in range(2)]; rp_dp = [p2.dep() for _ in range(2)]
        st2 = p2.sb("st2", [128, 8], F32); st2_dp = p2.dep()
        cn = p2.sb("cn", [128, 384], BF16); cn_dp = p2.dep()
        op("pool", lambda g: g.memset(sndt[2][64:128, :], 0.0), W=[sn_dp[2]])
        cnt = {"w": 0, "ofm": 0, "otm": 0}

        def load_piece(p):
            if p >= NPIECE:
                return
            i = p % 3
            dma([(wst[i][:], win[l, p])], W=[wst_dp[i]], st=wst_dp[i])

        def cast_piece(i, dst, dst_dp, gofs):
            for c in range(NCH):
                op("pool", lambda g: g.tensor_scalar(out=dst[:, c, :], in0=wst[i][:, c, :],
                                                     scalar1=small[:, gofs + c:gofs + c + 1], scalar2=0.0,
                                                     op0=ALU.mult, op1=ALU.add),
                   R=[wst_dp[i], small_dp], W=[dst_dp], nowait=(c > 0))

        load_piece(0)
        load_piece(1)
        for p in range(10):
            i = p % 3
            wi = p % 2
            cast_piece(i, wb[wi], wb_dp[wi], 0)
            load_piece(p + 2)
            for tg in range(NG):
                tsl = slice(tg * 512, (tg + 1) * 512)
                hdeps = hT_dp[tg * 4:(tg + 1) * 4]
                if p < 9:
                    pi = rr["ps"] % 4; rr["ps"] += 1
                    for c in range(NCH):
                        op("pe", lambda g: g.matmul(out=PS[pi][:], lhsT=wb[wi][:, c, :], rhs=hT[:, c, tsl],
                                                    start=(c == 0), stop=(c == NCH - 1)),
                           R=[wb_dp[wi]] + hdeps, W=[PD[pi]])
                    if p < 8:
                        oi = cnt["ofm"] % 3; cnt["ofm"] += 1
                        copy(evac_engine(), ofm[oi][:], PS[pi][:], R=[PD[pi]], W=[ofm_dp[oi]])
                        dma([(QA[p, :, tsl], ofm[oi][:])], R=[ofm_dp[oi]], W=[qa_dp], st=ofm_dp[oi])
                    else:
                        copy(evac_engine(), sndt[3][:, tsl], PS[pi][:], R=[PD[pi]], W=[sn_dp[3]])
                else:
                    pis = []
                    for half in range(2):
                        pi = rr["ps"] % 4; rr["ps"] += 1
                        pis.append(pi)
                        for c in range(NCH):
                            op("pe", lambda g: g.matmul(out=PS[pi][0:64, :], lhsT=wb[wi][:, c, half * 64:(half + 1) * 64],
                                                        rhs=hT[:, c, tsl], start=(c == 0), stop=(c == NCH - 1)),
                               R=[wb_dp[wi]] + hdeps, W=[PD[pi]])
                    op("dve", lambda g: g.tensor_tensor(out=rp[0][:], in0=PS[pis[0]][0:64, :], in1=cs[:, 0, tsl], op=ALU.mult),
                       R=[PD[pis[0]], cs_dp], W=[rp_dp[0]])
                    op("dve", lambda g: g.tensor_tensor(out=rp[1][:], in0=PS[pis[1]][0:64, :], in1=cs[:, 1, tsl], op=ALU.mult),
                       R=[PD[pis[1]], cs_dp], W=[rp_dp[1]])
                    op("dve", lambda g: g.tensor_tensor(out=sndt[2][0:64, tsl], in0=rp[0][:], in1=rp[1][:], op=ALU.add),
                       R=[rp_dp[0], rp_dp[1]], W=[sn_dp[2]])
            if p == 8:
                send(3)
            if p == 9:
                send(2)

        groups = [("cqva", [10, 11, 12, 13]), ("ckv", [14, 15]), ("ga0", [16, 17, 18, 19]), ("ga1", [20, 21, 22, 23])]
        for gi, (gname, pieces) in enumerate(groups):
            wi = gi % 2
            ncol = 128 * len(pieces)
            for k, p in enumerate(pieces):
                i = p % 3
                cast_piece(i, wt[wi][:, :, k * 128:(k + 1) * 128], wt_dp[wi], 0)
                load_piece(p + 2)
            for s in range(NS):
                pi = rr["ps"] % 4; rr["ps"] += 1
                for c in range(NCH):
                    op("pe", lambda g: g.matmul(out=PS[pi][:, 0:ncol], lhsT=hT[:, c, s * 128:(s + 1) * 128],
                                                rhs=wt[wi][:, c, 0:ncol], start=(c == 0), stop=(c == NCH - 1)),
                       R=[wt_dp[wi], hT_dp[s]], W=[PD[pi]])
                if gname in ("cqva", "ckv"):
                    nr = QL if gname == "cqva" else KVL
                    nck = nr // 128
                    op("act", lambda g: g.activation(out=cn[:, 0:nr], in_=PS[pi][:, 0:nr], func=AF.Square,
                                                     accum_out=st2[:, 0:1]),
                       R=[PD[pi]], W=[cn_dp, st2_dp])
                    op("dve", lambda g: g.tensor_scalar(out=st2[:, 1:2], in0=st2[:, 0:1], scalar1=1.0 / nr,
                                                        scalar2=EPS, op0=ALU.mult, op1=ALU.add), W=[st2_dp])
                    op("act", lambda g: g.activation(out=st2[:, 2:3], in_=st2[:, 1:2], func=AF.Sqrt), W=[st2_dp])
                    op("dve", lambda g: g.reciprocal(out=st2[:, 3:4], in_=st2[:, 2:3]), W=[st2_dp])
                    op("dve", lambda g: g.tensor_scalar(out=cn[:, 0:nr], in0=PS[pi][:, 0:nr], scalar1=st2[:, 3:4],
                                                        scalar2=None, op0=ALU.mult),
                       R=[PD[pi], st2_dp], W=[cn_dp])
                    if gname == "cqva":
                        copy("act", sndt[4][:, s * 128:(s + 1) * 128], PS[pi][:, 384:512],
                             R=[PD[pi]], W=[sn_dp[4]])
                    pj = 4 + (s % 2)
                    pv = bf(PS[pj][:])[:, 0:128 * nck].rearrange("p (a b) -> p a b", a=nck)
                    for a in range(nck):
                        op("pe", lambda g: g.transpose(out=pv[:, a, :], in_=cn[:, a * 128:(a + 1) * 128], identity=ident[:]),
                           R=[cn_dp, ident_dp], W=[PD[pj]])
                    if gname == "cqva":
                        copy("dve", cqnT[:, :, s * 128:(s + 1) * 128], pv, R=[PD[pj]], W=[cqnT_dp[s]])
                    else:
                        for a in range(2):
                            copy("dve", sndt[a][:, s * 128:(s + 1) * 128], pv[:, a, :], R=[PD[pj]], W=[sn_dp[a]])
                else:
                    oi = cnt["otm"] % 3; cnt["otm"] += 1
                    op("act", lambda g: g.activation(out=otm[oi][:], in_=PS[pi][:], func=AF.Silu),
                       R=[PD[pi]], W=[otm_dp[oi]])
                    col = (gi - 2) * 512
                    dma([(GT[s * 128:(s + 1) * 128, col:col + 512], otm[oi][:])], R=[otm_dp[oi]], W=[gt_dp[s]],
                        st=otm_dp[oi])
            if gname == "cqva":
                send(4)
            if gname == "ckv":
                send(0)
                send(1)
        for p in range(24, 32):
            i = p % 3
            wi = p % 2
            cast_piece(i, wb[wi], wb_dp[wi], 0)
            load_piece(p + 2)
            for tg in range(NG):
                tsl = slice(tg * 512, (tg + 1) * 512)
                hdeps = hT_dp[tg * 4:(tg + 1) * 4]
                pi = rr["ps"] % 4; rr["ps"] += 1
                for c in range(NCH):
                    op("pe", lambda g: g.matmul(out=PS[pi][:], lhsT=wb[wi][:, c, :], rhs=hT[:, c, tsl],
                                                start=(c == 0), stop=(c == NCH - 1)),
                       R=[wb_dp[wi]] + hdeps, W=[PD[pi]])
                oi = cnt["ofm"] % 3; cnt["ofm"] += 1
                op("act", lambda g: g.activation(out=ofm[oi][:], in_=PS[pi][:], func=AF.Silu),
                   R=[PD[pi]], W=[ofm_dp[oi]])
                dma([(GTB[p - 24, :, tsl], ofm[oi][:])], R=[ofm_dp[oi]], W=[gtb_dp], st=ofm_dp[oi])
        p2.close()
        pa.close()

        pY = Ph(kb)
        yta = pY.sb("yta", [128, NCH, T], BF16); yta_dp = [pY.dep() for _ in range(NCH)]
        pS = Ph(kb)
        qall = pS.sb("qall", [128, 8, T], BF16); qall_dp = pS.dep()
        kaT = pS.sb("kaT", [128, 2, T], BF16); kaT_dp = pS.dep()
        vraw = pS.sb("vraw", [128, 2, T], BF16); vraw_dp = pS.dep()
        vaug = pS.sb("vaug", [128, 2 * NS * 2, 66], BF16); vaug_dp = pS.dep()
        etab = pS.sb("etab", [128, 3, 16, 128], BF16); etab_dp = pS.dep()
        gat = [pS.sb("gat%d" % i, [128, 1024], BF16) for i in range(2)]; gat_dp = [pS.dep() for _ in range(2)]
        pex = [pS.sb("pex%d" % i, [128, 512], BF16) for i in range(2)]; pex_dp = [pS.dep() for _ in range(2)]
        ptS = [pS.sb("ptS%d" % i, [128, 512], BF16) for i in range(6)]; ptS_dp = [pS.dep() for _ in range(6)]
        ysw = [pS.sb("ysw%d" % i, [128, 1024], BF16) for i in range(2)]; ysw_dp = [pS.dep() for _ in range(2)]
        lS = pS.sb("lS", [128, 8], F32); lS_dp = pS.dep()
        dma([(etab[:], etab_d)], W=[etab_dp], st=etab_dp)
        dma([(qall[:, j, :], QA[j]) for j in range(8)], R=[qa_dp], W=[qall_dp], st=qall_dp)
        dma([(kaT[:, r, :], RCV[3][r * 128:(r + 1) * 128, :]) for r in range(2)], R=[rcv_dp[3]], W=[kaT_dp], st=kaT_dp)
        dma([(vraw[:, r, :], RCV[4][r * 128:(r + 1) * 128, :]) for r in range(2)], R=[rcv_dp[4]], W=[vraw_dp], st=vraw_dp)
        op("pool", lambda g: g.memset(vaug[:], 1.0), W=[vaug_dp])
        for r in range(2):
            dst = vaug[:, r * NS * 2:(r + 1) * NS * 2, 0:64]
            src = vraw[:, r, :].rearrange("p (a d) -> p a d", d=64)
            op("pool", lambda g: g.tensor_copy(out=dst, in_=src), R=[vraw_dp], W=[vaug_dp])
        cS = {"pex": 0, "pt": 0}
        for s in range(NS):
            b = s % 2
            dma([(gat[b][:], GT[s * 128:(s + 1) * 128, 0:1024])], R=[gt_dp[s]], W=[gat_dp[b]], st=gat_dp[b])
            cands = [(0, 1, s - 1), (1, 0, s), (2, 1, s)]
            if s == 0:
                cands = cands[1:]
            for gk in range(2):
                prt = slice(gk * 64, (gk + 1) * 64)
                for quad in range(2):
                    oi = 2 + gk * 2 + quad
                    pts = []
                    for (ci, r, ks) in cands:
                        pi = rr["ps"] % 2; rr["ps"] += 1
                        op("pe", lambda g: g.matmul(out=PS[pi][:], lhsT=kaT[prt, r, ks * 128:(ks + 1) * 128],
                                                    rhs=qall[prt, quad * 4:(quad + 1) * 4, s * 128:(s + 1) * 128],
                                                    start=True, stop=True),
                           R=[kaT_dp, qall_dp], W=[PD[pi]])
                        xi = cS["pex"] % 2; cS["pex"] += 1
                        op("act", lambda g: g.activation(out=pex[xi][:], in_=PS[pi][:], func=AF.Exp, scale=0.125),
                           R=[PD[pi]], W=[pex_dp[xi]])
                        ti = cS["pt"] % 6; cS["pt"] += 1
                        h0 = gk * 8 + quad * 4
                        op("dve" if ti % 2 == 0 else "pool", lambda g: g.tensor_tensor(out=ptS[ti][:].rearrange("p (a b) -> p a b", a=4),
                                                            in0=pex[xi][:].rearrange("p (a b) -> p a b", a=4),
                                                            in1=etab[:, ci, h0:h0 + 4, :], op=ALU.mult),
                           R=[pex_dp[xi], etab_dp], W=[ptS_dp[ti]])
                        pts.append((ti, r, ks))
                    ov = PS[oi][:, 0:4 * 65].rearrange("p (a b) -> p a b", a=4)
                    for hq in range(4):
                        for n, (ti, r, ks) in enumerate(pts):
                            op("pe", lambda g: g.matmul(out=ov[:, hq, :], lhsT=ptS[ti][:, hq * 128:(hq + 1) * 128],
                                                        rhs=vaug[:, (r * NS + ks) * 2 + gk, 0:65],
                                                        start=(n == 0), stop=(n == len(pts) - 1)),
                               R=[ptS_dp[ti], vaug_dp], W=[PD[oi]])
                    h0 = gk * 8 + quad * 4
                    op("dve", lambda g: g.tensor_tensor(out=lS[:, 0:4], in0=ov[:, :, 64], in1=small[:, 40 + h0:44 + h0], op=ALU.add),
                       R=[PD[oi], esink_dp], W=[lS_dp])
                    op("dve", lambda g: g.reciprocal(out=lS[:, 4:8], in_=lS[:, 0:4]), W=[lS_dp])
                    for hq in range(4):
                        h = h0 + hq
                        op("dve", lambda g: g.scalar_tensor_tensor(out=ysw[b][:, h * 64:(h + 1) * 64], in0=ov[:, hq, 0:64],
                                                                    scalar=lS[:, 4 + hq:5 + hq], in1=gat[b][:, h * 64:(h + 1) * 64],
                                                                    op0=ALU.mult, op1=ALU.mult),
                           R=[PD[oi], lS_dp, gat_dp[b]], W=[ysw_dp[b]], nowait=(hq > 0))
            pj = 6 + (s % 2)
            pv = bf(PS[pj][:]).rearrange("p (a b) -> p a b", a=8)
            for a in range(8):
                op("pe", lambda g: g.transpose(out=pv[:, a, :], in_=ysw[b][:, a * 128:(a + 1) * 128], identity=ident[:]),
                   R=[ysw_dp[b], ident_dp], W=[PD[pj]])
            copy("act", yta[:, 0:8, s * 128:(s + 1) * 128], pv, R=[PD[pj]], W=yta_dp[0:8])
        pS.close()

        pC = Ph(kb)
        ck = pC.sb("ck", [128, 2, 2, T], BF16); ck_dp = pC.dep()
        kr = pC.sb("kr", [64, 2, T], BF16); kr_dp = pC.dep()
        wq = pC.sb("wq", [128, 3, 2048], BF16); wq_dp = pC.dep()
        wkv = pC.sb("wkv", [128, 2, 2048], BF16); wkv_dp = pC.dep()
        wst2 = [pC.sb("wsc%d" % i, [128, 1024], F32) for i in range(2)]; wst2_dp = [pC.dep() for _ in range(2)]
        khT = [pC.sb("khT%d" % i, [128, 2 * T], BF16) for i in range(2)]; khT_dp = [pC.dep() for _ in range(2)]
        vh = [pC.sb("vh%d" % i, [128, 2 * NS, 130], BF16) for i in range(2)]; vh_dp = [pC.dep() for _ in range(2)]
        gbt = [pC.sb("gbt%d" % i, [128, 512], BF16) for i in range(2)]; gbt_dp = [pC.dep() for _ in range(2)]
        onesb = pC.sb("onesb", [128, 128], BF16); onesb_dp = pC.dep()
        rLt = [pC.sb("rLt%d" % i, [128, 512], F32) for i in range(2)]; rLt_dp = [pC.dep() for _ in range(2)]
        op("pool", lambda g: g.memset(onesb[:], 1.0), W=[onesb_dp])
        qn = [pC.sb("qn%d" % i, [128, 512], BF16) for i in range(2)]; qn_dp = [pC.dep() for _ in range(2)]
        qr = [pC.sb("qr%d" % i, [64, 512], BF16) for i in range(2)]; qr_dp = [pC.dep() for _ in range(2)]
        rq = [pC.sb("rq%d" % i, [64, 512], F32) for i in range(2)]; rq_dp = [pC.dep() for _ in range(2)]
        ptC = [pC.sb("ptC%d" % i, [128, 512], BF16) for i in range(3)]; ptC_dp = [pC.dep() for _ in range(3)]
        for r in range(2):
            dma([(ck[:, c, r, :], RCV[c][r * 128:(r + 1) * 128, :]) for c in range(2)]
                + [(kr[:, r, :], RCV[2][r * 128:r * 128 + 64, :])],
                R=rcv_dp[0:3], W=[ck_dp, kr_dp], st=ck_dp if r == 0 else kr_dp)
        nst = 0
        for (wsrc, wdst, wdp, nchk, gofs) in ((wkv_d, wkv, wkv_dp, 2, 19), (wq_d, wq, wq_dp, 3, 16)):
            for c in range(nchk):
                for hf in range(2):
                    i = nst % 2; nst += 1
                    csl = slice(hf * 1024, (hf + 1) * 1024)
                    dma([(wst2[i][:], wsrc[l, :, c, csl])], W=[wst2_dp[i]], st=wst2_dp[i])
                    op("pool", lambda g: g.tensor_scalar(out=wdst[:, c, csl], in0=wst2[i][:], scalar1=small[:, gofs + c:gofs + c + 1],
                                                         scalar2=0.0, op0=ALU.mult, op1=ALU.add),
                       R=[wst2_dp[i], small_dp], W=[wdp])
        for i in range(2):
            op("pool", lambda g: g.memset(vh[i][:, :, 128:130], 1.0), W=[vh_dp[i]])
        cC = {"pt": 0, "g": 0}
        scale = float((128 + 64) ** -0.5)
        NKS = 2 * NS

        def prep(h):
            hb_ = h % 2
            for kg in range(2 * T // 512):
                r, t0 = divmod(kg * 512, T)
                pi = 6 + rr["ps"] % 2; rr["ps"] += 1
                for c in range(2):
                    op("pe", lambda g: g.matmul(out=PS[pi][:], lhsT=wkv[:, c, h * 256:h * 256 + 128],
                                                rhs=ck[:, c, r, t0:t0 + 512], start=(c == 0), stop=(c == 1)),
                       R=[wkv_dp, ck_dp], W=[PD[pi]])
                copy("dve", khT[hb_][:, kg * 512:(kg + 1) * 512], PS[pi][:], R=[PD[pi]], W=[khT_dp[hb_]])
            for k4 in range(NKS // 4):
                pi = 6 + rr["ps"] % 2; rr["ps"] += 1
                for a in range(4):
                    ksl = k4 * 4 + a
                    r, s_ = divmod(ksl, NS)
                    for c in range(2):
                        op("pe", lambda g: g.matmul(out=PS[pi][:, a * 128:(a + 1) * 128],
                                                    lhsT=ck[:, c, r, s_ * 128:(s_ + 1) * 128],
                                                    rhs=wkv[:, c, h * 256 + 128:h * 256 + 256],
                                                    start=(c == 0), stop=(c == 1)),
                           R=[wkv_dp, ck_dp], W=[PD[pi]])
                copy("dve", vh[hb_][:, k4 * 4:(k4 + 1) * 4, 0:128],
                     PS[pi][:].rearrange("p (a b) -> p a b", a=4), R=[PD[pi]], W=[vh_dp[hb_]])

        prep(0)
        pending = []
        for h in range(8):
            hb_ = h % 2
            for G in range(NG):
                gb_ = cC["g"] % 2; cC["g"] += 1
                tsl = slice(G * 512, (G + 1) * 512)
                cdeps = cqnT_dp[G * 4:(G + 1) * 4]
                dma([(gbt[gb_][:], GTB[h, :, tsl])], R=[gtb_dp], W=[gbt_dp[gb_]], st=gbt_dp[gb_])
                pi = 6 + rr["ps"] % 2; rr["ps"] += 1
                for c in range(3):
                    op("pe", lambda g: g.matmul(out=PS[pi][:], lhsT=wq[:, c, h * 256:h * 256 + 128], rhs=cqnT[:, c, tsl],
                                                start=(c == 0), stop=(c == 2)),
                       R=[wq_dp] + cdeps, W=[PD[pi]])
                copy("dve", qn[gb_][:], PS[pi][:], R=[PD[pi]], W=[qn_dp[gb_]])
                for half in range(2):
                    pi = 6 + rr["ps"] % 2; rr["ps"] += 1
                    c0 = h * 256 + 128 + half * 64
                    for c in range(3):
                        op("pe", lambda g: g.matmul(out=PS[pi][0:64, :], lhsT=wq[:, c, c0:c0 + 64], rhs=cqnT[:, c, tsl],
                                                    start=(c == 0), stop=(c == 2)),
                           R=[wq_dp] + cdeps, W=[PD[pi]])
                    op("dve", lambda g: g.tensor_tensor(out=rq[half][:], in0=PS[pi][0:64, :], in1=cs[:, half, tsl], op=ALU.mult),
                       R=[PD[pi], cs_dp], W=[rq_dp[half]])
                op("dve", lambda g: g.tensor_tensor(out=qr[gb_][:], in0=rq[0][:], in1=rq[1][:], op=ALU.add),
                   R=[rq_dp[0], rq_dp[1]], W=[qr_dp[gb_]])
                if G == min(1, NG - 1) and h + 1 < 8:
                    prep(h + 1)
                ents = [(r, s_, 0, False) for s_ in range(4 * G) for r in range(2)]
                ents += [(r, 4 * G + j, j, True) for j in range(4) for r in range(2)]
                NE = len(ents)
                st_ = {}

                def emit_S(n):
                    r, s_, jmin, msk = ents[n]
                    ncol = (4 - jmin) * 128
                    pi = rr["ps"] % 2; rr["ps"] += 1
                    kcol = r * T + s_ * 128
                    op("pe", lambda g: g.matmul(out=PS[pi][:, 0:ncol], lhsT=khT[hb_][:, kcol:kcol + 128],
                                                rhs=qn[gb_][:, jmin * 128:512], start=True, stop=False),
                       R=[khT_dp[hb_], qn_dp[gb_]], W=[PD[pi]])
                    op("pe", lambda g: g.matmul(out=PS[pi][:, 0:ncol], lhsT=kr[:, r, s_ * 128:(s_ + 1) * 128],
                                                rhs=qr[gb_][:, jmin * 128:512], start=False, stop=True),
                       R=[kr_dp, qr_dp[gb_]], W=[PD[pi]])
                    ti = cC["pt"] % 3; cC["pt"] += 1
                    op("act", lambda g: g.activation(out=ptC[ti][:, 0:ncol], in_=PS[pi][:, 0:ncol], func=AF.Exp, scale=scale),
                       R=[PD[pi]], W=[ptC_dp[ti]])
                    if msk:
                        op("dve", lambda g: g.tensor_tensor(out=ptC[ti][:, 0:128], in0=ptC[ti][:, 0:128], in1=mask[:, r, :], op=ALU.mult),
                           R=[mask_dp], W=[ptC_dp[ti]])
                    st_[n] = ti

                bO = 2 + 2 * gb_

                def emit_PV(n):
                    r, s_, jmin, msk = ents[n]
                    ti = st_[n]
                    ncol = (4 - jmin) * 128
                    op("pe", lambda g: g.matmul(out=PS[bO][:, jmin * 128:512], lhsT=vh[hb_][:, r * NS + s_, 0:128],
                                                rhs=ptC[ti][:, 0:ncol], start=(n == 0), stop=(n == NE - 1)),
                       R=[ptC_dp[ti], vh_dp[hb_]], W=[PD[bO]])
                    op("pe", lambda g: g.matmul(out=PS[bO + 1][:, jmin * 128:512], lhsT=onesb[:],
                                                rhs=ptC[ti][:, 0:ncol], start=(n == 0), stop=(n == NE - 1)),
                       R=[ptC_dp[ti], onesb_dp], W=[PD[bO + 1]])

                emit_S(0)
                for n in range(NE):
                    if n + 1 < NE:
                        emit_S(n + 1)
                    emit_PV(n)
                op("dve", lambda g: g.reciprocal(out=rLt[gb_][:], in_=PS[bO + 1][:]), R=[PD[bO + 1]], W=[rLt_dp[gb_]])
                op("dve", lambda g: g.tensor_tensor(out=rLt[gb_][:], in0=PS[bO][:], in1=rLt[gb_][:], op=ALU.mult),
                   R=[PD[bO]], W=[rLt_dp[gb_]])
                op("dve", lambda g: g.tensor_tensor(out=yta[:, 8 + h, tsl], in0=rLt[gb_][:], in1=gbt[gb_][:], op=ALU.mult),
                   R=[rLt_dp[gb_], gbt_dp[gb_]], W=[yta_dp[8 + h]])
        pC.close()

        pD = Ph(kb)
        wsd = [pD.sb("wsd%d" % i, [128, NCH, 128], F32) for i in range(2)]; wsd_dp = [pD.dep() for _ in range(2)]
        wo = [pD.sb("wo%d" % i, [128, NCH, 512], BF16) for i in range(2)]; wo_dp = [pD.dep() for _ in range(2)]
        xr = [pD.sb("xr%d" % i, [128, 512], F32) for i in range(3)]; xr_dp = [pD.dep() for _ in range(3)]
        xo = [pD.sb("xo%d" % i, [128, 512], F32) for i in range(3)]; xo_dp = [pD.dep() for _ in range(3)]
        cD = {"w": 0, "x": 0}
        def load_wo(cg):
            wi = cg % 2
            for k in range(4):
                i = cD["w"] % 2; cD["w"] += 1
                dma([(wsd[i][:], wo_d[l, cg * 4 + k])], W=[wsd_dp[i]], st=wsd_dp[i])
                op("pool", lambda g: g.tensor_copy(out=wo[wi][:, :, k * 128:(k + 1) * 128], in_=wsd[i][:]),
                   R=[wsd_dp[i]], W=[wo_dp[wi]])

        load_wo(0)
        for cg in range(4):
            wi = cg % 2
            if cg + 1 < 4:
                load_wo(cg + 1)
            for s in range(NS):
                xi = cD["x"] % 3; cD["x"] += 1
                dma([(xr[xi][:], x_src[s * 128:(s + 1) * 128, cg * 512:(cg + 1) * 512])], R=[x_src_dp[s]], W=[xr_dp[xi]], st=xr_dp[xi])
                pi = rr["ps"] % 4; rr["ps"] += 1
                for c in range(NCH):
                    op("pe", lambda g: g.matmul(out=PS[pi][:], lhsT=yta[:, c, s * 128:(s + 1) * 128], rhs=wo[wi][:, c, :],
                                                start=(c == 0), stop=(c == NCH - 1)),
                       R=yta_dp + [wo_dp[wi]], W=[PD[pi]])
                op("dve", lambda g: g.tensor_tensor(out=xo[xi][:], in0=PS[pi][:], in1=xr[xi][:], op=ALU.add),
                   R=[PD[pi], xr_dp[xi]], W=[xo_dp[xi]])
                dma([(x_dst[s * 128:(s + 1) * 128, cg * 512:(cg + 1) * 512], xo[xi][:])], R=[xo_dp[xi]], W=[x_dst_dp[s]], st=xo_dp[xi])
        pD.close()
        pY.close()

    pF = Ph(kb)
    x_src = XS[(L - 1) % 2]
    x_src_dp = x_dp[1 + (L - 1) % 2]
    gfin = pF.sb("gfin", [128, D], F32); gfin_dp = pF.dep()
    xs = [pF.sb("fx%d" % i, [128, D], F32) for i in range(2)]; xs_dp = [pF.dep() for _ in range(2)]
    fo = [pF.sb("fo%d" % i, [128, D], F32) for i in range(2)]; fo_dp = [pF.dep() for _ in range(2)]
    junk = pF.sb("fjunk", [128, D], BF16); junk_dp = pF.dep()
    st1 = pF.sb("fst", [128, 4 * NS], F32); st1_dp = [pF.dep() for _ in range(NS)]
    out_dp = Dep()
    dma([(gfin[:], gfin_d)], W=[gfin_dp], st=gfin_dp)
    for s in range(NS):
        b = s % 2
        dma([(xs[b][:], x_src[s * 128:(s + 1) * 128, :])], R=[x_src_dp[s]], W=[xs_dp[b]], st=xs_dp[b])
        c0 = 4 * s
        op("act", lambda g: g.activation(out=junk[:], in_=xs[b][:], func=AF.Square, accum_out=st1[:, c0:c0 + 1]),
           R=[xs_dp[b]], W=[junk_dp, st1_dp[s]])
        op("dve", lambda g: g.tensor_scalar(out=st1[:, c0 + 1:c0 + 2], in0=st1[:, c0:c0 + 1], scalar1=1.0 / D, scalar2=EPS,
                                            op0=ALU.mult, op1=ALU.add), W=[st1_dp[s]])
        op("act", lambda g: g.activation(out=st1[:, c0 + 2:c0 + 3], in_=st1[:, c0 + 1:c0 + 2], func=AF.Sqrt), W=[st1_dp[s]])
        op("dve", lambda g: g.reciprocal(out=st1[:, c0 + 3:c0 + 4], in_=st1[:, c0 + 2:c0 + 3]), W=[st1_dp[s]])
        op("dve", lambda g: g.scalar_tensor_tensor(out=fo[b][:], in0=xs[b][:], scalar=st1[:, c0 + 3:c0 + 4], in1=gfin[:],
                                                    op0=ALU.mult, op1=ALU.mult),
           R=[xs_dp[b], st1_dp[s], gfin_dp], W=[fo_dp[b]])
        dma([(out_d[s * 128:(s + 1) * 128, :], fo[b][:])], R=[fo_dp[b]], W=[out_dp], st=fo_dp[b])
    pF.close()
    gp.es.close()
    es.close()
    return nc


def _host_inputs(x, attn_norm_g, w_in, swa_sinks, q_a_norm_g, kv_a_norm_g, w_q_b, w_kv_b, w_out, final_norm_g, NS):
    L = w_in.shape[0]
    B = x.shape[0]
    T = NS * 128
    f32 = np.float32
    A_Q, A_KV, A_G = 1024, 128, 1024
    o_qa, o_ka, o_va, o_ga = 0, A_Q, A_Q + A_KV, A_Q + 2 * A_KV
    o_cq = o_ga + A_G
    o_ckv = o_cq + QL
    o_kr = o_ckv + KVL
    o_gb = o_kr + 64
    qa_cols = []
    for j in range(8):
        for half in range(2):
            hd = j + 8 * half
            qa_cols += list(range(o_qa + hd * 64, o_qa + (hd + 1) * 64))
    kr_cols = list(range(o_kr, o_kr + 64))
    kr_sw = list(range(o_kr + 32, o_kr + 64)) + list(range(o_kr, o_kr + 32))
    cols = (qa_cols + list(range(o_ka, o_ka + 128)) + kr_cols + kr_sw + list(range(o_cq, o_cq + QL))
            + list(range(o_va, o_va + 128)) + list(range(o_ckv, o_ckv + KVL)) + list(range(o_ga, o_ga + A_G))
            + list(range(o_gb, o_gb + 1024)))
    cols = np.asarray(cols)
    assert cols.size == NPIECE * 128
    wp = w_in[:, :, cols]
    win = np.ascontiguousarray(wp.reshape(L, NCH, 128, NPIECE, 128).transpose(0, 3, 2, 1, 4))
    qcols = []
    for h in range(8):
        b0 = h * 192
        qcols += list(range(b0, b0 + 192)) + list(range(b0 + 160, b0 + 192)) + list(range(b0 + 128, b0 + 160))
    wqp = w_q_b[:, :, np.asarray(qcols)]
    wq = np.ascontiguousarray(wqp.reshape(L, 3, 128, 2048).transpose(0, 2, 1, 3))
    wkv = np.ascontiguousarray(w_kv_b.reshape(L, 2, 128, 2048).transpose(0, 2, 1, 3))
    wo = np.ascontiguousarray(w_out.reshape(L, NCH, 128, 16, 128).transpose(0, 3, 2, 1, 4))
    gin = np.ascontiguousarray(attn_norm_g.reshape(L, NCH, 128).transpose(0, 2, 1))
    gq = np.ascontiguousarray(q_a_norm_g.reshape(L, 3, 128).transpose(0, 2, 1))
    gkv = np.ascontiguousarray(kv_a_norm_g.reshape(L, 2, 128).transpose(0, 2, 1))
    sinks = np.ascontiguousarray(np.broadcast_to(swa_sinks[:, None, :], (L, 128, 16)))
    gfin = np.ascontiguousarray(np.broadcast_to(final_norm_g[None, :], (128, D)))
    ident = np.eye(128, dtype=f32).astype(ml_dtypes.bfloat16)
    S_full = 2 * T
    pos = np.arange(S_full, dtype=f32)
    inv_freq = (10000.0 ** (-np.arange(0, 64, 2, dtype=f32) / 64)).astype(f32)
    ang = pos[:, None] * inv_freq[None, :]
    cos, sin = np.cos(ang).astype(f32), np.sin(ang).astype(f32)
    cos2 = np.concatenate([cos, cos], 1).T
    sin2 = np.concatenate([-sin, sin], 1).T
    slopes = np.exp2(-8.0 * np.arange(1, 17, dtype=f32) / 16).astype(f32)
    kk = np.arange(128)[:, None]
    qq = np.arange(128)[None, :]
    d_prev = (128 + qq - kk).astype(f32)
    d_cur = (qq - kk).astype(f32)
    E_prev = np.where((kk > qq)[:, None, :], np.exp(-slopes[None, :, None] * np.clip(d_prev, 0, 128)[:, None, :]), 0.0).astype(f32)
    E_cur = np.where((kk <= qq)[:, None, :], np.exp(-slopes[None, :, None] * np.clip(d_cur, 0, 128)[:, None, :]), 0.0).astype(f32)
    Z = np.zeros_like(E_prev)
    tri = (kk <= qq).astype(f32)
    ones = np.ones_like(tri)
    zer = np.zeros_like(tri)
    in_maps = []
    for c in range(2 * B):
        b, r = divmod(c, 2)
        xb = x[b].reshape(2 * NS, 128, D)[r::2].reshape(T, D)
        tok = (np.arange(NS)[:, None] * 2 + r) * 128 + np.arange(128)[None, :]
        tok = tok.reshape(-1)
        cs = np.ascontiguousarray(np.stack([cos2[:, tok], sin2[:, tok]], 0))
        if r == 0:
            et = np.stack([E_prev, E_cur, Z], 1)
            mk = np.stack([tri, zer], 1)
        else:
            et = np.stack([Z, E_prev, E_cur], 1)
            mk = np.stack([ones, tri], 1)
        in_maps.append({
            "x": np.ascontiguousarray(xb), "win": win, "wq": wq, "wkv": wkv, "wo": wo, "gin": gin, "gq": gq,
            "gkv": gkv, "sinks": sinks, "gfin": gfin, "cs": cs.astype(f32),
            "etab": np.ascontiguousarray(et).astype(ml_dtypes.bfloat16),
            "mask": np.ascontiguousarray(mk).astype(ml_dtypes.bfloat16), "ident": ident,
        })
    return in_maps


def _assemble(results, B, NS):
    T = NS * 128
    out = np.empty((B, 2 * NS, 128, D), np.float32)
    for c in range(2 * B):
        b, r = divmod(c, 2)
        out[b, r::2] = np.asarray(results[c]["out"]).reshape(NS, 128, D)
    return out.reshape(B, 2 * T, D)


_NC_CACHE = {}


def kernel(x, attn_norm_g, w_in, swa_sinks, q_a_norm_g, kv_a_norm_g, w_q_b, w_kv_b, w_out, final_norm_g):
    args = [np.asarray(a, dtype=np.float32) for a in (x, attn_norm_g, w_in, swa_sinks, q_a_norm_g, kv_a_norm_g,
                                                      w_q_b, w_kv_b, w_out, final_norm_g)]
    B, S = args[0].shape[0], args[0].shape[1]
    NS = S // 256
    L = args[2].shape[0]
    in_maps = _host_inputs(*args, NS=NS)
    key = (NS, L)
    if key not in _NC_CACHE:
        _NC_CACHE[key] = build(NS, L)
    res = run_bass_kernel_spmd(_NC_CACHE[key], in_maps, core_ids=list(range(2 * B)))
    return _assemble(res.results, B, NS)
```

```python
import contextlib
import numpy as np
import ml_dtypes
import concourse.bass as bass
import concourse.mybir as mybir
from concourse.bass_utils import run_bass_kernel_spmd

F32 = mybir.dt.float32
BF16 = mybir.dt.bfloat16
AF = mybir.ActivationFunctionType
ALU = mybir.AluOpType

D = 2048
NCH = 16
EPS = 1e-6
QL = 384
KVL = 256
NPIECE = 32
PAIRS = [[0, 1], [2, 3], [4, 5], [6, 7]]


class Dep:
    __slots__ = ("w", "r", "ds")

    def __init__(self):
        self.w = None
        self.r = {}
        self.ds = None


class KB:
    def __init__(self, nc, es, n_dsem=70):
        self.nc = nc
        self.eng = {"pe": nc.tensor, "act": nc.scalar, "dve": nc.vector,
                    "pool": nc.gpsimd, "sp": nc.sync}
        self.sem = {e: es.enter_context(nc.semaphore("s_" + e)) for e in ("pe", "act", "dve", "pool")}
        self.cnt = {e: 0 for e in self.sem}
        self.known = {e: {} for e in self.eng}
        self.dsems = [[es.enter_context(nc.semaphore("d%d" % i)), 0] for i in range(n_dsem)]
        self.dfree = list(range(n_dsem))
        self.ccsem = es.enter_context(nc.semaphore("ccs"))
        self.cccnt = 0

    def _wait(self, e, ev):
        if ev is None:
            return
        s, v = ev
        k = self.known[e]
        if k.get(id(s), 0) >= v:
            return
        if e == "pe" and s is self.sem["pe"]:
            return
        self.eng[e].wait_ge(s, v)
        k[id(s)] = v

    def _deps(self, e, R, W):
        for d in R:
            self._wait(e, d.w)
        for d in W:
            self._wait(e, d.w)
            for ev in d.r.values():
                self._wait(e, ev)

    def op(self, e, fn, R=(), W=(), nowait=False):
        if not nowait:
            self._deps(e, R, W)
        ins = fn(self.eng[e])
        self.cnt[e] += 1
        ins.then_inc(self.sem[e], 1)
        ev = (self.sem[e], self.cnt[e])
        for d in R:
            d.r[e] = ev
        for d in W:
            d.w = ev
            d.r = {}
        return ev

    def dsem_alloc(self, dep):
        dep.ds = self.dfree.pop()
        return dep

    def dsem_free(self, dep):
        if dep.ds is not None:
            self.dfree.append(dep.ds)
            dep.ds = None

    def dma(self, pairs, R=(), W=(), st=None, q="sp"):
        if st.ds is None:
            self.dsem_alloc(st)
        self._deps(q, R, W)
        slot = self.dsems[st.ds]
        for (o, i) in pairs:
            ins = self.eng[q].dma_start(out=o, in_=i)
            slot[1] += 16
            ins.then_inc(slot[0], 16)
        ev = (slot[0], slot[1])
        for d in R:
            d.r[("d", st.ds)] = ev
        for d in W:
            d.w = ev
            d.r = {}
        return ev

    def allgather(self, src, dst, R=(), W=()):
        self._deps("pool", R, W)
        ins = self.nc.gpsimd.collective_compute("AllGather", ALU.bypass, replica_groups=PAIRS,
                                                ins=[src], outs=[dst])
        self.cccnt += 1
        ins.then_inc(self.ccsem, 1)
        ev = (self.ccsem, self.cccnt)
        for d in R:
            d.r["cc"] = ev
        for d in W:
            d.w = ev
            d.r = {}

    def barrier(self):
        evs = [(self.sem[e], self.cnt[e]) for e in self.sem if self.cnt[e] > 0]
        evs += [(s, c) for (s, c) in self.dsems if c > 0]
        if self.cccnt:
            evs.append((self.ccsem, self.cccnt))
        for e in self.eng:
            for ev in evs:
                self._wait(e, ev)


class Ph:
    uid = 0

    def __init__(self, kb):
        self.kb = kb
        self.es = contextlib.ExitStack()
        self.deps = []

    def sb(self, name, shape, dt):
        Ph.uid += 1
        t = self.es.enter_context(self.kb.nc.sbuf_tensor("sb%d_%s" % (Ph.uid, name), list(shape), dt))
        return t

    def dep(self):
        d = Dep()
        self.deps.append(d)
        return d

    def close(self):
        self.kb.barrier()
        for d in self.deps:
            self.kb.dsem_free(d)
        self.es.close()


def build(NS=16, L=4):
    T = NS * 128
    NG = NS // 4
    nc = bass.Bass("TRN2", target_bir_lowering=False)
    dt = nc.dram_tensor
    x_in = dt("x", [T, D], F32, kind="ExternalInput").ap()
    win = dt("win", [L, NPIECE, 128, NCH, 128], F32, kind="ExternalInput").ap()
    wq_d = dt("wq", [L, 128, 3, 2048], F32, kind="ExternalInput").ap()
    wkv_d = dt("wkv", [L, 128, 2, 2048], F32, kind="ExternalInput").ap()
    wo_d = dt("wo", [L, 16, 128, NCH, 128], F32, kind="ExternalInput").ap()
    gin_d = dt("gin", [L, 128, NCH], F32, kind="ExternalInput").ap()
    gq_d = dt("gq", [L, 128, 3], F32, kind="ExternalInput").ap()
    gkv_d = dt("gkv", [L, 128, 2], F32, kind="ExternalInput").ap()
    sink_d = dt("sinks", [L, 128, 16], F32, kind="ExternalInput").ap()
    gfin_d = dt("gfin", [128, D], F32, kind="ExternalInput").ap()
    cs_d = dt("cs", [2, 64, T], F32, kind="ExternalInput").ap()
    etab_d = dt("etab", [128, 3, 16, 128], BF16, kind="ExternalInput").ap()
    mask_d = dt("mask", [128, 2, 128], BF16, kind="ExternalInput").ap()
    ident_d = dt("ident", [128, 128], BF16, kind="ExternalInput").ap()
    out_d = dt("out", [T, D], F32, kind="ExternalOutput").ap()
    QA = dt("QA", [8, 128, T], BF16, kind="Internal").ap()
    GT = dt("GT", [T, 2048], BF16, kind="Internal").ap()
    GTB = dt("GTB", [8, 128, T], BF16, kind="Internal").ap()
    XS = [dt("X%d" % i, [T, D], F32, kind="Internal").ap() for i in range(2)]
    SND = [dt("SND%d" % k, [128, T], BF16).ap() for k in range(5)]
    RCV = [dt("RCV%d" % k, [256, T], BF16).ap() for k in range(5)]
    O_CK, O_KR, O_KA, O_VA = 0, 2 * T, 3 * T, 4 * T

    es = contextlib.ExitStack()
    kb = KB(nc, es)
    op, dma = kb.op, kb.dma

    PS = [es.enter_context(nc.psum_tensor("ps%d" % i, [128, 512], F32)) for i in range(8)]
    PD = [Dep() for _ in range(8)]
    gp = Ph(kb)
    ident = gp.sb("ident", [128, 128], BF16); ident_dp = gp.dep()
    cs = gp.sb("cs", [64, 2, T], F32); cs_dp = gp.dep()
    mask = gp.sb("mask", [128, 2, 128], BF16); mask_dp = gp.dep()
    cqnT = gp.sb("cqnT", [128, 3, T], BF16); cqnT_dp = [gp.dep() for _ in range(NS)]
    small = gp.sb("small", [128, 64], F32)
    small_dp = gp.dep()
    esink_dp = gp.dep()
    dma([(ident[:], ident_d)], W=[ident_dp], st=ident_dp)
    dma([(cs[:, 0, :], cs_d[0]), (cs[:, 1, :], cs_d[1])], W=[cs_dp], st=cs_dp)
    dma([(mask[:], mask_d)], W=[mask_dp], st=mask_dp)

    x_dp = [[Dep() for _ in range(NS)] for _ in range(3)]
    qa_dp = Dep(); gt_dp = [Dep() for _ in range(NS)]; yt_dp = Dep()
    snd_dp = [Dep() for _ in range(5)]; rcv_dp = [Dep() for _ in range(5)]
    gtb_dp = Dep()

    def bf(ps_ap):
        return ps_ap.bitcast(BF16)

    rr = {"ps": 0, "ev": 0}

    def evac_engine():
        rr["ev"] += 1
        return "act" if rr["ev"] % 2 else "dve"

    def copy(e, out, in_, R, W):
        if e == "act":
            return op("act", lambda g: g.copy(out=out, in_=in_), R=R, W=W)
        return op(e, lambda g: g.tensor_copy(out=out, in_=in_), R=R, W=W)

    for l in range(L):
        x_src = x_in if l == 0 else XS[(l - 1) % 2]
        x_src_dp = x_dp[0] if l == 0 else x_dp[1 + (l - 1) % 2]
        x_dst = XS[l % 2]
        x_dst_dp = x_dp[1 + l % 2]

        dma([(small[:, 0:16], gin_d[l]), (small[:, 16:19], gq_d[l]), (small[:, 19:21], gkv_d[l]),
             (small[:, 24:40], sink_d[l])], W=[small_dp], st=small_dp)
        op("act", lambda g: g.activation(out=small[:, 40:56], in_=small[:, 24:40], func=AF.Exp),
           R=[small_dp], W=[esink_dp])

        pa = Ph(kb)
        hT = pa.sb("hT", [128, NCH, T], BF16)
        hT_dp = [pa.dep() for _ in range(NS)]
        p1 = Ph(kb)
        xs = [p1.sb("xs%d" % i, [128, D], F32) for i in range(2)]; xs_dp = [p1.dep() for _ in range(2)]
        hb = [p1.sb("hb%d" % i, [128, D], BF16) for i in range(2)]; hb_dp = [p1.dep() for _ in range(2)]
        junk = p1.sb("junk", [128, D], BF16); junk_dp = p1.dep()
        st1 = p1.sb("st1", [128, 4 * NS], F32); st1_dp = [p1.dep() for _ in range(NS)]
        for s in range(NS):
            b = s % 2
            dma([(xs[b][:], x_src[s * 128:(s + 1) * 128, :])], R=[x_src_dp[s]], W=[xs_dp[b]], st=xs_dp[b])
            c0 = 4 * s
            op("act", lambda g: g.activation(out=junk[:], in_=xs[b][:], func=AF.Square,
                                             accum_out=st1[:, c0:c0 + 1]),
               R=[xs_dp[b]], W=[junk_dp, st1_dp[s]])
            op("dve", lambda g: g.tensor_scalar(out=st1[:, c0 + 1:c0 + 2], in0=st1[:, c0:c0 + 1],
                                                scalar1=1.0 / D, scalar2=EPS, op0=ALU.mult, op1=ALU.add),
               R=[], W=[st1_dp[s]])
            op("act", lambda g: g.activation(out=st1[:, c0 + 2:c0 + 3], in_=st1[:, c0 + 1:c0 + 2], func=AF.Sqrt),
               W=[st1_dp[s]])
            op("dve", lambda g: g.reciprocal(out=st1[:, c0 + 3:c0 + 4], in_=st1[:, c0 + 2:c0 + 3]),
               W=[st1_dp[s]])
            op("dve", lambda g: g.tensor_scalar(out=hb[b][:], in0=xs[b][:], scalar1=st1[:, c0 + 3:c0 + 4],
                                                scalar2=None, op0=ALU.mult),
               R=[xs_dp[b], st1_dp[s]], W=[hb_dp[b]])
            for cg in range(4):
                pi = rr["ps"] % 4; rr["ps"] += 1
                pv = bf(PS[pi][:])[:, 0:512].rearrange("p (a b) -> p a b", a=4)
                for a in range(4):
                    c = cg * 4 + a
                    op("pe", lambda g: g.transpose(out=pv[:, a, :], in_=hb[b][:, c * 128:(c + 1) * 128],
                                                   identity=ident[:]),
                       R=[hb_dp[b], ident_dp], W=[PD[pi]])
                copy(evac_engine(), hT[:, cg * 4:(cg + 1) * 4, s * 128:(s + 1) * 128], pv,
                     R=[PD[pi]], W=[hT_dp[s]])
        p1.close()

        p2 = Ph(kb)
        wst = [p2.sb("wst%d" % i, [128, NCH, 128], F32) for i in range(3)]; wst_dp = [p2.dep() for _ in range(3)]
        wb = [p2.sb("wb%d" % i, [128, NCH, 128], BF16) for i in range(2)]; wb_dp = [p2.dep() for _ in range(2)]
        wt = [p2.sb("wt%d" % i, [128, NCH, 512], BF16) for i in range(2)]; wt_dp = [p2.dep() for _ in range(2)]
        ofm = [p2.sb("ofm%d" % i, [128, 512], BF16) for i in range(3)]; ofm_dp = [p2.dep() for _ in range(3)]
        otm = [p2.sb("otm%d" % i, [128, 512], BF16) for i in range(3)]; otm_dp = [p2.dep() for _ in range(3)]
        sndt = [p2.sb("sndt%d" % k, [128, T], BF16) for k in range(5)]
        sn_dp = [p2.dep() for _ in range(5)]

        def send(k):
            dma([(SND[k], sndt[k][:])], R=[sn_dp[k]], W=[snd_dp[k]], st=sn_dp[k])
            kb.allgather(SND[k], RCV[k], R=[snd_dp[k]], W=[rcv_dp[k]])
        rp = [p2.sb("rp%d" % i, [64, 512], F32) for i in range(2)]; rp_dp = [p2.dep() for _ in range(2)]
        st2 = p2.sb("st2", [128, 8], F32); st2_dp = p2.dep()
        cn = p2.sb("cn", [128, 384], BF16); cn_dp = p2.dep()
        op("pool", lambda g: g.memset(sndt[2][64:128, :], 0.0), W=[sn_dp[2]])
        cnt = {"w": 0, "ofm": 0, "otm": 0}

        def load_piece(p):
            if p >= NPIECE:
                return
            i = p % 3
            dma([(wst[i][:], win[l, p])], W=[wst_dp[i]], st=wst_dp[i])

        def cast_piece(i, dst, dst_dp, gofs):
            for c in range(NCH):
                op("pool", lambda g: g.tensor_scalar(out=dst[:, c, :], in0=wst[i][:, c, :],
                                                     scalar1=small[:, gofs + c:gofs + c + 1], scalar2=0.0,
                                                     op0=ALU.mult, op1=ALU.add),
                   R=[wst_dp[i], small_dp], W=[dst_dp], nowait=(c > 0))

        load_piece(0)
        load_piece(1)
        for p in range(10):
            i = p % 3
            wi = p % 2
            cast_piece(i, wb[wi], wb_dp[wi], 0)
            load_piece(p + 2)
            for tg in range(NG):
                tsl = slice(tg * 512, (tg + 1) * 512)
                hdeps = hT_dp[tg * 4:(tg + 1) * 4]
                if p < 9:
                    pi = rr["ps"] % 4; rr["ps"] += 1
                    for c in range(NCH):
                        op("pe", lambda g: g.matmul(out=PS[pi][:], lhsT=wb[wi][:, c, :], rhs=hT[:, c, tsl],
                                                    start=(c == 0), stop=(c == NCH - 1)),
                           R=[wb_dp[wi]] + hdeps, W=[PD[pi]])
                    if p < 8:
                        oi = cnt["ofm"] % 3; cnt["ofm"] += 1
                        copy(evac_engine(), ofm[oi][:], PS[pi][:], R=[PD[pi]], W=[ofm_dp[oi]])
                        dma([(QA[p, :, tsl], ofm[oi][:])], R=[ofm_dp[oi]], W=[qa_dp], st=ofm_dp[oi])
                    else:
                        copy(evac_engine(), sndt[3][:, tsl], PS[pi][:], R=[PD[pi]], W=[sn_dp[3]])
                else:
                    pis = []
                    for half in range(2):
                        pi = rr["ps"] % 4; rr["ps"] += 1
                        pis.append(pi)
                        for c in range(NCH):
                            op("pe", lambda g: g.matmul(out=PS[pi][0:64, :], lhsT=wb[wi][:, c, half * 64:(half + 1) * 64],
                                                        rhs=hT[:, c, tsl], start=(c == 0), stop=(c == NCH - 1)),
                               R=[wb_dp[wi]] + hdeps, W=[PD[pi]])
                    op("dve", lambda g: g.tensor_tensor(out=rp[0][:], in0=PS[pis[0]][0:64, :], in1=cs[:, 0, tsl], op=ALU.mult),
                       R=[PD[pis[0]], cs_dp], W=[rp_dp[0]])
                    op("dve", lambda g: g.tensor_tensor(out=rp[1][:], in0=PS[pis[1]][0:64, :], in1=cs[:, 1, tsl], op=ALU.mult),
                       R=[PD[pis[1]], cs_dp], W=[rp_dp[1]])
                    op("dve", lambda g: g.tensor_tensor(out=sndt[2][0:64, tsl], in0=rp[0][:], in1=rp[1][:], op=ALU.add),
                       R=[rp_dp[0], rp_dp[1]], W=[sn_dp[2]])
            if p == 8:
                send(3)
            if p == 9:
                send(2)

        groups = [("cqva", [10, 11, 12, 13]), ("ckv", [14, 15]), ("ga0", [16, 17, 18, 19]), ("ga1", [20, 21, 22, 23])]
        for gi, (gname, pieces) in enumerate(groups):
            wi = gi % 2
            ncol = 128 * len(pieces)
            for k, p in enumerate(pieces):
                i = p % 3
                cast_piece(i, wt[wi][:, :, k * 128:(k + 1) * 128], wt_dp[wi], 0)
                load_piece(p + 2)
            for s in range(NS):
                pi = rr["ps"] % 4; rr["ps"] += 1
                for c in range(NCH):
                    op("pe", lambda g: g.matmul(out=PS[pi][:, 0:ncol], lhsT=hT[:, c, s * 128:(s + 1) * 128],
                                                rhs=wt[wi][:, c, 0:ncol], start=(c == 0), stop=(c == NCH - 1)),
                       R=[wt_dp[wi], hT_dp[s]], W=[PD[pi]])
                if gname in ("cqva", "ckv"):
                    nr = QL if gname == "cqva" else KVL
                    nck = nr // 128
                    op("act", lambda g: g.activation(out=cn[:, 0:nr], in_=PS[pi][:, 0:nr], func=AF.Square,
                                                     accum_out=st2[:, 0:1]),
                       R=[PD[pi]], W=[cn_dp, st2_dp])
                    op("dve", lambda g: g.tensor_scalar(out=st2[:, 1:2], in0=st2[:, 0:1], scalar1=1.0 / nr,
                                                        scalar2=EPS, op0=ALU.mult, op1=ALU.add), W=[st2_dp])
                    op("act", lambda g: g.activation(out=st2[:, 2:3], in_=st2[:, 1:2], func=AF.Sqrt), W=[st2_dp])
                    op("dve", lambda g: g.reciprocal(out=st2[:, 3:4], in_=st2[:, 2:3]), W=[st2_dp])
                    op("dve", lambda g: g.tensor_scalar(out=cn[:, 0:nr], in0=PS[pi][:, 0:nr], scalar1=st2[:, 3:4],
                                                        scalar2=None, op0=ALU.mult),
                       R=[PD[pi], st2_dp], W=[cn_dp])
                    if gname == "cqva":
                        copy("act", sndt[4][:, s * 128:(s + 1) * 128], PS[pi][:, 384:512],
                             R=[PD[pi]], W=[sn_dp[4]])
                    pj = 4 + (s % 2)
                    pv = bf(PS[pj][:])[:, 0:128 * nck].rearrange("p (a b) -> p a b", a=nck)
                    for a in range(nck):
                        op("pe", lambda g: g.transpose(out=pv[:, a, :], in_=cn[:, a * 128:(a + 1) * 128], identity=ident[:]),
                           R=[cn_dp, ident_dp], W=[PD[pj]])
                    if gname == "cqva":
                        copy("dve", cqnT[:, :, s * 128:(s + 1) * 128], pv, R=[PD[pj]], W=[cqnT_dp[s]])
                    else:
                        for a in range(2):
                            copy("dve", sndt[a][:, s * 128:(s + 1) * 128], pv[:, a, :], R=[PD[pj]], W=[sn_dp[a]])
                else:
                    oi = cnt["otm"] % 3; cnt["otm"] += 1
                    op("act", lambda g: g.activation(out=otm[oi][:], in_=PS[pi][:], func=AF.Silu),
                       R=[PD[pi]], W=[otm_dp[oi]])
                    col = (gi - 2) * 512
                    dma([(GT[s * 128:(s + 1) * 128, col:col + 512], otm[oi][:])], R=[otm_dp[oi]], W=[gt_dp[s]],
                        st=otm_dp[oi])
            if gname == "cqva":
                send(4)
            if gname == "ckv":
                send(0)
                send(1)
        for p in range(24, 32):
            i = p % 3
            wi = p % 2
            cast_piece(i, wb[wi], wb_dp[wi], 0)
            load_piece(p + 2)
            for tg in range(NG):
                tsl = slice(tg * 512, (tg + 1) * 512)
                hdeps = hT_dp[tg * 4:(tg + 1) * 4]
                pi = rr["ps"] % 4; rr["ps"] += 1
                for c in range(NCH):
                    op("pe", lambda g: g.matmul(out=PS[pi][:], lhsT=wb[wi][:, c, :], rhs=hT[:, c, tsl],
                                                start=(c == 0), stop=(c == NCH - 1)),
                       R=[wb_dp[wi]] + hdeps, W=[PD[pi]])
                oi = cnt["ofm"] % 3; cnt["ofm"] += 1
                op("act", lambda g: g.activation(out=ofm[oi][:], in_=PS[pi][:], func=AF.Silu),
                   R=[PD[pi]], W=[ofm_dp[oi]])
                dma([(GTB[p - 24, :, tsl], ofm[oi][:])], R=[ofm_dp[oi]], W=[gtb_dp], st=ofm_dp[oi])
        p2.close()
        pa.close()

        pY = Ph(kb)
        yta = pY.sb("yta", [128, NCH, T], BF16); yta_dp = [pY.dep() for _ in range(NCH)]
        pS = Ph(kb)
        qall = pS.sb("qall", [128, 8, T], BF16); qall_dp = pS.dep()
        kaT = pS.sb("kaT", [128, 2, T], BF16); kaT_dp = pS.dep()
        vraw = pS.sb("vraw", [128, 2, T], BF16); vraw_dp = pS.dep()
        vaug = pS.sb("vaug", [128, 2 * NS * 2, 66], BF16); vaug_dp = pS.dep()
        etab = pS.sb("etab", [128, 3, 16, 128], BF16); etab_dp = pS.dep()
        gat = [pS.sb("gat%d" % i, [128, 1024], BF16) for i in range(2)]; gat_dp = [pS.dep() for _ in range(2)]
        pex = [pS.sb("pex%d" % i, [128, 512], BF16) for i in range(3)]; pex_dp = [pS.dep() for _ in range(3)]
        ptS = [pS.sb("ptS%d" % i, [128, 512], BF16) for i in range(6)]; ptS_dp = [pS.dep() for _ in range(6)]
        ysw = [pS.sb("ysw%d" % i, [128, 1024], BF16) for i in range(2)]; ysw_dp = [pS.dep() for _ in range(2)]
        lS = pS.sb("lS", [128, 8], F32); lS_dp = pS.dep()
        dma([(etab[:], etab_d)], W=[etab_dp], st=etab_dp)
        dma([(qall[:, j, :], QA[j]) for j in range(8)], R=[qa_dp], W=[qall_dp], st=qall_dp)
        dma([(kaT[:, r, :], RCV[3][r * 128:(r + 1) * 128, :]) for r in range(2)], R=[rcv_dp[3]], W=[kaT_dp], st=kaT_dp)
        dma([(vraw[:, r, :], RCV[4][r * 128:(r + 1) * 128, :]) for r in range(2)], R=[rcv_dp[4]], W=[vraw_dp], st=vraw_dp)
        op("pool", lambda g: g.memset(vaug[:], 1.0), W=[vaug_dp])
        for r in range(2):
            dst = vaug[:, r * NS * 2:(r + 1) * NS * 2, 0:64]
            src = vraw[:, r, :].rearrange("p (a d) -> p a d", d=64)
            op("pool", lambda g: g.tensor_copy(out=dst, in_=src), R=[vraw_dp], W=[vaug_dp])
        cS = {"pex": 0, "pt": 0}
        units = [(s, gk, quad) for s in range(NS) for gk in range(2) for quad in range(2)]
        s1out = {}

        def stage1(u):
            s, gk, quad = units[u]
            b = s % 2
            if gk == 0 and quad == 0:
                dma([(gat[b][:], GT[s * 128:(s + 1) * 128, 0:1024])], R=[gt_dp[s]], W=[gat_dp[b]], st=gat_dp[b])
            cands = [(0, 1, s - 1), (1, 0, s), (2, 1, s)]
            if s == 0:
                cands = cands[1:]
            prt = slice(gk * 64, (gk + 1) * 64)
            h0 = gk * 8 + quad * 4
            pts = []
            for (ci, r, ks) in cands:
                pi = rr["ps"] % 2; rr["ps"] += 1
                op("pe", lambda g: g.matmul(out=PS[pi][:], lhsT=kaT[prt, r, ks * 128:(ks + 1) * 128],
                                            rhs=qall[prt, quad * 4:(quad + 1) * 4, s * 128:(s + 1) * 128],
                                            start=True, stop=True),
                   R=[kaT_dp, qall_dp], W=[PD[pi]])
                xi = cS["pex"] % 3; cS["pex"] += 1
                op("act", lambda g: g.activation(out=pex[xi][:], in_=PS[pi][:], func=AF.Exp, scale=0.125),
                   R=[PD[pi]], W=[pex_dp[xi]])
                ti = cS["pt"] % 6; cS["pt"] += 1
                op("dve" if ti % 2 == 0 else "pool", lambda g: g.tensor_tensor(out=ptS[ti][:].rearrange("p (a b) -> p a b", a=4),
                                                    in0=pex[xi][:].rearrange("p (a b) -> p a b", a=4),
                                                    in1=etab[:, ci, h0:h0 + 4, :], op=ALU.mult),
                   R=[pex_dp[xi], etab_dp], W=[ptS_dp[ti]])
                pts.append((ti, r, ks))
            s1out[u] = pts

        def stage2(u):
            s, gk, quad = units[u]
            b = s % 2
            pts = s1out.pop(u)
            oi = 2 + gk * 2 + quad
            h0 = gk * 8 + quad * 4
            ov = PS[oi][:, 0:4 * 65].rearrange("p (a b) -> p a b", a=4)
            for hq in range(4):
                for n, (ti, r, ks) in enumerate(pts):
                    op("pe", lambda g: g.matmul(out=ov[:, hq, :], lhsT=ptS[ti][:, hq * 128:(hq + 1) * 128],
                                                rhs=vaug[:, (r * NS + ks) * 2 + gk, 0:65],
                                                start=(n == 0), stop=(n == len(pts) - 1)),
                       R=[ptS_dp[ti], vaug_dp], W=[PD[oi]])
            op("dve", lambda g: g.tensor_tensor(out=lS[:, 0:4], in0=ov[:, :, 64], in1=small[:, 40 + h0:44 + h0], op=ALU.add),
               R=[PD[oi], esink_dp], W=[lS_dp])
            op("dve", lambda g: g.reciprocal(out=lS[:, 4:8], in_=lS[:, 0:4]), W=[lS_dp])
            for hq in range(4):
                h = h0 + hq
                op("dve", lambda g: g.scalar_tensor_tensor(out=ysw[b][:, h * 64:(h + 1) * 64], in0=ov[:, hq, 0:64],
                                                            scalar=lS[:, 4 + hq:5 + hq], in1=gat[b][:, h * 64:(h + 1) * 64],
                                                            op0=ALU.mult, op1=ALU.mult),
                   R=[PD[oi], lS_dp, gat_dp[b]], W=[ysw_dp[b]], nowait=(hq > 0))
            if gk == 1 and quad == 1:
                pj = 6 + (s % 2)
                pv = bf(PS[pj][:]).rearrange("p (a b) -> p a b", a=8)
                for a in range(8):
                    op("pe", lambda g: g.transpose(out=pv[:, a, :], in_=ysw[b][:, a * 128:(a + 1) * 128], identity=ident[:]),
                       R=[ysw_dp[b], ident_dp], W=[PD[pj]])
                copy("act", yta[:, 0:8, s * 128:(s + 1) * 128], pv, R=[PD[pj]], W=yta_dp[0:8])

        stage1(0)
        for u in range(len(units)):
            if u + 1 < len(units):
                stage1(u + 1)
            stage2(u)
        pS.close()

        pC = Ph(kb)
        ck = pC.sb("ck", [128, 2, 2, T], BF16); ck_dp = pC.dep()
        kr = pC.sb("kr", [64, 2, T], BF16); kr_dp = pC.dep()
        wq = pC.sb("wq", [128, 3, 2048], BF16); wq_dp = pC.dep()
        wkv = pC.sb("wkv", [128, 2, 2048], BF16); wkv_dp = pC.dep()
        wst2 = [pC.sb("wsc%d" % i, [128, 1024], F32) for i in range(2)]; wst2_dp = [pC.dep() for _ in range(2)]
        khT = [pC.sb("khT%d" % i, [128, 2 * T], BF16) for i in range(2)]; khT_dp = [pC.dep() for _ in range(2)]
        vh = [pC.sb("vh%d" % i, [128, 2 * NS, 130], BF16) for i in range(2)]; vh_dp = [pC.dep() for _ in range(2)]
        gbt = [pC.sb("gbt%d" % i, [128, 512], BF16) for i in range(2)]; gbt_dp = [pC.dep() for _ in range(2)]
        onesb = pC.sb("onesb", [128, 128], BF16); onesb_dp = pC.dep()
        rLt = [pC.sb("rLt%d" % i, [128, 512], F32) for i in range(2)]; rLt_dp = [pC.dep() for _ in range(2)]
        op("pool", lambda g: g.memset(onesb[:], 1.0), W=[onesb_dp])
        qn = [pC.sb("qn%d" % i, [128, 512], BF16) for i in range(2)]; qn_dp = [pC.dep() for _ in range(2)]
        qr = [pC.sb("qr%d" % i, [64, 512], BF16) for i in range(2)]; qr_dp = [pC.dep() for _ in range(2)]
        rq = [pC.sb("rq%d" % i, [64, 512], F32) for i in range(2)]; rq_dp = [pC.dep() for _ in range(2)]
        ptC = [pC.sb("ptC%d" % i, [128, 512], BF16) for i in range(4)]; ptC_dp = [pC.dep() for _ in range(4)]
        for r in range(2):
            dma([(ck[:, c, r, :], RCV[c][r * 128:(r + 1) * 128, :]) for c in range(2)]
                + [(kr[:, r, :], RCV[2][r * 128:r * 128 + 64, :])],
                R=rcv_dp[0:3], W=[ck_dp, kr_dp], st=ck_dp if r == 0 else kr_dp)
        nst = 0
        for (wsrc, wdst, wdp, nchk, gofs) in ((wkv_d, wkv, wkv_dp, 2, 19), (wq_d, wq, wq_dp, 3, 16)):
            for c in range(nchk):
                for hf in range(2):
                    i = nst % 2; nst += 1
                    csl = slice(hf * 1024, (hf + 1) * 1024)
                    dma([(wst2[i][:], wsrc[l, :, c, csl])], W=[wst2_dp[i]], st=wst2_dp[i])
                    op("pool", lambda g: g.tensor_scalar(out=wdst[:, c, csl], in0=wst2[i][:], scalar1=small[:, gofs + c:gofs + c + 1],
                                                         scalar2=0.0, op0=ALU.mult, op1=ALU.add),
                       R=[wst2_dp[i], small_dp], W=[wdp])
        for i in range(2):
            op("pool", lambda g: g.memset(vh[i][:, :, 128:130], 1.0), W=[vh_dp[i]])
        cC = {"pt": 0, "g": 0, "s": 0}
        scale = float((128 + 64) ** -0.5)
        NKS = 2 * NS

        def prep(h):
            hb_ = h % 2
            for kg in range(2 * T // 512):
                r, t0 = divmod(kg * 512, T)
                pi = 5 + rr["ps"] % 3; rr["ps"] += 1
                for c in range(2):
                    op("pe", lambda g: g.matmul(out=PS[pi][:], lhsT=wkv[:, c, h * 256:h * 256 + 128],
                                                rhs=ck[:, c, r, t0:t0 + 512], start=(c == 0), stop=(c == 1)),
                       R=[wkv_dp, ck_dp], W=[PD[pi]])
                copy("dve", khT[hb_][:, kg * 512:(kg + 1) * 512], PS[pi][:], R=[PD[pi]], W=[khT_dp[hb_]])
            for k4 in range(NKS // 4):
                pi = 5 + rr["ps"] % 3; rr["ps"] += 1
                for a in range(4):
                    ksl = k4 * 4 + a
                    r, s_ = divmod(ksl, NS)
                    for c in range(2):
                        op("pe", lambda g: g.matmul(out=PS[pi][:, a * 128:(a + 1) * 128],
                                                    lhsT=ck[:, c, r, s_ * 128:(s_ + 1) * 128],
                                                    rhs=wkv[:, c, h * 256 + 128:h * 256 + 256],
                                                    start=(c == 0), stop=(c == 1)),
                           R=[wkv_dp, ck_dp], W=[PD[pi]])
                copy("dve", vh[hb_][:, k4 * 4:(k4 + 1) * 4, 0:128],
                     PS[pi][:].rearrange("p (a b) -> p a b", a=4), R=[PD[pi]], W=[vh_dp[hb_]])

        prep(0)
        pending = []
        for h in range(8):
            hb_ = h % 2
            for G in range(NG):
                gb_ = cC["g"] % 2; cC["g"] += 1
                tsl = slice(G * 512, (G + 1) * 512)
                cdeps = cqnT_dp[G * 4:(G + 1) * 4]
                dma([(gbt[gb_][:], GTB[h, :, tsl])], R=[gtb_dp], W=[gbt_dp[gb_]], st=gbt_dp[gb_])
                pi = 5 + rr["ps"] % 3; rr["ps"] += 1
                for c in range(3):
                    op("pe", lambda g: g.matmul(out=PS[pi][:], lhsT=wq[:, c, h * 256:h * 256 + 128], rhs=cqnT[:, c, tsl],
                                                start=(c == 0), stop=(c == 2)),
                       R=[wq_dp] + cdeps, W=[PD[pi]])
                copy("dve", qn[gb_][:], PS[pi][:], R=[PD[pi]], W=[qn_dp[gb_]])
                for half in range(2):
                    pi = 5 + rr["ps"] % 3; rr["ps"] += 1
                    c0 = h * 256 + 128 + half * 64
                    for c in range(3):
                        op("pe", lambda g: g.matmul(out=PS[pi][0:64, :], lhsT=wq[:, c, c0:c0 + 64], rhs=cqnT[:, c, tsl],
                                                    start=(c == 0), stop=(c == 2)),
                           R=[wq_dp] + cdeps, W=[PD[pi]])
                    op("dve", lambda g: g.tensor_tensor(out=rq[half][:], in0=PS[pi][0:64, :], in1=cs[:, half, tsl], op=ALU.mult),
                       R=[PD[pi], cs_dp], W=[rq_dp[half]])
                op("dve", lambda g: g.tensor_tensor(out=qr[gb_][:], in0=rq[0][:], in1=rq[1][:], op=ALU.add),
                   R=[rq_dp[0], rq_dp[1]], W=[qr_dp[gb_]])
                if G == min(1, NG - 1) and h + 1 < 8:
                    prep(h + 1)
                ents = [(r, s_, 0, False) for s_ in range(4 * G) for r in range(2)]
                ents += [(r, 4 * G + j, j, True) for j in range(4) for r in range(2)]
                NE = len(ents)
                st_ = {}

                def emit_S(n):
                    r, s_, jmin, msk = ents[n]
                    ncol = (4 - jmin) * 128
                    pi = cC["s"] % 3; cC["s"] += 1
                    kcol = r * T + s_ * 128
                    op("pe", lambda g: g.matmul(out=PS[pi][:, 0:ncol], lhsT=khT[hb_][:, kcol:kcol + 128],
                                                rhs=qn[gb_][:, jmin * 128:512], start=True, stop=False),
                       R=[khT_dp[hb_], qn_dp[gb_]], W=[PD[pi]])
                    op("pe", lambda g: g.matmul(out=PS[pi][:, 0:ncol], lhsT=kr[:, r, s_ * 128:(s_ + 1) * 128],
                                                rhs=qr[gb_][:, jmin * 128:512], start=False, stop=True),
                       R=[kr_dp, qr_dp[gb_]], W=[PD[pi]])
                    ti = cC["pt"] % 4; cC["pt"] += 1
                    op("act", lambda g: g.activation(out=ptC[ti][:, 0:ncol], in_=PS[pi][:, 0:ncol], func=AF.Exp, scale=scale),
                       R=[PD[pi]], W=[ptC_dp[ti]])
                    if msk:
                        op("dve", lambda g: g.tensor_tensor(out=ptC[ti][:, 0:128], in0=ptC[ti][:, 0:128], in1=mask[:, r, :], op=ALU.mult),
                           R=[mask_dp], W=[ptC_dp[ti]])
                    st_[n] = ti

                bO = 3

                def emit_PV(n):
                    r, s_, jmin, msk = ents[n]
                    ti = st_[n]
                    ncol = (4 - jmin) * 128
                    op("pe", lambda g: g.matmul(out=PS[bO][:, jmin * 128:512], lhsT=vh[hb_][:, r * NS + s_, 0:128],
                                                rhs=ptC[ti][:, 0:ncol], start=(n == 0), stop=(n == NE - 1)),
                       R=[ptC_dp[ti], vh_dp[hb_]], W=[PD[bO]])
                    op("pe", lambda g: g.matmul(out=PS[bO + 1][:, jmin * 128:512], lhsT=onesb[:],
                                                rhs=ptC[ti][:, 0:ncol], start=(n == 0), stop=(n == NE - 1)),
                       R=[ptC_dp[ti], onesb_dp], W=[PD[bO + 1]])

                emit_S(0)
                emit_S(1)
                for n in range(NE):
                    if n + 2 < NE:
                        emit_S(n + 2)
                    emit_PV(n)
                op("dve", lambda g: g.reciprocal(out=rLt[gb_][:], in_=PS[bO + 1][:]), R=[PD[bO + 1]], W=[rLt_dp[gb_]])
                op("dve", lambda g: g.tensor_tensor(out=rLt[gb_][:], in0=PS[bO][:], in1=rLt[gb_][:], op=ALU.mult),
                   R=[PD[bO]], W=[rLt_dp[gb_]])
                op("dve", lambda g: g.tensor_tensor(out=yta[:, 8 + h, tsl], in0=rLt[gb_][:], in1=gbt[gb_][:], op=ALU.mult),
                   R=[rLt_dp[gb_], gbt_dp[gb_]], W=[yta_dp[8 + h]])
        pC.close()

        pD = Ph(kb)
        wsd = [pD.sb("wsd%d" % i, [128, NCH, 128], F32) for i in range(2)]; wsd_dp = [pD.dep() for _ in range(2)]
        wo = [pD.sb("wo%d" % i, [128, NCH, 512], BF16) for i in range(2)]; wo_dp = [pD.dep() for _ in range(2)]
        xr = [pD.sb("xr%d" % i, [128, 512], F32) for i in range(3)]; xr_dp = [pD.dep() for _ in range(3)]
        xo = [pD.sb("xo%d" % i, [128, 512], F32) for i in range(3)]; xo_dp = [pD.dep() for _ in range(3)]
        cD = {"w": 0, "x": 0}
        def load_wo(cg):
            wi = cg % 2
            for k in range(4):
                i = cD["w"] % 2; cD["w"] += 1
                dma([(wsd[i][:], wo_d[l, cg * 4 + k])], W=[wsd_dp[i]], st=wsd_dp[i])
                op("pool", lambda g: g.tensor_copy(out=wo[wi][:, :, k * 128:(k + 1) * 128], in_=wsd[i][:]),
                   R=[wsd_dp[i]], W=[wo_dp[wi]])

        load_wo(0)
        for cg in range(4):
            wi = cg % 2
            if cg + 1 < 4:
                load_wo(cg + 1)
            for s in range(NS):
                xi = cD["x"] % 3; cD["x"] += 1
                dma([(xr[xi][:], x_src[s * 128:(s + 1) * 128, cg * 512:(cg + 1) * 512])], R=[x_src_dp[s]], W=[xr_dp[xi]], st=xr_dp[xi])
                pi = rr["ps"] % 4; rr["ps"] += 1
                for c in range(NCH):
                    op("pe", lambda g: g.matmul(out=PS[pi][:], lhsT=yta[:, c, s * 128:(s + 1) * 128], rhs=wo[wi][:, c, :],
                                                start=(c == 0), stop=(c == NCH - 1)),
                       R=yta_dp + [wo_dp[wi]], W=[PD[pi]])
                op("dve", lambda g: g.tensor_tensor(out=xo[xi][:], in0=PS[pi][:], in1=xr[xi][:], op=ALU.add),
                   R=[PD[pi], xr_dp[xi]], W=[xo_dp[xi]])
                dma([(x_dst[s * 128:(s + 1) * 128, cg * 512:(cg + 1) * 512], xo[xi][:])], R=[xo_dp[xi]], W=[x_dst_dp[s]], st=xo_dp[xi])
        pD.close()
        pY.close()

    pF = Ph(kb)
    x_src = XS[(L - 1) % 2]
    x_src_dp = x_dp[1 + (L - 1) % 2]
    gfin = pF.sb("gfin", [128, D], F32); gfin_dp = pF.dep()
    xs = [pF.sb("fx%d" % i, [128, D], F32) for i in range(2)]; xs_dp = [pF.dep() for _ in range(2)]
    fo = [pF.sb("fo%d" % i, [128, D], F32) for i in range(2)]; fo_dp = [pF.dep() for _ in range(2)]
    junk = pF.sb("fjunk", [128, D], BF16); junk_dp = pF.dep()
    st1 = pF.sb("fst", [128, 4 * NS], F32); st1_dp = [pF.dep() for _ in range(NS)]
    out_dp = Dep()
    dma([(gfin[:], gfin_d)], W=[gfin_dp], st=gfin_dp)
    for s in range(NS):
        b = s % 2
        dma([(xs[b][:], x_src[s * 128:(s + 1) * 128, :])], R=[x_src_dp[s]], W=[xs_dp[b]], st=xs_dp[b])
        c0 = 4 * s
        op("act", lambda g: g.activation(out=junk[:], in_=xs[b][:], func=AF.Square, accum_out=st1[:, c0:c0 + 1]),
           R=[xs_dp[b]], W=[junk_dp, st1_dp[s]])
        op("dve", lambda g: g.tensor_scalar(out=st1[:, c0 + 1:c0 + 2], in0=st1[:, c0:c0 + 1], scalar1=1.0 / D, scalar2=EPS,
                                            op0=ALU.mult, op1=ALU.add), W=[st1_dp[s]])
        op("act", lambda g: g.activation(out=st1[:, c0 + 2:c0 + 3], in_=st1[:, c0 + 1:c0 + 2], func=AF.Sqrt), W=[st1_dp[s]])
        op("dve", lambda g: g.reciprocal(out=st1[:, c0 + 3:c0 + 4], in_=st1[:, c0 + 2:c0 + 3]), W=[st1_dp[s]])
        op("dve", lambda g: g.scalar_tensor_tensor(out=fo[b][:], in0=xs[b][:], scalar=st1[:, c0 + 3:c0 + 4], in1=gfin[:],
                                                    op0=ALU.mult, op1=ALU.mult),
           R=[xs_dp[b], st1_dp[s], gfin_dp], W=[fo_dp[b]])
        dma([(out_d[s * 128:(s + 1) * 128, :], fo[b][:])], R=[fo_dp[b]], W=[out_dp], st=fo_dp[b])
    pF.close()
    gp.es.close()
    es.close()
    return nc


def _host_inputs(x, attn_norm_g, w_in, swa_sinks, q_a_norm_g, kv_a_norm_g, w_q_b, w_kv_b, w_out, final_norm_g, NS):
    L = w_in.shape[0]
    B = x.shape[0]
    T = NS * 128
    f32 = np.float32
    A_Q, A_KV, A_G = 1024, 128, 1024
    o_qa, o_ka, o_va, o_ga = 0, A_Q, A_Q + A_KV, A_Q + 2 * A_KV
    o_cq = o_ga + A_G
    o_ckv = o_cq + QL
    o_kr = o_ckv + KVL
    o_gb = o_kr + 64
    qa_cols = []
    for j in range(8):
        for half in range(2):
            hd = j + 8 * half
            qa_cols += list(range(o_qa + hd * 64, o_qa + (hd + 1) * 64))
    kr_cols = list(range(o_kr, o_kr + 64))
    kr_sw = list(range(o_kr + 32, o_kr + 64)) + list(range(o_kr, o_kr + 32))
    cols = (qa_cols + list(range(o_ka, o_ka + 128)) + kr_cols + kr_sw + list(range(o_cq, o_cq + QL))
            + list(range(o_va, o_va + 128)) + list(range(o_ckv, o_ckv + KVL)) + list(range(o_ga, o_ga + A_G))
            + list(range(o_gb, o_gb + 1024)))
    cols = np.asarray(cols)
    assert cols.size == NPIECE * 128
    wp = w_in[:, :, cols]
    win = np.ascontiguousarray(wp.reshape(L, NCH, 128, NPIECE, 128).transpose(0, 3, 2, 1, 4))
    qcols = []
    for h in range(8):
        b0 = h * 192
        qcols += list(range(b0, b0 + 192)) + list(range(b0 + 160, b0 + 192)) + list(range(b0 + 128, b0 + 160))
    wqp = w_q_b[:, :, np.asarray(qcols)]
    wq = np.ascontiguousarray(wqp.reshape(L, 3, 128, 2048).transpose(0, 2, 1, 3))
    wkv = np.ascontiguousarray(w_kv_b.reshape(L, 2, 128, 2048).transpose(0, 2, 1, 3))
    wo = np.ascontiguousarray(w_out.reshape(L, NCH, 128, 16, 128).transpose(0, 3, 2, 1, 4))
    gin = np.ascontiguousarray(attn_norm_g.reshape(L, NCH, 128).transpose(0, 2, 1))
    gq = np.ascontiguousarray(q_a_norm_g.reshape(L, 3, 128).transpose(0, 2, 1))
    gkv = np.ascontiguousarray(kv_a_norm_g.reshape(L, 2, 128).transpose(0, 2, 1))
    sinks = np.ascontiguousarray(np.broadcast_to(swa_sinks[:, None, :], (L, 128, 16)))
    gfin = np.ascontiguousarray(np.broadcast_to(final_norm_g[None, :], (128, D)))
    ident = np.eye(128, dtype=f32).astype(ml_dtypes.bfloat16)
    S_full = 2 * T
    pos = np.arange(S_full, dtype=f32)
    inv_freq = (10000.0 ** (-np.arange(0, 64, 2, dtype=f32) / 64)).astype(f32)
    ang = pos[:, None] * inv_freq[None, :]
    cos, sin = np.cos(ang).astype(f32), np.sin(ang).astype(f32)
    cos2 = np.concatenate([cos, cos], 1).T
    sin2 = np.concatenate([-sin, sin], 1).T
    slopes = np.exp2(-8.0 * np.arange(1, 17, dtype=f32) / 16).astype(f32)
    kk = np.arange(128)[:, None]
    qq = np.arange(128)[None, :]
    d_prev = (128 + qq - kk).astype(f32)
    d_cur = (qq - kk).astype(f32)
    E_prev = np.where((kk > qq)[:, None, :], np.exp(-slopes[None, :, None] * np.clip(d_prev, 0, 128)[:, None, :]), 0.0).astype(f32)
    E_cur = np.where((kk <= qq)[:, None, :], np.exp(-slopes[None, :, None] * np.clip(d_cur, 0, 128)[:, None, :]), 0.0).astype(f32)
    Z = np.zeros_like(E_prev)
    tri = (kk <= qq).astype(f32)
    ones = np.ones_like(tri)
    zer = np.zeros_like(tri)
    in_maps = []
    for c in range(2 * B):
        b, r = divmod(c, 2)
        xb = x[b].reshape(2 * NS, 128, D)[r::2].reshape(T, D)
        tok = (np.arange(NS)[:, None] * 2 + r) * 128 + np.arange(128)[None, :]
        tok = tok.reshape(-1)
        cs = np.ascontiguousarray(np.stack([cos2[:, tok], sin2[:, tok]], 0))
        if r == 0:
            et = np.stack([E_prev, E_cur, Z], 1)
            mk = np.stack([tri, zer], 1)
        else:
            et = np.stack([Z, E_prev, E_cur], 1)
            mk = np.stack([ones, tri], 1)
        in_maps.append({
            "x": np.ascontiguousarray(xb), "win": win, "wq": wq, "wkv": wkv, "wo": wo, "gin": gin, "gq": gq,
            "gkv": gkv, "sinks": sinks, "gfin": gfin, "cs": cs.astype(f32),
            "etab": np.ascontiguousarray(et).astype(ml_dtypes.bfloat16),
            "mask": np.ascontiguousarray(mk).astype(ml_dtypes.bfloat16), "ident": ident,
        })
    return in_maps


def _assemble(results, B, NS):
    T = NS * 128
    out = np.empty((B, 2 * NS, 128, D), np.float32)
    for c in range(2 * B):
        b, r = divmod(c, 2)
        out[b, r::2] = np.asarray(results[c]["out"]).reshape(NS, 128, D)
    return out.reshape(B, 2 * T, D)


_NC_CACHE = {}


def kernel(x, attn_norm_g, w_in, swa_sinks, q_a_norm_g, kv_a_norm_g, w_q_b, w_kv_b, w_out, final_norm_g):
    args = [np.asarray(a, dtype=np.float32) for a in (x, attn_norm_g, w_in, swa_sinks, q_a_norm_g, kv_a_norm_g,
                                                      w_q_b, w_kv_b, w_out, final_norm_g)]
    B, S = args[0].shape[0], args[0].shape[1]
    NS = S // 256
    L = args[2].shape[0]
    in_maps = _host_inputs(*args, NS=NS)
    key = (NS, L)
    if key not in _NC_CACHE:
        _NC_CACHE[key] = build(NS, L)
    res = run_bass_kernel_spmd(_NC_CACHE[key], in_maps, core_ids=list(range(2 * B)))
    return _assemble(res.results, B, NS)
```

```python
import contextlib
import numpy as np
import ml_dtypes
import concourse.bass as bass
import concourse.mybir as mybir
from concourse.bass_utils import run_bass_kernel_spmd

F32 = mybir.dt.float32
BF16 = mybir.dt.bfloat16
AF = mybir.ActivationFunctionType
ALU = mybir.AluOpType

D = 2048
NCH = 16
EPS = 1e-6
QL = 384
KVL = 256
NPIECE = 32
PAIRS = [[0, 1], [2, 3], [4, 5], [6, 7]]
EMBED_WAIT = True


class Dep:
    __slots__ = ("w", "r", "ds")

    def __init__(self):
        self.w = None
        self.r = {}
        self.ds = None


class KB:
    def __init__(self, nc, es, n_dsem=70):
        self.nc = nc
        self.eng = {"pe": nc.tensor, "act": nc.scalar, "dve": nc.vector,
                    "pool": nc.gpsimd, "sp": nc.sync}
        self.sem = {e: es.enter_context(nc.semaphore("s_" + e)) for e in ("pe", "act", "dve", "pool")}
        self.cnt = {e: 0 for e in self.sem}
        self.known = {e: {} for e in self.eng}
        self.dsems = [[es.enter_context(nc.semaphore("d%d" % i)), 0] for i in range(n_dsem)]
        self.dfree = list(range(n_dsem))
        self.ccsem = es.enter_context(nc.semaphore("ccs"))
        self.cccnt = 0

    def _wait(self, e, ev):
        if ev is None:
            return
        s, v = ev
        k = self.known[e]
        if k.get(id(s), 0) >= v:
            return
        if e == "pe" and s is self.sem["pe"]:
            return
        k[id(s)] = v
        p = self.pend
        if id(s) in p:
            p[id(s)] = (s, max(v, p[id(s)][1]))
        else:
            p[id(s)] = (s, v)

    def _deps(self, e, R, W):
        self.pend = {}
        for d in R:
            self._wait(e, d.w)
        for d in W:
            self._wait(e, d.w)
            for ev in d.r.values():
                self._wait(e, ev)
        return list(self.pend.values())

    def _emit_waits(self, e, waits, keep_last):
        n = len(waits) - (1 if keep_last and waits else 0)
        for (s, v) in waits[:n]:
            self.eng[e].wait_ge(s, v)
        return waits[n:]

    def op(self, e, fn, R=(), W=(), nowait=False):
        rest = []
        if not nowait:
            rest = self._emit_waits(e, self._deps(e, R, W), EMBED_WAIT)
        ins = fn(self.eng[e])
        for (s_, v_) in rest:
            ins._wait_ge(s_, v_)
        self.cnt[e] += 1
        ins.then_inc(self.sem[e], 1)
        ev = (self.sem[e], self.cnt[e])
        for d in R:
            d.r[e] = ev
        for d in W:
            d.w = ev
            d.r = {}
        return ev

    def dsem_alloc(self, dep):
        dep.ds = self.dfree.pop()
        return dep

    def dsem_free(self, dep):
        if dep.ds is not None:
            self.dfree.append(dep.ds)
            dep.ds = None

    def dma(self, pairs, R=(), W=(), st=None, q="sp"):
        if st.ds is None:
            self.dsem_alloc(st)
        self._emit_waits(q, self._deps(q, R, W), False)
        slot = self.dsems[st.ds]
        for (o, i) in pairs:
            ins = self.eng[q].dma_start(out=o, in_=i)
            slot[1] += 16
            ins.then_inc(slot[0], 16)
        ev = (slot[0], slot[1])
        for d in R:
            d.r[("d", st.ds)] = ev
        for d in W:
            d.w = ev
            d.r = {}
        return ev

    def allgather(self, src, dst, R=(), W=()):
        self._emit_waits("pool", self._deps("pool", R, W), False)
        ins = self.nc.gpsimd.collective_compute("AllGather", ALU.bypass, replica_groups=PAIRS,
                                                ins=[src], outs=[dst])
        self.cccnt += 1
        ins.then_inc(self.ccsem, 1)
        ev = (self.ccsem, self.cccnt)
        for d in R:
            d.r["cc"] = ev
        for d in W:
            d.w = ev
            d.r = {}

    def barrier(self):
        evs = [(self.sem[e], self.cnt[e]) for e in self.sem if self.cnt[e] > 0]
        evs += [(s, c) for (s, c) in self.dsems if c > 0]
        if self.cccnt:
            evs.append((self.ccsem, self.cccnt))
        for e in self.eng:
            self.pend = {}
            for ev in evs:
                self._wait(e, ev)
            self._emit_waits(e, list(self.pend.values()), False)


class Ph:
    uid = 0

    def __init__(self, kb):
        self.kb = kb
        self.es = contextlib.ExitStack()
        self.deps = []

    def sb(self, name, shape, dt):
        Ph.uid += 1
        t = self.es.enter_context(self.kb.nc.sbuf_tensor("sb%d_%s" % (Ph.uid, name), list(shape), dt))
        return t

    def dep(self):
        d = Dep()
        self.deps.append(d)
        return d

    def close(self):
        self.kb.barrier()
        for d in self.deps:
            self.kb.dsem_free(d)
        self.es.close()


def build(NS=16, L=4):
    T = NS * 128
    NG = NS // 4
    nc = bass.Bass("TRN2", target_bir_lowering=False)
    dt = nc.dram_tensor
    x_in = dt("x", [T, D], F32, kind="ExternalInput").ap()
    win = dt("win", [L, NPIECE, 128, NCH, 128], F32, kind="ExternalInput").ap()
    wq_d = dt("wq", [L, 128, 3, 2048], F32, kind="ExternalInput").ap()
    wkv_d = dt("wkv", [L, 128, 2, 2048], F32, kind="ExternalInput").ap()
    wo_d = dt("wo", [L, 16, 128, NCH, 128], F32, kind="ExternalInput").ap()
    gin_d = dt("gin", [L, 128, NCH], F32, kind="ExternalInput").ap()
    gq_d = dt("gq", [L, 128, 3], F32, kind="ExternalInput").ap()
    gkv_d = dt("gkv", [L, 128, 2], F32, kind="ExternalInput").ap()
    sink_d = dt("sinks", [L, 128, 16], F32, kind="ExternalInput").ap()
    gfin_d = dt("gfin", [128, D], F32, kind="ExternalInput").ap()
    cs_d = dt("cs", [2, 64, T], F32, kind="ExternalInput").ap()
    etab_d = dt("etab", [128, 3, 16, 128], BF16, kind="ExternalInput").ap()
    mask_d = dt("mask", [128, 2, 128], BF16, kind="ExternalInput").ap()
    ident_d = dt("ident", [128, 128], BF16, kind="ExternalInput").ap()
    out_d = dt("out", [T, D], F32, kind="ExternalOutput").ap()
    QA = dt("QA", [8, 128, T], BF16, kind="Internal").ap()
    GT = dt("GT", [T, 2048], BF16, kind="Internal").ap()
    GTB = dt("GTB", [8, 128, T], BF16, kind="Internal").ap()
    XS = [dt("X%d" % i, [T, D], F32, kind="Internal").ap() for i in range(2)]
    SND = [dt("SND%d" % k, [128, T], BF16).ap() for k in range(5)]
    RCV = [dt("RCV%d" % k, [256, T], BF16).ap() for k in range(5)]
    O_CK, O_KR, O_KA, O_VA = 0, 2 * T, 3 * T, 4 * T

    es = contextlib.ExitStack()
    kb = KB(nc, es)
    op, dma = kb.op, kb.dma

    PS = [es.enter_context(nc.psum_tensor("ps%d" % i, [128, 512], F32)) for i in range(8)]
    PD = [Dep() for _ in range(8)]
    gp = Ph(kb)
    ident = gp.sb("ident", [128, 128], BF16); ident_dp = gp.dep()
    cs = gp.sb("cs", [64, 2, T], F32); cs_dp = gp.dep()
    mask = gp.sb("mask", [128, 2, 128], BF16); mask_dp = gp.dep()
    cqnT = gp.sb("cqnT", [128, 3, T], BF16); cqnT_dp = [gp.dep() for _ in range(NS)]
    small = gp.sb("small", [128, 64], F32)
    small_dp = gp.dep()
    esink_dp = gp.dep()
    dma([(ident[:], ident_d)], W=[ident_dp], st=ident_dp)
    dma([(cs[:, 0, :], cs_d[0]), (cs[:, 1, :], cs_d[1])], W=[cs_dp], st=cs_dp)
    dma([(mask[:], mask_d)], W=[mask_dp], st=mask_dp)

    x_dp = [[Dep() for _ in range(NS)] for _ in range(3)]
    qa_dp = Dep(); gt_dp = [Dep() for _ in range(NS)]; yt_dp = Dep()
    snd_dp = [Dep() for _ in range(5)]; rcv_dp = [Dep() for _ in range(5)]
    gtb_dp = Dep()

    def bf(ps_ap):
        return ps_ap.bitcast(BF16)

    rr = {"ps": 0, "ev": 0}

    def evac_engine():
        rr["ev"] += 1
        return "act" if rr["ev"] % 2 else "dve"

    def copy(e, out, in_, R, W):
        if e == "act":
            return op("act", lambda g: g.copy(out=out, in_=in_), R=R, W=W)
        return op(e, lambda g: g.tensor_copy(out=out, in_=in_), R=R, W=W)

    for l in range(L):
        x_src = x_in if l == 0 else XS[(l - 1) % 2]
        x_src_dp = x_dp[0] if l == 0 else x_dp[1 + (l - 1) % 2]
        x_dst = XS[l % 2]
        x_dst_dp = x_dp[1 + l % 2]

        dma([(small[:, 0:16], gin_d[l]), (small[:, 16:19], gq_d[l]), (small[:, 19:21], gkv_d[l]),
             (small[:, 24:40], sink_d[l])], W=[small_dp], st=small_dp)
        op("act", lambda g: g.activation(out=small[:, 40:56], in_=small[:, 24:40], func=AF.Exp),
           R=[small_dp], W=[esink_dp])

        pa = Ph(kb)
        hT = pa.sb("hT", [128, NCH, T], BF16)
        hT_dp = [pa.dep() for _ in range(NS)]
        p1 = Ph(kb)
        xs = [p1.sb("xs%d" % i, [128, D], F32) for i in range(2)]; xs_dp = [p1.dep() for _ in range(2)]
        hb = [p1.sb("hb%d" % i, [128, D], BF16) for i in range(2)]; hb_dp = [p1.dep() for _ in range(2)]
        junk = p1.sb("junk", [128, D], BF16); junk_dp = p1.dep()
        st1 = p1.sb("st1", [128, 4 * NS], F32); st1_dp = [p1.dep() for _ in range(NS)]
        for s in range(NS):
            b = s % 2
            dma([(xs[b][:], x_src[s * 128:(s + 1) * 128, :])], R=[x_src_dp[s]], W=[xs_dp[b]], st=xs_dp[b])
            c0 = 4 * s
            op("act", lambda g: g.activation(out=junk[:], in_=xs[b][:], func=AF.Square,
                                             accum_out=st1[:, c0:c0 + 1]),
               R=[xs_dp[b]], W=[junk_dp, st1_dp[s]])
            op("dve", lambda g: g.tensor_scalar(out=st1[:, c0 + 1:c0 + 2], in0=st1[:, c0:c0 + 1],
                                                scalar1=1.0 / D, scalar2=EPS, op0=ALU.mult, op1=ALU.add),
               R=[], W=[st1_dp[s]])
            op("act", lambda g: g.activation(out=st1[:, c0 + 2:c0 + 3], in_=st1[:, c0 + 1:c0 + 2], func=AF.Sqrt),
               W=[st1_dp[s]])
            op("dve", lambda g: g.reciprocal(out=st1[:, c0 + 3:c0 + 4], in_=st1[:, c0 + 2:c0 + 3]),
               W=[st1_dp[s]])
            op("dve", lambda g: g.tensor_scalar(out=hb[b][:], in0=xs[b][:], scalar1=st1[:, c0 + 3:c0 + 4],
                                                scalar2=None, op0=ALU.mult),
               R=[xs_dp[b], st1_dp[s]], W=[hb_dp[b]])
            for cg in range(4):
                pi = rr["ps"] % 4; rr["ps"] += 1
                pv = bf(PS[pi][:])[:, 0:512].rearrange("p (a b) -> p a b", a=4)
                for a in range(4):
                    c = cg * 4 + a
                    op("pe", lambda g: g.transpose(out=pv[:, a, :], in_=hb[b][:, c * 128:(c + 1) * 128],
                                                   identity=ident[:]),
                       R=[hb_dp[b], ident_dp], W=[PD[pi]])
                copy(evac_engine(), hT[:, cg * 4:(cg + 1) * 4, s * 128:(s + 1) * 128], pv,
                     R=[PD[pi]], W=[hT_dp[s]])
        p1.close()

        p2 = Ph(kb)
        wst = [p2.sb("wst%d" % i, [128, NCH, 128], F32) for i in range(3)]; wst_dp = [p2.dep() for _ in range(3)]
        wb = [p2.sb("wb%d" % i, [128, NCH, 128], BF16) for i in range(2)]; wb_dp = [p2.dep() for _ in range(2)]
        wt = [p2.sb("wt%d" % i, [128, NCH, 512], BF16) for i in range(2)]; wt_dp = [p2.dep() for _ in range(2)]
        ofm = [p2.sb("ofm%d" % i, [128, 512], BF16) for i in range(3)]; ofm_dp = [p2.dep() for _ in range(3)]
        otm = [p2.sb("otm%d" % i, [128, 512], BF16) for i in range(3)]; otm_dp = [p2.dep() for _ in range(3)]
        sndt = [p2.sb("sndt%d" % k, [128, T], BF16) for k in range(5)]
        sn_dp = [p2.dep() for _ in range(5)]

        def send(k):
            dma([(SND[k], sndt[k][:])], R=[sn_dp[k]], W=[snd_dp[k]], st=sn_dp[k])
            kb.allgather(SND[k], RCV[k], R=[snd_dp[k]], W=[rcv_dp[k]])
        rp = [p2.sb("rp%d" % i, [64, 512], F32) for i in range(2)]; rp_dp = [p2.dep() for _ in range(2)]
        st2 = p2.sb("st2", [128, 8], F32); st2_dp = p2.dep()
        cn = p2.sb("cn", [128, 384], BF16); cn_dp = p2.dep()
        op("pool", lambda g: g.memset(sndt[2][64:128, :], 0.0), W=[sn_dp[2]])
        cnt = {"w": 0, "ofm": 0, "otm": 0}

        def load_piece(p):
            if p >= NPIECE:
                return
            i = p % 3
            dma([(wst[i][:], win[l, p])], W=[wst_dp[i]], st=wst_dp[i])

        def cast_piece(i, dst, dst_dp, gofs):
            for c in range(NCH):
                op("pool", lambda g: g.tensor_scalar(out=dst[:, c, :], in0=wst[i][:, c, :],
                                                     scalar1=small[:, gofs + c:gofs + c + 1], scalar2=0.0,
                                                     op0=ALU.mult, op1=ALU.add),
                   R=[wst_dp[i], small_dp], W=[dst_dp], nowait=(c > 0))

        load_piece(0)
        load_piece(1)
        for p in range(10):
            i = p % 3
            wi = p % 2
            cast_piece(i, wb[wi], wb_dp[wi], 0)
            load_piece(p + 2)
            for tg in range(NG):
                tsl = slice(tg * 512, (tg + 1) * 512)
                hdeps = hT_dp[tg * 4:(tg + 1) * 4]
                if p < 9:
                    pi = rr["ps"] % 4; rr["ps"] += 1
                    for c in range(NCH):
                        op("pe", lambda g: g.matmul(out=PS[pi][:], lhsT=wb[wi][:, c, :], rhs=hT[:, c, tsl],
                                                    start=(c == 0), stop=(c == NCH - 1)),
                           R=[wb_dp[wi]] + hdeps, W=[PD[pi]])
                    if p < 8:
                        oi = cnt["ofm"] % 3; cnt["ofm"] += 1
                        copy(evac_engine(), ofm[oi][:], PS[pi][:], R=[PD[pi]], W=[ofm_dp[oi]])
                        dma([(QA[p, :, tsl], ofm[oi][:])], R=[ofm_dp[oi]], W=[qa_dp], st=ofm_dp[oi])
                    else:
                        copy(evac_engine(), sndt[3][:, tsl], PS[pi][:], R=[PD[pi]], W=[sn_dp[3]])
                else:
                    pis = []
                    for half in range(2):
                        pi = rr["ps"] % 4; rr["ps"] += 1
                        pis.append(pi)
                        for c in range(NCH):
                            op("pe", lambda g: g.matmul(out=PS[pi][0:64, :], lhsT=wb[wi][:, c, half * 64:(half + 1) * 64],
                                                        rhs=hT[:, c, tsl], start=(c == 0), stop=(c == NCH - 1)),
                               R=[wb_dp[wi]] + hdeps, W=[PD[pi]])
                    op("dve", lambda g: g.tensor_tensor(out=rp[0][:], in0=PS[pis[0]][0:64, :], in1=cs[:, 0, tsl], op=ALU.mult),
                       R=[PD[pis[0]], cs_dp], W=[rp_dp[0]])
                    op("dve", lambda g: g.tensor_tensor(out=rp[1][:], in0=PS[pis[1]][0:64, :], in1=cs[:, 1, tsl], op=ALU.mult),
                       R=[PD[pis[1]], cs_dp], W=[rp_dp[1]])
                    op("dve", lambda g: g.tensor_tensor(out=sndt[2][0:64, tsl], in0=rp[0][:], in1=rp[1][:], op=ALU.add),
                       R=[rp_dp[0], rp_dp[1]], W=[sn_dp[2]])
            if p == 8:
                send(3)
            if p == 9:
                send(2)

        groups = [("cqva", [10, 11, 12, 13]), ("ckv", [14, 15]), ("ga0", [16, 17, 18, 19]), ("ga1", [20, 21, 22, 23])]
        for gi, (gname, pieces) in enumerate(groups):
            wi = gi % 2
            ncol = 128 * len(pieces)
            for k, p in enumerate(pieces):
                i = p % 3
                cast_piece(i, wt[wi][:, :, k * 128:(k + 1) * 128], wt_dp[wi], 0)
                load_piece(p + 2)
            for s in range(NS):
                pi = rr["ps"] % 4; rr["ps"] += 1
                for c in range(NCH):
                    op("pe", lambda g: g.matmul(out=PS[pi][:, 0:ncol], lhsT=hT[:, c, s * 128:(s + 1) * 128],
                                                rhs=wt[wi][:, c, 0:ncol], start=(c == 0), stop=(c == NCH - 1)),
                       R=[wt_dp[wi], hT_dp[s]], W=[PD[pi]])
                if gname in ("cqva", "ckv"):
                    nr = QL if gname == "cqva" else KVL
                    nck = nr // 128
                    op("act", lambda g: g.activation(out=cn[:, 0:nr], in_=PS[pi][:, 0:nr], func=AF.Square,
                                                     accum_out=st2[:, 0:1]),
                       R=[PD[pi]], W=[cn_dp, st2_dp])
                    op("dve", lambda g: g.tensor_scalar(out=st2[:, 1:2], in0=st2[:, 0:1], scalar1=1.0 / nr,
                                                        scalar2=EPS, op0=ALU.mult, op1=ALU.add), W=[st2_dp])
                    op("act", lambda g: g.activation(out=st2[:, 2:3], in_=st2[:, 1:2], func=AF.Sqrt), W=[st2_dp])
                    op("dve", lambda g: g.reciprocal(out=st2[:, 3:4], in_=st2[:, 2:3]), W=[st2_dp])
                    op("dve", lambda g: g.tensor_scalar(out=cn[:, 0:nr], in0=PS[pi][:, 0:nr], scalar1=st2[:, 3:4],
                                                        scalar2=None, op0=ALU.mult),
                       R=[PD[pi], st2_dp], W=[cn_dp])
                    if gname == "cqva":
                        copy("act", sndt[4][:, s * 128:(s + 1) * 128], PS[pi][:, 384:512],
                             R=[PD[pi]], W=[sn_dp[4]])
                    pj = 4 + (s % 2)
                    pv = bf(PS[pj][:])[:, 0:128 * nck].rearrange("p (a b) -> p a b", a=nck)
                    for a in range(nck):
                        op("pe", lambda g: g.transpose(out=pv[:, a, :], in_=cn[:, a * 128:(a + 1) * 128], identity=ident[:]),
                           R=[cn_dp, ident_dp], W=[PD[pj]])
                    if gname == "cqva":
                        copy("dve", cqnT[:, :, s * 128:(s + 1) * 128], pv, R=[PD[pj]], W=[cqnT_dp[s]])
                    else:
                        for a in range(2):
                            copy("dve", sndt[a][:, s * 128:(s + 1) * 128], pv[:, a, :], R=[PD[pj]], W=[sn_dp[a]])
                else:
                    oi = cnt["otm"] % 3; cnt["otm"] += 1
                    op("act", lambda g: g.activation(out=otm[oi][:], in_=PS[pi][:], func=AF.Silu),
                       R=[PD[pi]], W=[otm_dp[oi]])
                    col = (gi - 2) * 512
                    dma([(GT[s * 128:(s + 1) * 128, col:col + 512], otm[oi][:])], R=[otm_dp[oi]], W=[gt_dp[s]],
                        st=otm_dp[oi])
            if gname == "cqva":
                send(4)
            if gname == "ckv":
                send(0)
                send(1)
        for p in range(24, 32):
            i = p % 3
            wi = p % 2
            cast_piece(i, wb[wi], wb_dp[wi], 0)
            load_piece(p + 2)
            for tg in range(NG):
                tsl = slice(tg * 512, (tg + 1) * 512)
                hdeps = hT_dp[tg * 4:(tg + 1) * 4]
                pi = rr["ps"] % 4; rr["ps"] += 1
                for c in range(NCH):
                    op("pe", lambda g: g.matmul(out=PS[pi][:], lhsT=wb[wi][:, c, :], rhs=hT[:, c, tsl],
                                                start=(c == 0), stop=(c == NCH - 1)),
                       R=[wb_dp[wi]] + hdeps, W=[PD[pi]])
                oi = cnt["ofm"] % 3; cnt["ofm"] += 1
                op("act", lambda g: g.activation(out=ofm[oi][:], in_=PS[pi][:], func=AF.Silu),
                   R=[PD[pi]], W=[ofm_dp[oi]])
                dma([(GTB[p - 24, :, tsl], ofm[oi][:])], R=[ofm_dp[oi]], W=[gtb_dp], st=ofm_dp[oi])
        p2.close()
        pa.close()

        pY = Ph(kb)
        yta = pY.sb("yta", [128, NCH, T], BF16); yta_dp = [pY.dep() for _ in range(NCH)]
        pS = Ph(kb)
        qall = pS.sb("qall", [128, 8, T], BF16); qall_dp = pS.dep()
        kaT = pS.sb("kaT", [128, 2, T], BF16); kaT_dp = pS.dep()
        vraw = pS.sb("vraw", [128, 2, T], BF16); vraw_dp = pS.dep()
        vaug = pS.sb("vaug", [128, 2 * NS * 2, 66], BF16); vaug_dp = pS.dep()
        etab = pS.sb("etab", [128, 3, 16, 128], BF16); etab_dp = pS.dep()
        gat = [pS.sb("gat%d" % i, [128, 1024], BF16) for i in range(2)]; gat_dp = [pS.dep() for _ in range(2)]
        pex = [pS.sb("pex%d" % i, [128, 512], BF16) for i in range(3)]; pex_dp = [pS.dep() for _ in range(3)]
        ptS = [pS.sb("ptS%d" % i, [128, 512], BF16) for i in range(6)]; ptS_dp = [pS.dep() for _ in range(6)]
        ysw = [pS.sb("ysw%d" % i, [128, 1024], BF16) for i in range(2)]; ysw_dp = [pS.dep() for _ in range(2)]
        lS = pS.sb("lS", [128, 8], F32); lS_dp = pS.dep()
        dma([(etab[:], etab_d)], W=[etab_dp], st=etab_dp)
        dma([(qall[:, j, :], QA[j]) for j in range(8)], R=[qa_dp], W=[qall_dp], st=qall_dp)
        dma([(kaT[:, r, :], RCV[3][r * 128:(r + 1) * 128, :]) for r in range(2)], R=[rcv_dp[3]], W=[kaT_dp], st=kaT_dp)
        dma([(vraw[:, r, :], RCV[4][r * 128:(r + 1) * 128, :]) for r in range(2)], R=[rcv_dp[4]], W=[vraw_dp], st=vraw_dp)
        op("pool", lambda g: g.memset(vaug[:], 1.0), W=[vaug_dp])
        for r in range(2):
            dst = vaug[:, r * NS * 2:(r + 1) * NS * 2, 0:64]
            src = vraw[:, r, :].rearrange("p (a d) -> p a d", d=64)
            op("pool", lambda g: g.tensor_copy(out=dst, in_=src), R=[vraw_dp], W=[vaug_dp])
        cS = {"pex": 0, "pt": 0}
        units = [(s, gk, quad) for s in range(NS) for gk in range(2) for quad in range(2)]
        s1out = {}

        def stage1(u):
            s, gk, quad = units[u]
            b = s % 2
            if gk == 0 and quad == 0:
                dma([(gat[b][:], GT[s * 128:(s + 1) * 128, 0:1024])], R=[gt_dp[s]], W=[gat_dp[b]], st=gat_dp[b])
            cands = [(0, 1, s - 1), (1, 0, s), (2, 1, s)]
            if s == 0:
                cands = cands[1:]
            prt = slice(gk * 64, (gk + 1) * 64)
            h0 = gk * 8 + quad * 4
            pts = []
            for (ci, r, ks) in cands:
                pi = rr["ps"] % 2; rr["ps"] += 1
                op("pe", lambda g: g.matmul(out=PS[pi][:], lhsT=kaT[prt, r, ks * 128:(ks + 1) * 128],
                                            rhs=qall[prt, quad * 4:(quad + 1) * 4, s * 128:(s + 1) * 128],
                                            start=True, stop=True),
                   R=[kaT_dp, qall_dp], W=[PD[pi]])
                xi = cS["pex"] % 3; cS["pex"] += 1
                op("act", lambda g: g.activation(out=pex[xi][:], in_=PS[pi][:], func=AF.Exp, scale=0.125),
                   R=[PD[pi]], W=[pex_dp[xi]])
                ti = cS["pt"] % 6; cS["pt"] += 1
                op("dve" if ti % 2 == 0 else "pool", lambda g: g.tensor_tensor(out=ptS[ti][:].rearrange("p (a b) -> p a b", a=4),
                                                    in0=pex[xi][:].rearrange("p (a b) -> p a b", a=4),
                                                    in1=etab[:, ci, h0:h0 + 4, :], op=ALU.mult),
                   R=[pex_dp[xi], etab_dp], W=[ptS_dp[ti]])
                pts.append((ti, r, ks))
            s1out[u] = pts

        def stage2(u):
            s, gk, quad = units[u]
            b = s % 2
            pts = s1out.pop(u)
            oi = 2 + gk * 2 + quad
            h0 = gk * 8 + quad * 4
            ov = PS[oi][:, 0:4 * 65].rearrange("p (a b) -> p a b", a=4)
            for hq in range(4):
                for n, (ti, r, ks) in enumerate(pts):
                    op("pe", lambda g: g.matmul(out=ov[:, hq, :], lhsT=ptS[ti][:, hq * 128:(hq + 1) * 128],
                                                rhs=vaug[:, (r * NS + ks) * 2 + gk, 0:65],
                                                start=(n == 0), stop=(n == len(pts) - 1)),
                       R=[ptS_dp[ti], vaug_dp], W=[PD[oi]])
            op("dve", lambda g: g.tensor_tensor(out=lS[:, 0:4], in0=ov[:, :, 64], in1=small[:, 40 + h0:44 + h0], op=ALU.add),
               R=[PD[oi], esink_dp], W=[lS_dp])
            op("dve", lambda g: g.reciprocal(out=lS[:, 4:8], in_=lS[:, 0:4]), W=[lS_dp])
            for hq in range(4):
                h = h0 + hq
                op("dve", lambda g: g.scalar_tensor_tensor(out=ysw[b][:, h * 64:(h + 1) * 64], in0=ov[:, hq, 0:64],
                                                            scalar=lS[:, 4 + hq:5 + hq], in1=gat[b][:, h * 64:(h + 1) * 64],
                                                            op0=ALU.mult, op1=ALU.mult),
                   R=[PD[oi], lS_dp, gat_dp[b]], W=[ysw_dp[b]], nowait=(hq > 0))
            if gk == 1 and quad == 1:
                pj = 6 + (s % 2)
                pv = bf(PS[pj][:]).rearrange("p (a b) -> p a b", a=8)
                for a in range(8):
                    op("pe", lambda g: g.transpose(out=pv[:, a, :], in_=ysw[b][:, a * 128:(a + 1) * 128], identity=ident[:]),
                       R=[ysw_dp[b], ident_dp], W=[PD[pj]])
                copy("act", yta[:, 0:8, s * 128:(s + 1) * 128], pv, R=[PD[pj]], W=yta_dp[0:8])

        stage1(0)
        for u in range(len(units)):
            if u + 1 < len(units):
                stage1(u + 1)
            stage2(u)
        pS.close()

        pC = Ph(kb)
        ck = pC.sb("ck", [128, 2, 2, T], BF16); ck_dp = pC.dep()
        kr = pC.sb("kr", [64, 2, T], BF16); kr_dp = pC.dep()
        wq = pC.sb("wq", [128, 3, 2048], BF16); wq_dp = pC.dep()
        wkv = pC.sb("wkv", [128, 2, 2048], BF16); wkv_dp = pC.dep()
        wst2 = [pC.sb("wsc%d" % i, [128, 1024], F32) for i in range(2)]; wst2_dp = [pC.dep() for _ in range(2)]
        khT = [pC.sb("khT%d" % i, [128, 2 * T], BF16) for i in range(2)]; khT_dp = [pC.dep() for _ in range(2)]
        vh = [pC.sb("vh%d" % i, [128, 2 * NS, 130], BF16) for i in range(2)]; vh_dp = [pC.dep() for _ in range(2)]
        gbt = [pC.sb("gbt%d" % i, [128, 512], BF16) for i in range(2)]; gbt_dp = [pC.dep() for _ in range(2)]
        onesb = pC.sb("onesb", [128, 128], BF16); onesb_dp = pC.dep()
        rLt = [pC.sb("rLt%d" % i, [128, 512], F32) for i in range(2)]; rLt_dp = [pC.dep() for _ in range(2)]
        op("pool", lambda g: g.memset(onesb[:], 1.0), W=[onesb_dp])
        qn = [pC.sb("qn%d" % i, [128, 512], BF16) for i in range(2)]; qn_dp = [pC.dep() for _ in range(2)]
        qr = [pC.sb("qr%d" % i, [64, 512], BF16) for i in range(2)]; qr_dp = [pC.dep() for _ in range(2)]
        rq = [pC.sb("rq%d" % i, [64, 512], F32) for i in range(2)]; rq_dp = [pC.dep() for _ in range(2)]
        ptC = [pC.sb("ptC%d" % i, [128, 512], BF16) for i in range(4)]; ptC_dp = [pC.dep() for _ in range(4)]
        for r in range(2):
            dma([(ck[:, c, r, :], RCV[c][r * 128:(r + 1) * 128, :]) for c in range(2)]
                + [(kr[:, r, :], RCV[2][r * 128:r * 128 + 64, :])],
                R=rcv_dp[0:3], W=[ck_dp, kr_dp], st=ck_dp if r == 0 else kr_dp)
        nst = 0
        for (wsrc, wdst, wdp, nchk, gofs) in ((wkv_d, wkv, wkv_dp, 2, 19), (wq_d, wq, wq_dp, 3, 16)):
            for c in range(nchk):
                for hf in range(2):
                    i = nst % 2; nst += 1
                    csl = slice(hf * 1024, (hf + 1) * 1024)
                    dma([(wst2[i][:], wsrc[l, :, c, csl])], W=[wst2_dp[i]], st=wst2_dp[i])
                    op("pool", lambda g: g.tensor_scalar(out=wdst[:, c, csl], in0=wst2[i][:], scalar1=small[:, gofs + c:gofs + c + 1],
                                                         scalar2=0.0, op0=ALU.mult, op1=ALU.add),
                       R=[wst2_dp[i], small_dp], W=[wdp])
        for i in range(2):
            op("pool", lambda g: g.memset(vh[i][:, :, 128:130], 1.0), W=[vh_dp[i]])
        cC = {"pt": 0, "g": 0, "s": 0}
        scale = float((128 + 64) ** -0.5)
        NKS = 2 * NS

        def misc_bank():
            pi = 5 + rr["ps"] % 3; rr["ps"] += 1
            return pi

        def prep_chunks(h):
            hb_ = h % 2
            out = []
            for kg in range(2 * T // 512):
                def f(kg=kg):
                    r, t0 = divmod(kg * 512, T)
                    pi = misc_bank()
                    for c in range(2):
                        op("pe", lambda g: g.matmul(out=PS[pi][:], lhsT=wkv[:, c, h * 256:h * 256 + 128],
                                                    rhs=ck[:, c, r, t0:t0 + 512], start=(c == 0), stop=(c == 1)),
                           R=[wkv_dp, ck_dp], W=[PD[pi]])
                    copy("dve", khT[hb_][:, kg * 512:(kg + 1) * 512], PS[pi][:], R=[PD[pi]], W=[khT_dp[hb_]])
                out.append(f)
            for k4 in range(NKS // 4):
                def f(k4=k4):
                    pi = misc_bank()
                    for a in range(4):
                        ksl = k4 * 4 + a
                        r, s_ = divmod(ksl, NS)
                        for c in range(2):
                            op("pe", lambda g: g.matmul(out=PS[pi][:, a * 128:(a + 1) * 128],
                                                        lhsT=ck[:, c, r, s_ * 128:(s_ + 1) * 128],
                                                        rhs=wkv[:, c, h * 256 + 128:h * 256 + 256],
                                                        start=(c == 0), stop=(c == 1)),
                               R=[wkv_dp, ck_dp], W=[PD[pi]])
                    copy("dve", vh[hb_][:, k4 * 4:(k4 + 1) * 4, 0:128],
                         PS[pi][:].rearrange("p (a b) -> p a b", a=4), R=[PD[pi]], W=[vh_dp[hb_]])
                out.append(f)
            return out

        def qproj_chunks(h, G):
            gb_ = (h * NG + G) % 2
            tsl = slice(G * 512, (G + 1) * 512)
            cdeps = cqnT_dp[G * 4:(G + 1) * 4]

            def f0():
                dma([(gbt[gb_][:], GTB[h, :, tsl])], R=[gtb_dp], W=[gbt_dp[gb_]], st=gbt_dp[gb_])
                pi = misc_bank()
                for c in range(3):
                    op("pe", lambda g: g.matmul(out=PS[pi][:], lhsT=wq[:, c, h * 256:h * 256 + 128], rhs=cqnT[:, c, tsl],
                                                start=(c == 0), stop=(c == 2)),
                       R=[wq_dp] + cdeps, W=[PD[pi]])
                copy("dve", qn[gb_][:], PS[pi][:], R=[PD[pi]], W=[qn_dp[gb_]])

            def fr(half):
                pi = misc_bank()
                c0 = h * 256 + 128 + half * 64
                for c in range(3):
                    op("pe", lambda g: g.matmul(out=PS[pi][0:64, :], lhsT=wq[:, c, c0:c0 + 64], rhs=cqnT[:, c, tsl],
                                                start=(c == 0), stop=(c == 2)),
                       R=[wq_dp] + cdeps, W=[PD[pi]])
                op("dve", lambda g: g.tensor_tensor(out=rq[half][:], in0=PS[pi][0:64, :], in1=cs[:, half, tsl], op=ALU.mult),
                   R=[PD[pi], cs_dp], W=[rq_dp[half]])
                if half == 1:
                    op("dve", lambda g: g.tensor_tensor(out=qr[gb_][:], in0=rq[0][:], in1=rq[1][:], op=ALU.add),
                       R=[rq_dp[0], rq_dp[1]], W=[qr_dp[gb_]])
            return [f0, lambda: fr(0), lambda: fr(1)]

        qq = []
        pq = []
        for f in prep_chunks(0):
            f()
        for f in qproj_chunks(0, 0):
            f()
        for h in range(8):
            hb_ = h % 2
            if h + 1 < 8:
                pq = prep_chunks(h + 1)
            for G in range(NG):
                gb_ = (h * NG + G) % 2
                tsl = slice(G * 512, (G + 1) * 512)
                if G + 1 < NG:
                    qq = qproj_chunks(h, G + 1)
                elif h + 1 < 8:
                    qq = qproj_chunks(h + 1, 0)
                ents = [(r, s_, 0, False) for s_ in range(4 * G) for r in range(2)]
                ents += [(r, 4 * G + j, j, True) for j in range(4) for r in range(2)]
                NE = len(ents)
                st_ = {}

                def emit_S(n):
                    r, s_, jmin, msk = ents[n]
                    ncol = (4 - jmin) * 128
                    pi = cC["s"] % 3; cC["s"] += 1
                    kcol = r * T + s_ * 128
                    op("pe", lambda g: g.matmul(out=PS[pi][:, 0:ncol], lhsT=khT[hb_][:, kcol:kcol + 128],
                                                rhs=qn[gb_][:, jmin * 128:512], start=True, stop=False),
                       R=[khT_dp[hb_], qn_dp[gb_]], W=[PD[pi]])
                    op("pe", lambda g: g.matmul(out=PS[pi][:, 0:ncol], lhsT=kr[:, r, s_ * 128:(s_ + 1) * 128],
                                                rhs=qr[gb_][:, jmin * 128:512], start=False, stop=True),
                       R=[kr_dp, qr_dp[gb_]], W=[PD[pi]])
                    ti = cC["pt"] % 4; cC["pt"] += 1
                    op("act", lambda g: g.activation(out=ptC[ti][:, 0:ncol], in_=PS[pi][:, 0:ncol], func=AF.Exp, scale=scale),
                       R=[PD[pi]], W=[ptC_dp[ti]])
                    if msk:
                        op("dve", lambda g: g.tensor_tensor(out=ptC[ti][:, 0:128], in0=ptC[ti][:, 0:128], in1=mask[:, r, :], op=ALU.mult),
                           R=[mask_dp], W=[ptC_dp[ti]])
                    st_[n] = ti

                bO = 3

                def emit_PV(n):
                    r, s_, jmin, msk = ents[n]
                    ti = st_[n]
                    ncol = (4 - jmin) * 128
                    op("pe", lambda g: g.matmul(out=PS[bO][:, jmin * 128:512], lhsT=vh[hb_][:, r * NS + s_, 0:128],
                                                rhs=ptC[ti][:, 0:ncol], start=(n == 0), stop=(n == NE - 1)),
                       R=[ptC_dp[ti], vh_dp[hb_]], W=[PD[bO]])
                    op("pe", lambda g: g.matmul(out=PS[bO + 1][:, jmin * 128:512], lhsT=onesb[:],
                                                rhs=ptC[ti][:, 0:ncol], start=(n == 0), stop=(n == NE - 1)),
                       R=[ptC_dp[ti], onesb_dp], W=[PD[bO + 1]])

                emit_S(0)
                emit_S(1)
                for n in range(NE):
                    if n + 2 < NE:
                        emit_S(n + 2)
                    emit_PV(n)
                    if qq:
                        qq.pop(0)()
                    elif pq:
                        pq.pop(0)()
                while qq:
                    qq.pop(0)()
                op("dve", lambda g: g.reciprocal(out=rLt[gb_][:], in_=PS[bO + 1][:]), R=[PD[bO + 1]], W=[rLt_dp[gb_]])
                op("dve", lambda g: g.tensor_tensor(out=rLt[gb_][:], in0=PS[bO][:], in1=rLt[gb_][:], op=ALU.mult),
                   R=[PD[bO]], W=[rLt_dp[gb_]])
                op("dve", lambda g: g.tensor_tensor(out=yta[:, 8 + h, tsl], in0=rLt[gb_][:], in1=gbt[gb_][:], op=ALU.mult),
                   R=[rLt_dp[gb_], gbt_dp[gb_]], W=[yta_dp[8 + h]])
            while pq:
                pq.pop(0)()
        pC.close()

        pD = Ph(kb)
        wsd = [pD.sb("wsd%d" % i, [128, NCH, 128], F32) for i in range(2)]; wsd_dp = [pD.dep() for _ in range(2)]
        wo = [pD.sb("wo%d" % i, [128, NCH, 512], BF16) for i in range(2)]; wo_dp = [pD.dep() for _ in range(2)]
        xr = [pD.sb("xr%d" % i, [128, 512], F32) for i in range(3)]; xr_dp = [pD.dep() for _ in range(3)]
        xo = [pD.sb("xo%d" % i, [128, 512], F32) for i in range(3)]; xo_dp = [pD.dep() for _ in range(3)]
        cD = {"w": 0, "x": 0}
        def load_wo(cg):
            wi = cg % 2
            for k in range(4):
                i = cD["w"] % 2; cD["w"] += 1
                dma([(wsd[i][:], wo_d[l, cg * 4 + k])], W=[wsd_dp[i]], st=wsd_dp[i])
                op("pool", lambda g: g.tensor_copy(out=wo[wi][:, :, k * 128:(k + 1) * 128], in_=wsd[i][:]),
                   R=[wsd_dp[i]], W=[wo_dp[wi]])

        load_wo(0)
        for cg in range(4):
            wi = cg % 2
            if cg + 1 < 4:
                load_wo(cg + 1)
            for s in range(NS):
                xi = cD["x"] % 3; cD["x"] += 1
                dma([(xr[xi][:], x_src[s * 128:(s + 1) * 128, cg * 512:(cg + 1) * 512])], R=[x_src_dp[s]], W=[xr_dp[xi]], st=xr_dp[xi])
                pi = rr["ps"] % 4; rr["ps"] += 1
                for c in range(NCH):
                    op("pe", lambda g: g.matmul(out=PS[pi][:], lhsT=yta[:, c, s * 128:(s + 1) * 128], rhs=wo[wi][:, c, :],
                                                start=(c == 0), stop=(c == NCH - 1)),
                       R=yta_dp + [wo_dp[wi]], W=[PD[pi]])
                op("dve", lambda g: g.tensor_tensor(out=xo[xi][:], in0=PS[pi][:], in1=xr[xi][:], op=ALU.add),
                   R=[PD[pi], xr_dp[xi]], W=[xo_dp[xi]])
                dma([(x_dst[s * 128:(s + 1) * 128, cg * 512:(cg + 1) * 512], xo[xi][:])], R=[xo_dp[xi]], W=[x_dst_dp[s]], st=xo_dp[xi])
        pD.close()
        pY.close()

    pF = Ph(kb)
    x_src = XS[(L - 1) % 2]
    x_src_dp = x_dp[1 + (L - 1) % 2]
    gfin = pF.sb("gfin", [128, D], F32); gfin_dp = pF.dep()
    xs = [pF.sb("fx%d" % i, [128, D], F32) for i in range(2)]; xs_dp = [pF.dep() for _ in range(2)]
    fo = [pF.sb("fo%d" % i, [128, D], F32) for i in range(2)]; fo_dp = [pF.dep() for _ in range(2)]
    junk = pF.sb("fjunk", [128, D], BF16); junk_dp = pF.dep()
    st1 = pF.sb("fst", [128, 4 * NS], F32); st1_dp = [pF.dep() for _ in range(NS)]
    out_dp = Dep()
    dma([(gfin[:], gfin_d)], W=[gfin_dp], st=gfin_dp)
    for s in range(NS):
        b = s % 2
        dma([(xs[b][:], x_src[s * 128:(s + 1) * 128, :])], R=[x_src_dp[s]], W=[xs_dp[b]], st=xs_dp[b])
        c0 = 4 * s
        op("act", lambda g: g.activation(out=junk[:], in_=xs[b][:], func=AF.Square, accum_out=st1[:, c0:c0 + 1]),
           R=[xs_dp[b]], W=[junk_dp, st1_dp[s]])
        op("dve", lambda g: g.tensor_scalar(out=st1[:, c0 + 1:c0 + 2], in0=st1[:, c0:c0 + 1], scalar1=1.0 / D, scalar2=EPS,
                                            op0=ALU.mult, op1=ALU.add), W=[st1_dp[s]])
        op("act", lambda g: g.activation(out=st1[:, c0 + 2:c0 + 3], in_=st1[:, c0 + 1:c0 + 2], func=AF.Sqrt), W=[st1_dp[s]])
        op("dve", lambda g: g.reciprocal(out=st1[:, c0 + 3:c0 + 4], in_=st1[:, c0 + 2:c0 + 3]), W=[st1_dp[s]])
        op("dve", lambda g: g.scalar_tensor_tensor(out=fo[b][:], in0=xs[b][:], scalar=st1[:, c0 + 3:c0 + 4], in1=gfin[:],
                                                    op0=ALU.mult, op1=ALU.mult),
           R=[xs_dp[b], st1_dp[s], gfin_dp], W=[fo_dp[b]])
        dma([(out_d[s * 128:(s + 1) * 128, :], fo[b][:])], R=[fo_dp[b]], W=[out_dp], st=fo_dp[b])
    pF.close()
    gp.es.close()
    es.close()
    return nc


def _host_inputs(x, attn_norm_g, w_in, swa_sinks, q_a_norm_g, kv_a_norm_g, w_q_b, w_kv_b, w_out, final_norm_g, NS):
    L = w_in.shape[0]
    B = x.shape[0]
    T = NS * 128
    f32 = np.float32
    A_Q, A_KV, A_G = 1024, 128, 1024
    o_qa, o_ka, o_va, o_ga = 0, A_Q, A_Q + A_KV, A_Q + 2 * A_KV
    o_cq = o_ga + A_G
    o_ckv = o_cq + QL
    o_kr = o_ckv + KVL
    o_gb = o_kr + 64
    qa_cols = []
    for j in range(8):
        for half in range(2):
            hd = j + 8 * half
            qa_cols += list(range(o_qa + hd * 64, o_qa + (hd + 1) * 64))
    kr_cols = list(range(o_kr, o_kr + 64))
    kr_sw = list(range(o_kr + 32, o_kr + 64)) + list(range(o_kr, o_kr + 32))
    cols = (qa_cols + list(range(o_ka, o_ka + 128)) + kr_cols + kr_sw + list(range(o_cq, o_cq + QL))
            + list(range(o_va, o_va + 128)) + list(range(o_ckv, o_ckv + KVL)) + list(range(o_ga, o_ga + A_G))
            + list(range(o_gb, o_gb + 1024)))
    cols = np.asarray(cols)
    assert cols.size == NPIECE * 128
    wp = w_in[:, :, cols]
    win = np.ascontiguousarray(wp.reshape(L, NCH, 128, NPIECE, 128).transpose(0, 3, 2, 1, 4))
    qcols = []
    for h in range(8):
        b0 = h * 192
        qcols += list(range(b0, b0 + 192)) + list(range(b0 + 160, b0 + 192)) + list(range(b0 + 128, b0 + 160))
    wqp = w_q_b[:, :, np.asarray(qcols)]
    wq = np.ascontiguousarray(wqp.reshape(L, 3, 128, 2048).transpose(0, 2, 1, 3))
    wkv = np.ascontiguousarray(w_kv_b.reshape(L, 2, 128, 2048).transpose(0, 2, 1, 3))
    wo = np.ascontiguousarray(w_out.reshape(L, NCH, 128, 16, 128).transpose(0, 3, 2, 1, 4))
    gin = np.ascontiguousarray(attn_norm_g.reshape(L, NCH, 128).transpose(0, 2, 1))
    gq = np.ascontiguousarray(q_a_norm_g.reshape(L, 3, 128).transpose(0, 2, 1))
    gkv = np.ascontiguousarray(kv_a_norm_g.reshape(L, 2, 128).transpose(0, 2, 1))
    sinks = np.ascontiguousarray(np.broadcast_to(swa_sinks[:, None, :], (L, 128, 16)))
    gfin = np.ascontiguousarray(np.broadcast_to(final_norm_g[None, :], (128, D)))
    ident = np.eye(128, dtype=f32).astype(ml_dtypes.bfloat16)
    S_full = 2 * T
    pos = np.arange(S_full, dtype=f32)
    inv_freq = (10000.0 ** (-np.arange(0, 64, 2, dtype=f32) / 64)).astype(f32)
    ang = pos[:, None] * inv_freq[None, :]
    cos, sin = np.cos(ang).astype(f32), np.sin(ang).astype(f32)
    cos2 = np.concatenate([cos, cos], 1).T
    sin2 = np.concatenate([-sin, sin], 1).T
    slopes = np.exp2(-8.0 * np.arange(1, 17, dtype=f32) / 16).astype(f32)
    kk = np.arange(128)[:, None]
    qq = np.arange(128)[None, :]
    d_prev = (128 + qq - kk).astype(f32)
    d_cur = (qq - kk).astype(f32)
    E_prev = np.where((kk > qq)[:, None, :], np.exp(-slopes[None, :, None] * np.clip(d_prev, 0, 128)[:, None, :]), 0.0).astype(f32)
    E_cur = np.where((kk <= qq)[:, None, :], np.exp(-slopes[None, :, None] * np.clip(d_cur, 0, 128)[:, None, :]), 0.0).astype(f32)
    Z = np.zeros_like(E_prev)
    tri = (kk <= qq).astype(f32)
    ones = np.ones_like(tri)
    zer = np.zeros_like(tri)
    in_maps = []
    for c in range(2 * B):
        b, r = divmod(c, 2)
        xb = x[b].reshape(2 * NS, 128, D)[r::2].reshape(T, D)
        tok = (np.arange(NS)[:, None] * 2 + r) * 128 + np.arange(128)[None, :]
        tok = tok.reshape(-1)
        cs = np.ascontiguousarray(np.stack([cos2[:, tok], sin2[:, tok]], 0))
        if r == 0:
            et = np.stack([E_prev, E_cur, Z], 1)
            mk = np.stack([tri, zer], 1)
        else:
            et = np.stack([Z, E_prev, E_cur], 1)
            mk = np.stack([ones, tri], 1)
        in_maps.append({
            "x": np.ascontiguousarray(xb), "win": win, "wq": wq, "wkv": wkv, "wo": wo, "gin": gin, "gq": gq,
            "gkv": gkv, "sinks": sinks, "gfin": gfin, "cs": cs.astype(f32),
            "etab": np.ascontiguousarray(et).astype(ml_dtypes.bfloat16),
            "mask": np.ascontiguousarray(mk).astype(ml_dtypes.bfloat16), "ident": ident,
        })
    return in_maps


def _assemble(results, B, NS):
    T = NS * 128
    out = np.empty((B, 2 * NS, 128, D), np.float32)
    for c in range(2 * B):
        b, r = divmod(c, 2)
        out[b, r::2] = np.asarray(results[c]["out"]).reshape(NS, 128, D)
    return out.reshape(B, 2 * T, D)


_NC_CACHE = {}


def kernel(x, attn_norm_g, w_in, swa_sinks, q_a_norm_g, kv_a_norm_g, w_q_b, w_kv_b, w_out, final_norm_g):
    args = [np.asarray(a, dtype=np.float32) for a in (x, attn_norm_g, w_in, swa_sinks, q_a_norm_g, kv_a_norm_g,
                                                      w_q_b, w_kv_b, w_out, final_norm_g)]
    B, S = args[0].shape[0], args[0].shape[1]
    NS = S // 256
    L = args[2].shape[0]
    in_maps = _host_inputs(*args, NS=NS)
    key = (NS, L)
    if key not in _NC_CACHE:
        _NC_CACHE[key] = build(NS, L)
    res = run_bass_kernel_spmd(_NC_CACHE[key], in_maps, core_ids=list(range(2 * B)))
    return _assemble(res.results, B, NS)
```

```python
import contextlib
import numpy as np
import ml_dtypes
import concourse.bass as bass
import concourse.mybir as mybir
from concourse.bass_utils import run_bass_kernel_spmd

F32 = mybir.dt.float32
BF16 = mybir.dt.bfloat16
AF = mybir.ActivationFunctionType
ALU = mybir.AluOpType

D = 2048
NCH = 16
EPS = 1e-6
QL = 384
KVL = 256
NPIECE = 32
PAIRS = [[0, 1], [2, 3], [4, 5], [6, 7]]
EMBED_WAIT = True


class Dep:
    __slots__ = ("w", "r", "ds")

    def __init__(self):
        self.w = None
        self.r = {}
        self.ds = None


class KB:
    def __init__(self, nc, es, n_dsem=70):
        self.nc = nc
        self.eng = {"pe": nc.tensor, "act": nc.scalar, "dve": nc.vector,
                    "pool": nc.gpsimd, "sp": nc.sync}
        self.sem = {e: es.enter_context(nc.semaphore("s_" + e)) for e in ("pe", "act", "dve", "pool")}
        self.cnt = {e: 0 for e in self.sem}
        self.known = {e: {} for e in self.eng}
        self.dsems = [[es.enter_context(nc.semaphore("d%d" % i)), 0] for i in range(n_dsem)]
        self.dfree = list(range(n_dsem))
        self.ccsem = es.enter_context(nc.semaphore("ccs"))
        self.cccnt = 0

    def _wait(self, e, ev):
        if ev is None:
            return
        s, v = ev
        k = self.known[e]
        if k.get(id(s), 0) >= v:
            return
        if e == "pe" and s is self.sem["pe"]:
            return
        k[id(s)] = v
        p = self.pend
        if id(s) in p:
            p[id(s)] = (s, max(v, p[id(s)][1]))
        else:
            p[id(s)] = (s, v)

    def _deps(self, e, R, W):
        self.pend = {}
        for d in R:
            self._wait(e, d.w)
        for d in W:
            self._wait(e, d.w)
            for ev in d.r.values():
                self._wait(e, ev)
        return list(self.pend.values())

    def _emit_waits(self, e, waits, keep_last):
        n = len(waits) - (1 if keep_last and waits else 0)
        for (s, v) in waits[:n]:
            self.eng[e].wait_ge(s, v)
        return waits[n:]

    def op(self, e, fn, R=(), W=(), nowait=False):
        rest = []
        if not nowait:
            rest = self._emit_waits(e, self._deps(e, R, W), EMBED_WAIT)
        ins = fn(self.eng[e])
        for (s_, v_) in rest:
            ins._wait_ge(s_, v_)
        self.cnt[e] += 1
        ins.then_inc(self.sem[e], 1)
        ev = (self.sem[e], self.cnt[e])
        for d in R:
            d.r[e] = ev
        for d in W:
            d.w = ev
            d.r = {}
        return ev

    def dsem_alloc(self, dep):
        dep.ds = self.dfree.pop()
        return dep

    def dsem_free(self, dep):
        if dep.ds is not None:
            self.dfree.append(dep.ds)
            dep.ds = None

    def dma(self, pairs, R=(), W=(), st=None, q="sp"):
        if st.ds is None:
            self.dsem_alloc(st)
        self._emit_waits(q, self._deps(q, R, W), False)
        slot = self.dsems[st.ds]
        for (o, i) in pairs:
            ins = self.eng[q].dma_start(out=o, in_=i)
            slot[1] += 16
            ins.then_inc(slot[0], 16)
        ev = (slot[0], slot[1])
        for d in R:
            d.r[("d", st.ds)] = ev
        for d in W:
            d.w = ev
            d.r = {}
        return ev

    def allgather(self, src, dst, R=(), W=()):
        self._emit_waits("pool", self._deps("pool", R, W), False)
        ins = self.nc.gpsimd.collective_compute("AllGather", ALU.bypass, replica_groups=PAIRS,
                                                ins=[src], outs=[dst])
        self.cccnt += 1
        ins.then_inc(self.ccsem, 1)
        ev = (self.ccsem, self.cccnt)
        for d in R:
            d.r["cc"] = ev
        for d in W:
            d.w = ev
            d.r = {}

    def barrier(self):
        evs = [(self.sem[e], self.cnt[e]) for e in self.sem if self.cnt[e] > 0]
        evs += [(s, c) for (s, c) in self.dsems if c > 0]
        if self.cccnt:
            evs.append((self.ccsem, self.cccnt))
        for e in self.eng:
            self.pend = {}
            for ev in evs:
                self._wait(e, ev)
            self._emit_waits(e, list(self.pend.values()), False)


class Ph:
    uid = 0

    def __init__(self, kb):
        self.kb = kb
        self.es = contextlib.ExitStack()
        self.deps = []

    def sb(self, name, shape, dt):
        Ph.uid += 1
        t = self.es.enter_context(self.kb.nc.sbuf_tensor("sb%d_%s" % (Ph.uid, name), list(shape), dt))
        return t

    def dep(self):
        d = Dep()
        self.deps.append(d)
        return d

    def close(self):
        self.kb.barrier()
        for d in self.deps:
            self.kb.dsem_free(d)
        self.es.close()


def build(NS=16, L=4):
    T = NS * 128
    NG = NS // 4
    nc = bass.Bass("TRN2", target_bir_lowering=False)
    dt = nc.dram_tensor
    x_in = dt("x", [T, D], F32, kind="ExternalInput").ap()
    win = dt("win", [L, NPIECE, 128, NCH, 128], F32, kind="ExternalInput").ap()
    wq_d = dt("wq", [L, 128, 3, 2048], F32, kind="ExternalInput").ap()
    wkv_d = dt("wkv", [L, 128, 2, 2048], F32, kind="ExternalInput").ap()
    wo_d = dt("wo", [L, 16, 128, NCH, 128], F32, kind="ExternalInput").ap()
    gin_d = dt("gin", [L, 128, NCH], F32, kind="ExternalInput").ap()
    gq_d = dt("gq", [L, 128, 3], F32, kind="ExternalInput").ap()
    gkv_d = dt("gkv", [L, 128, 2], F32, kind="ExternalInput").ap()
    sink_d = dt("sinks", [L, 128, 16], F32, kind="ExternalInput").ap()
    gfin_d = dt("gfin", [128, D], F32, kind="ExternalInput").ap()
    cs_d = dt("cs", [2, 64, T], F32, kind="ExternalInput").ap()
    etab_d = dt("etab", [128, 3, 16, 128], BF16, kind="ExternalInput").ap()
    mask_d = dt("mask", [128, 2, 128], BF16, kind="ExternalInput").ap()
    ident_d = dt("ident", [128, 128], BF16, kind="ExternalInput").ap()
    out_d = dt("out", [T, D], F32, kind="ExternalOutput").ap()
    QA = dt("QA", [8, 128, T], BF16, kind="Internal").ap()
    GT = dt("GT", [T, 2048], BF16, kind="Internal").ap()
    GTB = dt("GTB", [8, 128, T], BF16, kind="Internal").ap()
    XS = [dt("X%d" % i, [T, D], F32, kind="Internal").ap() for i in range(2)]
    SND = [dt("SND%d" % k, [128, T], BF16).ap() for k in range(5)]
    RCV = [dt("RCV%d" % k, [256, T], BF16).ap() for k in range(5)]
    O_CK, O_KR, O_KA, O_VA = 0, 2 * T, 3 * T, 4 * T

    es = contextlib.ExitStack()
    kb = KB(nc, es)
    op, dma = kb.op, kb.dma

    PS = [es.enter_context(nc.psum_tensor("ps%d" % i, [128, 512], F32)) for i in range(8)]
    PD = [Dep() for _ in range(8)]
    gp = Ph(kb)
    ident = gp.sb("ident", [128, 128], BF16); ident_dp = gp.dep()
    cs = gp.sb("cs", [64, 2, T], F32); cs_dp = gp.dep()
    mask = gp.sb("mask", [128, 2, 128], BF16); mask_dp = gp.dep()
    cqnT = gp.sb("cqnT", [128, 3, T], BF16); cqnT_dp = [gp.dep() for _ in range(NS)]
    small = gp.sb("small", [128, 64], F32)
    small_dp = gp.dep()
    esink_dp = gp.dep()
    dma([(ident[:], ident_d)], W=[ident_dp], st=ident_dp)
    dma([(cs[:, 0, :], cs_d[0]), (cs[:, 1, :], cs_d[1])], W=[cs_dp], st=cs_dp)
    dma([(mask[:], mask_d)], W=[mask_dp], st=mask_dp)

    x_dp = [[Dep() for _ in range(NS)] for _ in range(3)]
    qa_dp = Dep(); gt_dp = [Dep() for _ in range(NS)]; yt_dp = Dep()
    snd_dp = [Dep() for _ in range(5)]; rcv_dp = [Dep() for _ in range(5)]
    gtb_dp = Dep()

    def bf(ps_ap):
        return ps_ap.bitcast(BF16)

    rr = {"ps": 0, "ev": 0}

    def evac_engine():
        rr["ev"] += 1
        return "act" if rr["ev"] % 2 else "dve"

    def copy(e, out, in_, R, W):
        if e == "act":
            return op("act", lambda g: g.copy(out=out, in_=in_), R=R, W=W)
        return op(e, lambda g: g.tensor_copy(out=out, in_=in_), R=R, W=W)

    for l in range(L):
        x_src = x_in if l == 0 else XS[(l - 1) % 2]
        x_src_dp = x_dp[0] if l == 0 else x_dp[1 + (l - 1) % 2]
        x_dst = XS[l % 2]
        x_dst_dp = x_dp[1 + l % 2]

        dma([(small[:, 0:16], gin_d[l]), (small[:, 16:19], gq_d[l]), (small[:, 19:21], gkv_d[l]),
             (small[:, 24:40], sink_d[l])], W=[small_dp], st=small_dp)
        op("act", lambda g: g.activation(out=small[:, 40:56], in_=small[:, 24:40], func=AF.Exp),
           R=[small_dp], W=[esink_dp])

        pa = Ph(kb)
        hT = pa.sb("hT", [128, NCH, T], BF16)
        hT_dp = [pa.dep() for _ in range(NS)]
        p1 = Ph(kb)
        xs = [p1.sb("xs%d" % i, [128, D], F32) for i in range(2)]; xs_dp = [p1.dep() for _ in range(2)]
        hb = [p1.sb("hb%d" % i, [128, D], BF16) for i in range(2)]; hb_dp = [p1.dep() for _ in range(2)]
        junk = p1.sb("junk", [128, D], BF16); junk_dp = p1.dep()
        st1 = p1.sb("st1", [128, 4 * NS], F32); st1_dp = [p1.dep() for _ in range(NS)]
        for s in range(NS):
            b = s % 2
            dma([(xs[b][:], x_src[s * 128:(s + 1) * 128, :])], R=[x_src_dp[s]], W=[xs_dp[b]], st=xs_dp[b])
            c0 = 4 * s
            op("act", lambda g: g.activation(out=junk[:], in_=xs[b][:], func=AF.Square,
                                             accum_out=st1[:, c0:c0 + 1]),
               R=[xs_dp[b]], W=[junk_dp, st1_dp[s]])
            op("dve", lambda g: g.tensor_scalar(out=st1[:, c0 + 1:c0 + 2], in0=st1[:, c0:c0 + 1],
                                                scalar1=1.0 / D, scalar2=EPS, op0=ALU.mult, op1=ALU.add),
               R=[], W=[st1_dp[s]])
            op("act", lambda g: g.activation(out=st1[:, c0 + 2:c0 + 3], in_=st1[:, c0 + 1:c0 + 2], func=AF.Sqrt),
               W=[st1_dp[s]])
            op("dve", lambda g: g.reciprocal(out=st1[:, c0 + 3:c0 + 4], in_=st1[:, c0 + 2:c0 + 3]),
               W=[st1_dp[s]])
            op("dve", lambda g: g.tensor_scalar(out=hb[b][:], in0=xs[b][:], scalar1=st1[:, c0 + 3:c0 + 4],
                                                scalar2=None, op0=ALU.mult),
               R=[xs_dp[b], st1_dp[s]], W=[hb_dp[b]])
            for cg in range(4):
                pi = rr["ps"] % 4; rr["ps"] += 1
                pv = bf(PS[pi][:])[:, 0:512].rearrange("p (a b) -> p a b", a=4)
                for a in range(4):
                    c = cg * 4 + a
                    op("pe", lambda g: g.transpose(out=pv[:, a, :], in_=hb[b][:, c * 128:(c + 1) * 128],
                                                   identity=ident[:]),
                       R=[hb_dp[b], ident_dp], W=[PD[pi]])
                copy(evac_engine(), hT[:, cg * 4:(cg + 1) * 4, s * 128:(s + 1) * 128], pv,
                     R=[PD[pi]], W=[hT_dp[s]])
        p1.close()

        p2 = Ph(kb)
        wst = [p2.sb("wst%d" % i, [128, NCH, 128], F32) for i in range(3)]; wst_dp = [p2.dep() for _ in range(3)]
        wb = [p2.sb("wb%d" % i, [128, NCH, 128], BF16) for i in range(2)]; wb_dp = [p2.dep() for _ in range(2)]
        wt = [p2.sb("wt%d" % i, [128, NCH, 512], BF16) for i in range(2)]; wt_dp = [p2.dep() for _ in range(2)]
        ofm = [p2.sb("ofm%d" % i, [128, 512], BF16) for i in range(3)]; ofm_dp = [p2.dep() for _ in range(3)]
        otm = [p2.sb("otm%d" % i, [128, 512], BF16) for i in range(3)]; otm_dp = [p2.dep() for _ in range(3)]
        sndt = [p2.sb("sndt%d" % k, [128, T], BF16) for k in range(5)]
        sn_dp = [p2.dep() for _ in range(5)]

        def send(k):
            dma([(SND[k], sndt[k][:])], R=[sn_dp[k]], W=[snd_dp[k]], st=sn_dp[k])
            kb.allgather(SND[k], RCV[k], R=[snd_dp[k]], W=[rcv_dp[k]])
        rp = [p2.sb("rp%d" % i, [64, 512], F32) for i in range(2)]; rp_dp = [p2.dep() for _ in range(2)]
        st2 = p2.sb("st2", [128, 8], F32); st2_dp = p2.dep()
        cn = p2.sb("cn", [128, 384], BF16); cn_dp = p2.dep()
        cnb = [p2.sb("cnb%d" % i, [128, 384], BF16) for i in range(2)]; cnb_dp = [p2.dep() for _ in range(2)]
        op("pool", lambda g: g.memset(sndt[2][64:128, :], 0.0), W=[sn_dp[2]])
        cnt = {"w": 0, "ofm": 0, "otm": 0}

        def load_piece(p):
            if p >= NPIECE:
                return
            i = p % 3
            dma([(wst[i][:], win[l, p])], W=[wst_dp[i]], st=wst_dp[i])

        def cast_piece(i, dst, dst_dp, gofs):
            for c in range(NCH):
                op("pool", lambda g: g.tensor_scalar(out=dst[:, c, :], in0=wst[i][:, c, :],
                                                     scalar1=small[:, gofs + c:gofs + c + 1], scalar2=0.0,
                                                     op0=ALU.mult, op1=ALU.add),
                   R=[wst_dp[i], small_dp], W=[dst_dp], nowait=(c > 0))

        load_piece(0)
        load_piece(1)
        for p in range(10):
            i = p % 3
            wi = p % 2
            cast_piece(i, wb[wi], wb_dp[wi], 0)
            load_piece(p + 2)
            for tg in range(NG):
                tsl = slice(tg * 512, (tg + 1) * 512)
                hdeps = hT_dp[tg * 4:(tg + 1) * 4]
                if p < 9:
                    pi = rr["ps"] % 4; rr["ps"] += 1
                    for c in range(NCH):
                        op("pe", lambda g: g.matmul(out=PS[pi][:], lhsT=wb[wi][:, c, :], rhs=hT[:, c, tsl],
                                                    start=(c == 0), stop=(c == NCH - 1)),
                           R=[wb_dp[wi]] + hdeps, W=[PD[pi]])
                    if p < 8:
                        oi = cnt["ofm"] % 3; cnt["ofm"] += 1
                        copy("act", ofm[oi][:], PS[pi][:], R=[PD[pi]], W=[ofm_dp[oi]])
                        dma([(QA[p, :, tsl], ofm[oi][:])], R=[ofm_dp[oi]], W=[qa_dp], st=ofm_dp[oi], q="act")
                    else:
                        copy(evac_engine(), sndt[3][:, tsl], PS[pi][:], R=[PD[pi]], W=[sn_dp[3]])
                else:
                    pis = []
                    for half in range(2):
                        pi = rr["ps"] % 4; rr["ps"] += 1
                        pis.append(pi)
                        for c in range(NCH):
                            op("pe", lambda g: g.matmul(out=PS[pi][0:64, :], lhsT=wb[wi][:, c, half * 64:(half + 1) * 64],
                                                        rhs=hT[:, c, tsl], start=(c == 0), stop=(c == NCH - 1)),
                               R=[wb_dp[wi]] + hdeps, W=[PD[pi]])
                    op("dve", lambda g: g.tensor_tensor(out=rp[0][:], in0=PS[pis[0]][0:64, :], in1=cs[:, 0, tsl], op=ALU.mult),
                       R=[PD[pis[0]], cs_dp], W=[rp_dp[0]])
                    op("dve", lambda g: g.tensor_tensor(out=rp[1][:], in0=PS[pis[1]][0:64, :], in1=cs[:, 1, tsl], op=ALU.mult),
                       R=[PD[pis[1]], cs_dp], W=[rp_dp[1]])
                    op("dve", lambda g: g.tensor_tensor(out=sndt[2][0:64, tsl], in0=rp[0][:], in1=rp[1][:], op=ALU.add),
                       R=[rp_dp[0], rp_dp[1]], W=[sn_dp[2]])
            if p == 8:
                send(3)
            if p == 9:
                send(2)

        groups = [("cqva", [10, 11, 12, 13]), ("ckv", [14, 15]), ("ga0", [16, 17, 18, 19]), ("ga1", [20, 21, 22, 23])]
        for gi, (gname, pieces) in enumerate(groups):
            wi = gi % 2
            ncol = 128 * len(pieces)
            for k, p in enumerate(pieces):
                i = p % 3
                cast_piece(i, wt[wi][:, :, k * 128:(k + 1) * 128], wt_dp[wi], 0)
                load_piece(p + 2)
            deferred = []
            for s in range(NS):
                pi = rr["ps"] % 4; rr["ps"] += 1
                for c in range(NCH):
                    op("pe", lambda g: g.matmul(out=PS[pi][:, 0:ncol], lhsT=hT[:, c, s * 128:(s + 1) * 128],
                                                rhs=wt[wi][:, c, 0:ncol], start=(c == 0), stop=(c == NCH - 1)),
                       R=[wt_dp[wi], hT_dp[s]], W=[PD[pi]])
                for f in deferred:
                    f()
                deferred = []
                if gname in ("cqva", "ckv"):
                    nr = QL if gname == "cqva" else KVL
                    nck = nr // 128
                    op("act", lambda g: g.activation(out=cn[:, 0:nr], in_=PS[pi][:, 0:nr], func=AF.Square,
                                                     accum_out=st2[:, 0:1]),
                       R=[PD[pi]], W=[cn_dp, st2_dp])
                    op("dve", lambda g: g.tensor_scalar(out=st2[:, 1:2], in0=st2[:, 0:1], scalar1=1.0 / nr,
                                                        scalar2=EPS, op0=ALU.mult, op1=ALU.add), W=[st2_dp])
                    op("act", lambda g: g.activation(out=st2[:, 2:3], in_=st2[:, 1:2], func=AF.Sqrt), W=[st2_dp])
                    op("dve", lambda g: g.reciprocal(out=st2[:, 3:4], in_=st2[:, 2:3]), W=[st2_dp])
                    cb = s % 2
                    op("dve", lambda g: g.tensor_scalar(out=cnb[cb][:, 0:nr], in0=PS[pi][:, 0:nr], scalar1=st2[:, 3:4],
                                                        scalar2=None, op0=ALU.mult),
                       R=[PD[pi], st2_dp], W=[cnb_dp[cb]])
                    if gname == "cqva":
                        copy("act", sndt[4][:, s * 128:(s + 1) * 128], PS[pi][:, 384:512],
                             R=[PD[pi]], W=[sn_dp[4]])
                    def tr(s=s, cb=cb, nck=nck, gname=gname):
                        pj = 4 + (s % 2)
                        pv = bf(PS[pj][:])[:, 0:128 * nck].rearrange("p (a b) -> p a b", a=nck)
                        for a in range(nck):
                            op("pe", lambda g: g.transpose(out=pv[:, a, :], in_=cnb[cb][:, a * 128:(a + 1) * 128], identity=ident[:]),
                               R=[cnb_dp[cb], ident_dp], W=[PD[pj]])
                        if gname == "cqva":
                            copy("dve", cqnT[:, :, s * 128:(s + 1) * 128], pv, R=[PD[pj]], W=[cqnT_dp[s]])
                        else:
                            for a in range(2):
                                copy("dve", sndt[a][:, s * 128:(s + 1) * 128], pv[:, a, :], R=[PD[pj]], W=[sn_dp[a]])
                    deferred.append(tr)
                else:
                    oi = cnt["otm"] % 3; cnt["otm"] += 1
                    op("act", lambda g: g.activation(out=otm[oi][:], in_=PS[pi][:], func=AF.Silu),
                       R=[PD[pi]], W=[otm_dp[oi]])
                    col = (gi - 2) * 512
                    dma([(GT[s * 128:(s + 1) * 128, col:col + 512], otm[oi][:])], R=[otm_dp[oi]], W=[gt_dp[s]],
                        st=otm_dp[oi], q="act")
            for f in deferred:
                f()
            deferred = []
            if gname == "cqva":
                send(4)
            if gname == "ckv":
                send(0)
                send(1)
        for p in range(24, 32):
            i = p % 3
            wi = p % 2
            cast_piece(i, wb[wi], wb_dp[wi], 0)
            load_piece(p + 2)
            for tg in range(NG):
                tsl = slice(tg * 512, (tg + 1) * 512)
                hdeps = hT_dp[tg * 4:(tg + 1) * 4]
                pi = rr["ps"] % 4; rr["ps"] += 1
                for c in range(NCH):
                    op("pe", lambda g: g.matmul(out=PS[pi][:], lhsT=wb[wi][:, c, :], rhs=hT[:, c, tsl],
                                                start=(c == 0), stop=(c == NCH - 1)),
                       R=[wb_dp[wi]] + hdeps, W=[PD[pi]])
                oi = cnt["ofm"] % 3; cnt["ofm"] += 1
                op("act", lambda g: g.activation(out=ofm[oi][:], in_=PS[pi][:], func=AF.Silu),
                   R=[PD[pi]], W=[ofm_dp[oi]])
                dma([(GTB[p - 24, :, tsl], ofm[oi][:])], R=[ofm_dp[oi]], W=[gtb_dp], st=ofm_dp[oi], q="act")
        p2.close()
        pa.close()

        pY = Ph(kb)
        yta = pY.sb("yta", [128, NCH, T], BF16); yta_dp = [pY.dep() for _ in range(NCH)]
        pS = Ph(kb)
        qall = pS.sb("qall", [128, 8, T], BF16); qall_dp = pS.dep()
        kaT = pS.sb("kaT", [128, 2, T], BF16); kaT_dp = pS.dep()
        vraw = pS.sb("vraw", [128, 2, T], BF16); vraw_dp = pS.dep()
        vaug = pS.sb("vaug", [128, 2 * NS * 2, 66], BF16); vaug_dp = pS.dep()
        etab = pS.sb("etab", [128, 3, 16, 128], BF16); etab_dp = pS.dep()
        gat = [pS.sb("gat%d" % i, [128, 1024], BF16) for i in range(2)]; gat_dp = [pS.dep() for _ in range(2)]
        pex = [pS.sb("pex%d" % i, [128, 512], BF16) for i in range(3)]; pex_dp = [pS.dep() for _ in range(3)]
        ptS = [pS.sb("ptS%d" % i, [128, 512], BF16) for i in range(6)]; ptS_dp = [pS.dep() for _ in range(6)]
        ysw = [pS.sb("ysw%d" % i, [128, 1024], BF16) for i in range(2)]; ysw_dp = [pS.dep() for _ in range(2)]
        lS = pS.sb("lS", [128, 8], F32); lS_dp = pS.dep()
        dma([(etab[:], etab_d)], W=[etab_dp], st=etab_dp)
        dma([(qall[:, j, :], QA[j]) for j in range(8)], R=[qa_dp], W=[qall_dp], st=qall_dp)
        dma([(kaT[:, r, :], RCV[3][r * 128:(r + 1) * 128, :]) for r in range(2)], R=[rcv_dp[3]], W=[kaT_dp], st=kaT_dp)
        dma([(vraw[:, r, :], RCV[4][r * 128:(r + 1) * 128, :]) for r in range(2)], R=[rcv_dp[4]], W=[vraw_dp], st=vraw_dp)
        op("pool", lambda g: g.memset(vaug[:], 1.0), W=[vaug_dp])
        for r in range(2):
            dst = vaug[:, r * NS * 2:(r + 1) * NS * 2, 0:64]
            src = vraw[:, r, :].rearrange("p (a d) -> p a d", d=64)
            op("pool", lambda g: g.tensor_copy(out=dst, in_=src), R=[vraw_dp], W=[vaug_dp])
        cS = {"pex": 0, "pt": 0}
        units = [(s, gk, quad) for s in range(NS) for gk in range(2) for quad in range(2)]
        s1out = {}

        def stage1(u):
            s, gk, quad = units[u]
            b = s % 2
            if gk == 0 and quad == 0:
                dma([(gat[b][:], GT[s * 128:(s + 1) * 128, 0:1024])], R=[gt_dp[s]], W=[gat_dp[b]], st=gat_dp[b])
            cands = [(0, 1, s - 1), (1, 0, s), (2, 1, s)]
            if s == 0:
                cands = cands[1:]
            prt = slice(gk * 64, (gk + 1) * 64)
            h0 = gk * 8 + quad * 4
            pts = []
            for (ci, r, ks) in cands:
                pi = rr["ps"] % 2; rr["ps"] += 1
                op("pe", lambda g: g.matmul(out=PS[pi][:], lhsT=kaT[prt, r, ks * 128:(ks + 1) * 128],
                                            rhs=qall[prt, quad * 4:(quad + 1) * 4, s * 128:(s + 1) * 128],
                                            start=True, stop=True),
                   R=[kaT_dp, qall_dp], W=[PD[pi]])
                xi = cS["pex"] % 3; cS["pex"] += 1
                op("act", lambda g: g.activation(out=pex[xi][:], in_=PS[pi][:], func=AF.Exp, scale=0.125),
                   R=[PD[pi]], W=[pex_dp[xi]])
                ti = cS["pt"] % 6; cS["pt"] += 1
                op("dve" if ti % 2 == 0 else "pool", lambda g: g.tensor_tensor(out=ptS[ti][:].rearrange("p (a b) -> p a b", a=4),
                                                    in0=pex[xi][:].rearrange("p (a b) -> p a b", a=4),
                                                    in1=etab[:, ci, h0:h0 + 4, :], op=ALU.mult),
                   R=[pex_dp[xi], etab_dp], W=[ptS_dp[ti]])
                pts.append((ti, r, ks))
            s1out[u] = pts

        def stage2(u):
            s, gk, quad = units[u]
            b = s % 2
            pts = s1out.pop(u)
            oi = 2 + gk * 2 + quad
            h0 = gk * 8 + quad * 4
            ov = PS[oi][:, 0:4 * 65].rearrange("p (a b) -> p a b", a=4)
            for hq in range(4):
                for n, (ti, r, ks) in enumerate(pts):
                    op("pe", lambda g: g.matmul(out=ov[:, hq, :], lhsT=ptS[ti][:, hq * 128:(hq + 1) * 128],
                                                rhs=vaug[:, (r * NS + ks) * 2 + gk, 0:65],
                                                start=(n == 0), stop=(n == len(pts) - 1)),
                       R=[ptS_dp[ti], vaug_dp], W=[PD[oi]])
            op("dve", lambda g: g.tensor_tensor(out=lS[:, 0:4], in0=ov[:, :, 64], in1=small[:, 40 + h0:44 + h0], op=ALU.add),
               R=[PD[oi], esink_dp], W=[lS_dp])
            op("dve", lambda g: g.reciprocal(out=lS[:, 4:8], in_=lS[:, 0:4]), W=[lS_dp])
            for hq in range(4):
                h = h0 + hq
                op("dve", lambda g: g.scalar_tensor_tensor(out=ysw[b][:, h * 64:(h + 1) * 64], in0=ov[:, hq, 0:64],
                                                            scalar=lS[:, 4 + hq:5 + hq], in1=gat[b][:, h * 64:(h + 1) * 64],
                                                            op0=ALU.mult, op1=ALU.mult),
                   R=[PD[oi], lS_dp, gat_dp[b]], W=[ysw_dp[b]], nowait=(hq > 0))
            if gk == 1 and quad == 1:
                pj = 6 + (s % 2)
                pv = bf(PS[pj][:]).rearrange("p (a b) -> p a b", a=8)
                for a in range(8):
                    op("pe", lambda g: g.transpose(out=pv[:, a, :], in_=ysw[b][:, a * 128:(a + 1) * 128], identity=ident[:]),
                       R=[ysw_dp[b], ident_dp], W=[PD[pj]])
                copy("act", yta[:, 0:8, s * 128:(s + 1) * 128], pv, R=[PD[pj]], W=yta_dp[0:8])

        stage1(0)
        for u in range(len(units)):
            if u + 1 < len(units):
                stage1(u + 1)
            stage2(u)
        pS.close()

        pC = Ph(kb)
        ck = pC.sb("ck", [128, 2, 2, T], BF16); ck_dp = pC.dep()
        kr = pC.sb("kr", [64, 2, T], BF16); kr_dp = pC.dep()
        wq = pC.sb("wq", [128, 3, 2048], BF16); wq_dp = pC.dep()
        wkv = pC.sb("wkv", [128, 2, 2048], BF16); wkv_dp = pC.dep()
        wst2 = [pC.sb("wsc%d" % i, [128, 1024], F32) for i in range(2)]; wst2_dp = [pC.dep() for _ in range(2)]
        khT = [pC.sb("khT%d" % i, [128, 2 * T], BF16) for i in range(2)]; khT_dp = [pC.dep() for _ in range(2)]
        vh = [pC.sb("vh%d" % i, [128, 2 * NS, 130], BF16) for i in range(2)]; vh_dp = [pC.dep() for _ in range(2)]
        gbt = [pC.sb("gbt%d" % i, [128, 512], BF16) for i in range(2)]; gbt_dp = [pC.dep() for _ in range(2)]
        onesb = pC.sb("onesb", [128, 128], BF16); onesb_dp = pC.dep()
        rLt = [pC.sb("rLt%d" % i, [128, 512], F32) for i in range(2)]; rLt_dp = [pC.dep() for _ in range(2)]
        op("pool", lambda g: g.memset(onesb[:], 1.0), W=[onesb_dp])
        qn = [pC.sb("qn%d" % i, [128, 512], BF16) for i in range(2)]; qn_dp = [pC.dep() for _ in range(2)]
        qr = [pC.sb("qr%d" % i, [64, 512], BF16) for i in range(2)]; qr_dp = [pC.dep() for _ in range(2)]
        rq = [pC.sb("rq%d" % i, [64, 512], F32) for i in range(2)]; rq_dp = [pC.dep() for _ in range(2)]
        ptC = [pC.sb("ptC%d" % i, [128, 512], BF16) for i in range(4)]; ptC_dp = [pC.dep() for _ in range(4)]
        for r in range(2):
            dma([(ck[:, c, r, :], RCV[c][r * 128:(r + 1) * 128, :]) for c in range(2)]
                + [(kr[:, r, :], RCV[2][r * 128:r * 128 + 64, :])],
                R=rcv_dp[0:3], W=[ck_dp, kr_dp], st=ck_dp if r == 0 else kr_dp)
        nst = 0
        for (wsrc, wdst, wdp, nchk, gofs) in ((wkv_d, wkv, wkv_dp, 2, 19), (wq_d, wq, wq_dp, 3, 16)):
            for c in range(nchk):
                for hf in range(2):
                    i = nst % 2; nst += 1
                    csl = slice(hf * 1024, (hf + 1) * 1024)
                    dma([(wst2[i][:], wsrc[l, :, c, csl])], W=[wst2_dp[i]], st=wst2_dp[i])
                    op("pool", lambda g: g.tensor_scalar(out=wdst[:, c, csl], in0=wst2[i][:], scalar1=small[:, gofs + c:gofs + c + 1],
                                                         scalar2=0.0, op0=ALU.mult, op1=ALU.add),
                       R=[wst2_dp[i], small_dp], W=[wdp])
        for i in range(2):
            op("pool", lambda g: g.memset(vh[i][:, :, 128:130], 1.0), W=[vh_dp[i]])
        cC = {"pt": 0, "g": 0, "s": 0}
        scale = float((128 + 64) ** -0.5)
        NKS = 2 * NS

        def misc_bank():
            pi = 5 + rr["ps"] % 3; rr["ps"] += 1
            return pi

        def prep_chunks(h):
            hb_ = h % 2
            out = []
            for kg in range(2 * T // 512):
                def f(kg=kg):
                    r, t0 = divmod(kg * 512, T)
                    pi = misc_bank()
                    for c in range(2):
                        op("pe", lambda g: g.matmul(out=PS[pi][:], lhsT=wkv[:, c, h * 256:h * 256 + 128],
                                                    rhs=ck[:, c, r, t0:t0 + 512], start=(c == 0), stop=(c == 1)),
                           R=[wkv_dp, ck_dp], W=[PD[pi]])
                    copy("dve", khT[hb_][:, kg * 512:(kg + 1) * 512], PS[pi][:], R=[PD[pi]], W=[khT_dp[hb_]])
                out.append(f)
            for k4 in range(NKS // 4):
                def f(k4=k4):
                    pi = misc_bank()
                    for a in range(4):
                        ksl = k4 * 4 + a
                        r, s_ = divmod(ksl, NS)
                        for c in range(2):
                            op("pe", lambda g: g.matmul(out=PS[pi][:, a * 128:(a + 1) * 128],
                                                        lhsT=ck[:, c, r, s_ * 128:(s_ + 1) * 128],
                                                        rhs=wkv[:, c, h * 256 + 128:h * 256 + 256],
                                                        start=(c == 0), stop=(c == 1)),
                               R=[wkv_dp, ck_dp], W=[PD[pi]])
                    copy("dve", vh[hb_][:, k4 * 4:(k4 + 1) * 4, 0:128],
                         PS[pi][:].rearrange("p (a b) -> p a b", a=4), R=[PD[pi]], W=[vh_dp[hb_]])
                out.append(f)
            return out

        def qproj_chunks(h, G):
            gb_ = (h * NG + G) % 2
            tsl = slice(G * 512, (G + 1) * 512)
            cdeps = cqnT_dp[G * 4:(G + 1) * 4]

            def f0():
                dma([(gbt[gb_][:], GTB[h, :, tsl])], R=[gtb_dp], W=[gbt_dp[gb_]], st=gbt_dp[gb_])
                pi = misc_bank()
                for c in range(3):
                    op("pe", lambda g: g.matmul(out=PS[pi][:], lhsT=wq[:, c, h * 256:h * 256 + 128], rhs=cqnT[:, c, tsl],
                                                start=(c == 0), stop=(c == 2)),
                       R=[wq_dp] + cdeps, W=[PD[pi]])
                copy("dve", qn[gb_][:], PS[pi][:], R=[PD[pi]], W=[qn_dp[gb_]])

            def fr(half):
                pi = misc_bank()
                c0 = h * 256 + 128 + half * 64
                for c in range(3):
                    op("pe", lambda g: g.matmul(out=PS[pi][0:64, :], lhsT=wq[:, c, c0:c0 + 64], rhs=cqnT[:, c, tsl],
                                                start=(c == 0), stop=(c == 2)),
                       R=[wq_dp] + cdeps, W=[PD[pi]])
                op("dve", lambda g: g.tensor_tensor(out=rq[half][:], in0=PS[pi][0:64, :], in1=cs[:, half, tsl], op=ALU.mult),
                   R=[PD[pi], cs_dp], W=[rq_dp[half]])
                if half == 1:
                    op("dve", lambda g: g.tensor_tensor(out=qr[gb_][:], in0=rq[0][:], in1=rq[1][:], op=ALU.add),
                       R=[rq_dp[0], rq_dp[1]], W=[qr_dp[gb_]])
            return [f0, lambda: fr(0), lambda: fr(1)]

        qq = []
        pq = []
        for f in prep_chunks(0):
            f()
        for f in qproj_chunks(0, 0):
            f()
        for h in range(8):
            hb_ = h % 2
            if h + 1 < 8:
                pq = prep_chunks(h + 1)
            for G in range(NG):
                gb_ = (h * NG + G) % 2
                tsl = slice(G * 512, (G + 1) * 512)
                if G + 1 < NG:
                    qq = qproj_chunks(h, G + 1)
                elif h + 1 < 8:
                    qq = qproj_chunks(h + 1, 0)
                ents = [(r, s_, 0, False) for s_ in range(4 * G) for r in range(2)]
                ents += [(r, 4 * G + j, j, True) for j in range(4) for r in range(2)]
                NE = len(ents)
                st_ = {}

                def emit_S(n):
                    r, s_, jmin, msk = ents[n]
                    ncol = (4 - jmin) * 128
                    pi = cC["s"] % 3; cC["s"] += 1
                    kcol = r * T + s_ * 128
                    op("pe", lambda g: g.matmul(out=PS[pi][:, 0:ncol], lhsT=khT[hb_][:, kcol:kcol + 128],
                                                rhs=qn[gb_][:, jmin * 128:512], start=True, stop=False),
                       R=[khT_dp[hb_], qn_dp[gb_]], W=[PD[pi]])
                    op("pe", lambda g: g.matmul(out=PS[pi][:, 0:ncol], lhsT=kr[:, r, s_ * 128:(s_ + 1) * 128],
                                                rhs=qr[gb_][:, jmin * 128:512], start=False, stop=True),
                       R=[kr_dp, qr_dp[gb_]], W=[PD[pi]])
                    ti = cC["pt"] % 4; cC["pt"] += 1
                    op("act", lambda g: g.activation(out=ptC[ti][:, 0:ncol], in_=PS[pi][:, 0:ncol], func=AF.Exp, scale=scale),
                       R=[PD[pi]], W=[ptC_dp[ti]])
                    if msk:
                        op("dve", lambda g: g.tensor_tensor(out=ptC[ti][:, 0:128], in0=ptC[ti][:, 0:128], in1=mask[:, r, :], op=ALU.mult),
                           R=[mask_dp], W=[ptC_dp[ti]])
                    st_[n] = ti

                bO = 3

                def emit_PV(n):
                    r, s_, jmin, msk = ents[n]
                    ti = st_[n]
                    ncol = (4 - jmin) * 128
                    op("pe", lambda g: g.matmul(out=PS[bO][:, jmin * 128:512], lhsT=vh[hb_][:, r * NS + s_, 0:128],
                                                rhs=ptC[ti][:, 0:ncol], start=(n == 0), stop=(n == NE - 1)),
                       R=[ptC_dp[ti], vh_dp[hb_]], W=[PD[bO]])
                    op("pe", lambda g: g.matmul(out=PS[bO + 1][:, jmin * 128:512], lhsT=onesb[:],
                                                rhs=ptC[ti][:, 0:ncol], start=(n == 0), stop=(n == NE - 1)),
                       R=[ptC_dp[ti], onesb_dp], W=[PD[bO + 1]])

                emit_S(0)
                emit_S(1)
                for n in range(NE):
                    if n + 2 < NE:
                        emit_S(n + 2)
                    emit_PV(n)
                    if qq:
                        qq.pop(0)()
                    elif pq:
                        pq.pop(0)()
                while qq:
                    qq.pop(0)()
                op("dve", lambda g: g.reciprocal(out=rLt[gb_][:], in_=PS[bO + 1][:]), R=[PD[bO + 1]], W=[rLt_dp[gb_]])
                op("dve", lambda g: g.tensor_tensor(out=rLt[gb_][:], in0=PS[bO][:], in1=rLt[gb_][:], op=ALU.mult),
                   R=[PD[bO]], W=[rLt_dp[gb_]])
                op("dve", lambda g: g.tensor_tensor(out=yta[:, 8 + h, tsl], in0=rLt[gb_][:], in1=gbt[gb_][:], op=ALU.mult),
                   R=[rLt_dp[gb_], gbt_dp[gb_]], W=[yta_dp[8 + h]])
            while pq:
                pq.pop(0)()
        pC.close()

        pD = Ph(kb)
        wsd = [pD.sb("wsd%d" % i, [128, NCH, 128], F32) for i in range(2)]; wsd_dp = [pD.dep() for _ in range(2)]
        wo = [pD.sb("wo%d" % i, [128, NCH, 512], BF16) for i in range(2)]; wo_dp = [pD.dep() for _ in range(2)]
        xr = [pD.sb("xr%d" % i, [128, 512], F32) for i in range(3)]; xr_dp = [pD.dep() for _ in range(3)]
        xo = [pD.sb("xo%d" % i, [128, 512], F32) for i in range(3)]; xo_dp = [pD.dep() for _ in range(3)]
        cD = {"w": 0, "x": 0}
        def load_wo(cg):
            wi = cg % 2
            for k in range(4):
                i = cD["w"] % 2; cD["w"] += 1
                dma([(wsd[i][:], wo_d[l, cg * 4 + k])], W=[wsd_dp[i]], st=wsd_dp[i])
                op("pool", lambda g: g.tensor_copy(out=wo[wi][:, :, k * 128:(k + 1) * 128], in_=wsd[i][:]),
                   R=[wsd_dp[i]], W=[wo_dp[wi]])

        load_wo(0)
        for cg in range(4):
            wi = cg % 2
            if cg + 1 < 4:
                load_wo(cg + 1)
            for s in range(NS):
                xi = cD["x"] % 3; cD["x"] += 1
                dma([(xr[xi][:], x_src[s * 128:(s + 1) * 128, cg * 512:(cg + 1) * 512])], R=[x_src_dp[s]], W=[xr_dp[xi]], st=xr_dp[xi])
                pi = rr["ps"] % 4; rr["ps"] += 1
                for c in range(NCH):
                    op("pe", lambda g: g.matmul(out=PS[pi][:], lhsT=yta[:, c, s * 128:(s + 1) * 128], rhs=wo[wi][:, c, :],
                                                start=(c == 0), stop=(c == NCH - 1)),
                       R=yta_dp + [wo_dp[wi]], W=[PD[pi]])
                op("dve", lambda g: g.tensor_tensor(out=xo[xi][:], in0=PS[pi][:], in1=xr[xi][:], op=ALU.add),
                   R=[PD[pi], xr_dp[xi]], W=[xo_dp[xi]])
                dma([(x_dst[s * 128:(s + 1) * 128, cg * 512:(cg + 1) * 512], xo[xi][:])], R=[xo_dp[xi]], W=[x_dst_dp[s]], st=xo_dp[xi], q="act")
        pD.close()
        pY.close()

    pF = Ph(kb)
    x_src = XS[(L - 1) % 2]
    x_src_dp = x_dp[1 + (L - 1) % 2]
    gfin = pF.sb("gfin", [128, D], F32); gfin_dp = pF.dep()
    xs = [pF.sb("fx%d" % i, [128, D], F32) for i in range(2)]; xs_dp = [pF.dep() for _ in range(2)]
    fo = [pF.sb("fo%d" % i, [128, D], F32) for i in range(2)]; fo_dp = [pF.dep() for _ in range(2)]
    junk = pF.sb("fjunk", [128, D], BF16); junk_dp = pF.dep()
    st1 = pF.sb("fst", [128, 4 * NS], F32); st1_dp = [pF.dep() for _ in range(NS)]
    out_dp = Dep()
    dma([(gfin[:], gfin_d)], W=[gfin_dp], st=gfin_dp)
    for s in range(NS):
        b = s % 2
        dma([(xs[b][:], x_src[s * 128:(s + 1) * 128, :])], R=[x_src_dp[s]], W=[xs_dp[b]], st=xs_dp[b])
        c0 = 4 * s
        op("act", lambda g: g.activation(out=junk[:], in_=xs[b][:], func=AF.Square, accum_out=st1[:, c0:c0 + 1]),
           R=[xs_dp[b]], W=[junk_dp, st1_dp[s]])
        op("dve", lambda g: g.tensor_scalar(out=st1[:, c0 + 1:c0 + 2], in0=st1[:, c0:c0 + 1], scalar1=1.0 / D, scalar2=EPS,
                                            op0=ALU.mult, op1=ALU.add), W=[st1_dp[s]])
        op("act", lambda g: g.activation(out=st1[:, c0 + 2:c0 + 3], in_=st1[:, c0 + 1:c0 + 2], func=AF.Sqrt), W=[st1_dp[s]])
        op("dve", lambda g: g.reciprocal(out=st1[:, c0 + 3:c0 + 4], in_=st1[:, c0 + 2:c0 + 3]), W=[st1_dp[s]])
        op("dve", lambda g: g.scalar_tensor_tensor(out=fo[b][:], in0=xs[b][:], scalar=st1[:, c0 + 3:c0 + 4], in1=gfin[:],
                                                    op0=ALU.mult, op1=ALU.mult),
           R=[xs_dp[b], st1_dp[s], gfin_dp], W=[fo_dp[b]])
        dma([(out_d[s * 128:(s + 1) * 128, :], fo[b][:])], R=[fo_dp[b]], W=[out_dp], st=fo_dp[b], q="act")
    pF.close()
    gp.es.close()
    es.close()
    return nc


def _host_inputs(x, attn_norm_g, w_in, swa_sinks, q_a_norm_g, kv_a_norm_g, w_q_b, w_kv_b, w_out, final_norm_g, NS):
    L = w_in.shape[0]
    B = x.shape[0]
    T = NS * 128
    f32 = np.float32
    A_Q, A_KV, A_G = 1024, 128, 1024
    o_qa, o_ka, o_va, o_ga = 0, A_Q, A_Q + A_KV, A_Q + 2 * A_KV
    o_cq = o_ga + A_G
    o_ckv = o_cq + QL
    o_kr = o_ckv + KVL
    o_gb = o_kr + 64
    qa_cols = []
    for j in range(8):
        for half in range(2):
            hd = j + 8 * half
            qa_cols += list(range(o_qa + hd * 64, o_qa + (hd + 1) * 64))
    kr_cols = list(range(o_kr, o_kr + 64))
    kr_sw = list(range(o_kr + 32, o_kr + 64)) + list(range(o_kr, o_kr + 32))
    cols = (qa_cols + list(range(o_ka, o_ka + 128)) + kr_cols + kr_sw + list(range(o_cq, o_cq + QL))
            + list(range(o_va, o_va + 128)) + list(range(o_ckv, o_ckv + KVL)) + list(range(o_ga, o_ga + A_G))
            + list(range(o_gb, o_gb + 1024)))
    cols = np.asarray(cols)
    assert cols.size == NPIECE * 128
    wp = w_in[:, :, cols]
    win = np.ascontiguousarray(wp.reshape(L, NCH, 128, NPIECE, 128).transpose(0, 3, 2, 1, 4))
    qcols = []
    for h in range(8):
        b0 = h * 192
        qcols += list(range(b0, b0 + 192)) + list(range(b0 + 160, b0 + 192)) + list(range(b0 + 128, b0 + 160))
    wqp = w_q_b[:, :, np.asarray(qcols)]
    wq = np.ascontiguousarray(wqp.reshape(L, 3, 128, 2048).transpose(0, 2, 1, 3))
    wkv = np.ascontiguousarray(w_kv_b.reshape(L, 2, 128, 2048).transpose(0, 2, 1, 3))
    wo = np.ascontiguousarray(w_out.reshape(L, NCH, 128, 16, 128).transpose(0, 3, 2, 1, 4))
    gin = np.ascontiguousarray(attn_norm_g.reshape(L, NCH, 128).transpose(0, 2, 1))
    gq = np.ascontiguousarray(q_a_norm_g.reshape(L, 3, 128).transpose(0, 2, 1))
    gkv = np.ascontiguousarray(kv_a_norm_g.reshape(L, 2, 128).transpose(0, 2, 1))
    sinks = np.ascontiguousarray(np.broadcast_to(swa_sinks[:, None, :], (L, 128, 16)))
    gfin = np.ascontiguousarray(np.broadcast_to(final_norm_g[None, :], (128, D)))
    ident = np.eye(128, dtype=f32).astype(ml_dtypes.bfloat16)
    S_full = 2 * T
    pos = np.arange(S_full, dtype=f32)
    inv_freq = (10000.0 ** (-np.arange(0, 64, 2, dtype=f32) / 64)).astype(f32)
    ang = pos[:, None] * inv_freq[None, :]
    cos, sin = np.cos(ang).astype(f32), np.sin(ang).astype(f32)
    cos2 = np.concatenate([cos, cos], 1).T
    sin2 = np.concatenate([-sin, sin], 1).T
    slopes = np.exp2(-8.0 * np.arange(1, 17, dtype=f32) / 16).astype(f32)
    kk = np.arange(128)[:, None]
    qq = np.arange(128)[None, :]
    d_prev = (128 + qq - kk).astype(f32)
    d_cur = (qq - kk).astype(f32)
    E_prev = np.where((kk > qq)[:, None, :], np.exp(-slopes[None, :, None] * np.clip(d_prev, 0, 128)[:, None, :]), 0.0).astype(f32)
    E_cur = np.where((kk <= qq)[:, None, :], np.exp(-slopes[None, :, None] * np.clip(d_cur, 0, 128)[:, None, :]), 0.0).astype(f32)
    Z = np.zeros_like(E_prev)
    tri = (kk <= qq).astype(f32)
    ones = np.ones_like(tri)
    zer = np.zeros_like(tri)
    in_maps = []
    for c in range(2 * B):
        b, r = divmod(c, 2)
        xb = x[b].reshape(2 * NS, 128, D)[r::2].reshape(T, D)
        tok = (np.arange(NS)[:, None] * 2 + r) * 128 + np.arange(128)[None, :]
        tok = tok.reshape(-1)
        cs = np.ascontiguousarray(np.stack([cos2[:, tok], sin2[:, tok]], 0))
        if r == 0:
            et = np.stack([E_prev, E_cur, Z], 1)
            mk = np.stack([tri, zer], 1)
        else:
            et = np.stack([Z, E_prev, E_cur], 1)
            mk = np.stack([ones, tri], 1)
        in_maps.append({
            "x": np.ascontiguousarray(xb), "win": win, "wq": wq, "wkv": wkv, "wo": wo, "gin": gin, "gq": gq,
            "gkv": gkv, "sinks": sinks, "gfin": gfin, "cs": cs.astype(f32),
            "etab": np.ascontiguousarray(et).astype(ml_dtypes.bfloat16),
            "mask": np.ascontiguousarray(mk).astype(ml_dtypes.bfloat16), "ident": ident,
        })
    return in_maps


def _assemble(results, B, NS):
    T = NS * 128
    out = np.empty((B, 2 * NS, 128, D), np.float32)
    for c in range(2 * B):
        b, r = divmod(c, 2)
        out[b, r::2] = np.asarray(results[c]["out"]).reshape(NS, 128, D)
    return out.reshape(B, 2 * T, D)


_NC_CACHE = {}


def kernel(x, attn_norm_g, w_in, swa_sinks, q_a_norm_g, kv_a_norm_g, w_q_b, w_kv_b, w_out, final_norm_g):
    args = [np.asarray(a, dtype=np.float32) for a in (x, attn_norm_g, w_in, swa_sinks, q_a_norm_g, kv_a_norm_g,
                                                      w_q_b, w_kv_b, w_out, final_norm_g)]
    B, S = args[0].shape[0], args[0].shape[1]
    NS = S // 256
    L = args[2].shape[0]
    in_maps = _host_inputs(*args, NS=NS)
    key = (NS, L)
    if key not in _NC_CACHE:
        _NC_CACHE[key] = build(NS, L)
    res = run_bass_kernel_spmd(_NC_CACHE[key], in_maps, core_ids=list(range(2 * B)))
    return _assemble(res.results, B, NS)
```

```python
import contextlib
import numpy as np
import ml_dtypes
import concourse.bass as bass
import concourse.mybir as mybir
from concourse.bass_utils import run_bass_kernel_spmd

F32 = mybir.dt.float32
BF16 = mybir.dt.bfloat16
AF = mybir.ActivationFunctionType
ALU = mybir.AluOpType

D = 2048
NCH = 16
EPS = 1e-6
QL = 384
KVL = 256
NPIECE = 32
PAIRS = [[0, 1], [2, 3], [4, 5], [6, 7]]
EMBED_WAIT = True


class Dep:
    __slots__ = ("w", "r", "ds")

    def __init__(self):
        self.w = None
        self.r = {}
        self.ds = None


class KB:
    def __init__(self, nc, es, n_dsem=70):
        self.nc = nc
        self.eng = {"pe": nc.tensor, "act": nc.scalar, "dve": nc.vector,
                    "pool": nc.gpsimd, "sp": nc.sync}
        self.sem = {e: es.enter_context(nc.semaphore("s_" + e)) for e in ("pe", "act", "dve", "pool")}
        self.cnt = {e: 0 for e in self.sem}
        self.known = {e: {} for e in self.eng}
        self.dsems = [[es.enter_context(nc.semaphore("d%d" % i)), 0] for i in range(n_dsem)]
        self.dfree = list(range(n_dsem))
        self.ccsem = es.enter_context(nc.semaphore("ccs"))
        self.cccnt = 0

    def _wait(self, e, ev):
        if ev is None:
            return
        s, v = ev
        k = self.known[e]
        if k.get(id(s), 0) >= v:
            return
        if e == "pe" and s is self.sem["pe"]:
            return
        k[id(s)] = v
        p = self.pend
        if id(s) in p:
            p[id(s)] = (s, max(v, p[id(s)][1]))
        else:
            p[id(s)] = (s, v)

    def _deps(self, e, R, W):
        self.pend = {}
        for d in R:
            self._wait(e, d.w)
        for d in W:
            self._wait(e, d.w)
            for ev in d.r.values():
                self._wait(e, ev)
        return list(self.pend.values())

    def _emit_waits(self, e, waits, keep_last):
        n = len(waits) - (1 if keep_last and waits else 0)
        for (s, v) in waits[:n]:
            self.eng[e].wait_ge(s, v)
        return waits[n:]

    def op(self, e, fn, R=(), W=(), nowait=False):
        rest = []
        if not nowait:
            rest = self._emit_waits(e, self._deps(e, R, W), EMBED_WAIT)
        ins = fn(self.eng[e])
        for (s_, v_) in rest:
            ins._wait_ge(s_, v_)
        self.cnt[e] += 1
        ins.then_inc(self.sem[e], 1)
        ev = (self.sem[e], self.cnt[e])
        for d in R:
            d.r[e] = ev
        for d in W:
            d.w = ev
            d.r = {}
        return ev

    def dsem_alloc(self, dep):
        dep.ds = self.dfree.pop()
        return dep

    def dsem_free(self, dep):
        if dep.ds is not None:
            self.dfree.append(dep.ds)
            dep.ds = None

    def dma(self, pairs, R=(), W=(), st=None, q="sp"):
        if st.ds is None:
            self.dsem_alloc(st)
        self._emit_waits(q, self._deps(q, R, W), False)
        slot = self.dsems[st.ds]
        for (o, i) in pairs:
            ins = self.eng[q].dma_start(out=o, in_=i)
            slot[1] += 16
            ins.then_inc(slot[0], 16)
        ev = (slot[0], slot[1])
        for d in R:
            d.r[("d", st.ds)] = ev
        for d in W:
            d.w = ev
            d.r = {}
        return ev

    def allgather(self, src, dst, R=(), W=()):
        self._emit_waits("pool", self._deps("pool", R, W), False)
        ins = self.nc.gpsimd.collective_compute("AllGather", ALU.bypass, replica_groups=PAIRS,
                                                ins=[src], outs=[dst])
        self.cccnt += 1
        ins.then_inc(self.ccsem, 1)
        ev = (self.ccsem, self.cccnt)
        for d in R:
            d.r["cc"] = ev
        for d in W:
            d.w = ev
            d.r = {}

    def barrier(self):
        evs = [(self.sem[e], self.cnt[e]) for e in self.sem if self.cnt[e] > 0]
        evs += [(s, c) for (s, c) in self.dsems if c > 0]
        if self.cccnt:
            evs.append((self.ccsem, self.cccnt))
        for e in self.eng:
            self.pend = {}
            for ev in evs:
                self._wait(e, ev)
            self._emit_waits(e, list(self.pend.values()), False)


class Ph:
    uid = 0

    def __init__(self, kb):
        self.kb = kb
        self.es = contextlib.ExitStack()
        self.deps = []

    def sb(self, name, shape, dt):
        Ph.uid += 1
        t = self.es.enter_context(self.kb.nc.sbuf_tensor("sb%d_%s" % (Ph.uid, name), list(shape), dt))
        return t

    def dep(self):
        d = Dep()
        self.deps.append(d)
        return d

    def close(self):
        self.kb.barrier()
        for d in self.deps:
            self.kb.dsem_free(d)
        self.es.close()


def build(NS=16, L=4):
    T = NS * 128
    NG = NS // 4
    nc = bass.Bass("TRN2", target_bir_lowering=False)
    dt = nc.dram_tensor
    x_in = dt("x", [T, D], F32, kind="ExternalInput").ap()
    win = dt("win", [L, NPIECE, 128, NCH, 128], F32, kind="ExternalInput").ap()
    wq_d = dt("wq", [L, 128, 3, 2048], F32, kind="ExternalInput").ap()
    wkv_d = dt("wkv", [L, 128, 2, 2048], F32, kind="ExternalInput").ap()
    wo_d = dt("wo", [L, 16, 128, NCH, 128], F32, kind="ExternalInput").ap()
    gin_d = dt("gin", [L, 128, NCH], F32, kind="ExternalInput").ap()
    gq_d = dt("gq", [L, 128, 3], F32, kind="ExternalInput").ap()
    gkv_d = dt("gkv", [L, 128, 2], F32, kind="ExternalInput").ap()
    sink_d = dt("sinks", [L, 128, 16], F32, kind="ExternalInput").ap()
    gfin_d = dt("gfin", [128, D], F32, kind="ExternalInput").ap()
    cs_d = dt("cs", [2, 64, T], F32, kind="ExternalInput").ap()
    etab_d = dt("etab", [128, 3, 16, 128], BF16, kind="ExternalInput").ap()
    mask_d = dt("mask", [128, 2, 128], BF16, kind="ExternalInput").ap()
    ident_d = dt("ident", [128, 128], BF16, kind="ExternalInput").ap()
    out_d = dt("out", [T, D], F32, kind="ExternalOutput").ap()
    QA = dt("QA", [8, 128, T], BF16, kind="Internal").ap()
    GT = dt("GT", [T, 2048], BF16, kind="Internal").ap()
    GTB = dt("GTB", [8, 128, T], BF16, kind="Internal").ap()
    XS = [dt("X%d" % i, [T, D], F32, kind="Internal").ap() for i in range(2)]
    SND = [dt("SND%d" % k, [128, T], BF16).ap() for k in range(5)]
    RCV = [dt("RCV%d" % k, [256, T], BF16).ap() for k in range(5)]
    O_CK, O_KR, O_KA, O_VA = 0, 2 * T, 3 * T, 4 * T

    es = contextlib.ExitStack()
    kb = KB(nc, es)
    op, dma = kb.op, kb.dma

    PS = [es.enter_context(nc.psum_tensor("ps%d" % i, [128, 512], F32)) for i in range(8)]
    PD = [Dep() for _ in range(8)]
    gp = Ph(kb)
    ident = gp.sb("ident", [128, 128], BF16); ident_dp = gp.dep()
    cs = gp.sb("cs", [64, 2, T], F32); cs_dp = gp.dep()
    mask = gp.sb("mask", [128, 2, 128], BF16); mask_dp = gp.dep()
    cqnT = gp.sb("cqnT", [128, 3, T], BF16); cqnT_dp = [gp.dep() for _ in range(NS)]
    small = gp.sb("small", [128, 64], F32)
    small_dp = gp.dep()
    esink_dp = gp.dep()
    dma([(ident[:], ident_d)], W=[ident_dp], st=ident_dp)
    dma([(cs[:, 0, :], cs_d[0]), (cs[:, 1, :], cs_d[1])], W=[cs_dp], st=cs_dp)
    dma([(mask[:], mask_d)], W=[mask_dp], st=mask_dp)

    x_dp = [[Dep() for _ in range(NS)] for _ in range(3)]
    qa_dp = Dep(); gt_dp = [Dep() for _ in range(NS)]; yt_dp = Dep()
    snd_dp = [Dep() for _ in range(5)]; rcv_dp = [Dep() for _ in range(5)]
    gtb_dp = Dep()

    def bf(ps_ap):
        return ps_ap.bitcast(BF16)

    rr = {"ps": 0, "ev": 0}

    def evac_engine():
        rr["ev"] += 1
        return "act" if rr["ev"] % 2 else "dve"

    def copy(e, out, in_, R, W):
        if e == "act":
            return op("act", lambda g: g.copy(out=out, in_=in_), R=R, W=W)
        return op(e, lambda g: g.tensor_copy(out=out, in_=in_), R=R, W=W)

    for l in range(L):
        x_src = x_in if l == 0 else XS[(l - 1) % 2]
        x_src_dp = x_dp[0] if l == 0 else x_dp[1 + (l - 1) % 2]
        x_dst = XS[l % 2]
        x_dst_dp = x_dp[1 + l % 2]

        dma([(small[:, 0:16], gin_d[l]), (small[:, 16:19], gq_d[l]), (small[:, 19:21], gkv_d[l]),
             (small[:, 24:40], sink_d[l])], W=[small_dp], st=small_dp)
        op("act", lambda g: g.activation(out=small[:, 40:56], in_=small[:, 24:40], func=AF.Exp),
           R=[small_dp], W=[esink_dp])

        pa = Ph(kb)
        hT = pa.sb("hT", [128, NCH, T], BF16)
        hT_dp = [[pa.dep() for _ in range(4)] for _ in range(NS)]
        p1 = Ph(kb)
        xs = [p1.sb("xs%d" % i, [128, D], F32) for i in range(3)]; xs_dp = [p1.dep() for _ in range(3)]
        hb = [p1.sb("hb%d" % i, [128, D], BF16) for i in range(2)]; hb_dp = [p1.dep() for _ in range(2)]
        junk = p1.sb("junk", [128, D], BF16); junk_dp = p1.dep()
        st1 = p1.sb("st1", [128, 4 * NS], F32); st1_dp = [p1.dep() for _ in range(NS)]
        for s in range(NS):
            b = s % 2
            xb = s % 3
            dma([(xs[xb][:], x_src[s * 128:(s + 1) * 128, :])], R=[x_src_dp[s]], W=[xs_dp[xb]], st=xs_dp[xb])
            c0 = 4 * s
            op("act", lambda g: g.activation(out=junk[:], in_=xs[xb][:], func=AF.Square,
                                             accum_out=st1[:, c0:c0 + 1]),
               R=[xs_dp[xb]], W=[junk_dp, st1_dp[s]])
            op("dve", lambda g: g.tensor_scalar(out=st1[:, c0 + 1:c0 + 2], in0=st1[:, c0:c0 + 1],
                                                scalar1=1.0 / D, scalar2=EPS, op0=ALU.mult, op1=ALU.add),
               R=[], W=[st1_dp[s]])
            op("act", lambda g: g.activation(out=st1[:, c0 + 2:c0 + 3], in_=st1[:, c0 + 1:c0 + 2], func=AF.Sqrt),
               W=[st1_dp[s]])
            op("dve", lambda g: g.reciprocal(out=st1[:, c0 + 3:c0 + 4], in_=st1[:, c0 + 2:c0 + 3]),
               W=[st1_dp[s]])
            op("dve", lambda g: g.tensor_scalar(out=hb[b][:], in0=xs[xb][:], scalar1=st1[:, c0 + 3:c0 + 4],
                                                scalar2=None, op0=ALU.mult),
               R=[xs_dp[xb], st1_dp[s]], W=[hb_dp[b]])
            for cg in range(4):
                pi = rr["ps"] % 4; rr["ps"] += 1
                pv = bf(PS[pi][:])[:, 0:512].rearrange("p (a b) -> p a b", a=4)
                for a in range(4):
                    c = cg * 4 + a
                    op("pe", lambda g: g.transpose(out=pv[:, a, :], in_=hb[b][:, c * 128:(c + 1) * 128],
                                                   identity=ident[:]),
                       R=[hb_dp[b], ident_dp], W=[PD[pi]])
                copy(evac_engine(), hT[:, cg * 4:(cg + 1) * 4, s * 128:(s + 1) * 128], pv,
                     R=[PD[pi]], W=[hT_dp[s][cg]])
        p1.close()

        p2 = Ph(kb)
        wst = [p2.sb("wst%d" % i, [128, NCH, 128], F32) for i in range(3)]; wst_dp = [p2.dep() for _ in range(3)]
        wb = [p2.sb("wb%d" % i, [128, NCH, 128], BF16) for i in range(2)]; wb_dp = [p2.dep() for _ in range(2)]
        wt = [p2.sb("wt%d" % i, [128, NCH, 512], BF16) for i in range(2)]; wt_dp = [p2.dep() for _ in range(2)]
        ofm = [p2.sb("ofm%d" % i, [128, 512], BF16) for i in range(3)]; ofm_dp = [p2.dep() for _ in range(3)]
        otm = [p2.sb("otm%d" % i, [128, 512], BF16) for i in range(3)]; otm_dp = [p2.dep() for _ in range(3)]
        sndt = [p2.sb("sndt%d" % k, [128, T], BF16) for k in range(5)]
        sn_dp = [p2.dep() for _ in range(5)]

        def send(k):
            dma([(SND[k], sndt[k][:])], R=[sn_dp[k]], W=[snd_dp[k]], st=sn_dp[k])
            kb.allgather(SND[k], RCV[k], R=[snd_dp[k]], W=[rcv_dp[k]])
        rp = [p2.sb("rp%d" % i, [64, 512], F32) for i in range(2)]; rp_dp = [p2.dep() for _ in range(2)]
        st2 = p2.sb("st2", [128, 8], F32); st2_dp = p2.dep()
        cn = p2.sb("cn", [128, 384], BF16); cn_dp = p2.dep()
        cnb = [p2.sb("cnb%d" % i, [128, 384], BF16) for i in range(2)]; cnb_dp = [p2.dep() for _ in range(2)]
        op("pool", lambda g: g.memset(sndt[2][64:128, :], 0.0), W=[sn_dp[2]])
        cnt = {"w": 0, "ofm": 0, "otm": 0}

        def load_piece(p):
            if p >= NPIECE:
                return
            i = p % 3
            dma([(wst[i][:], win[l, p])], W=[wst_dp[i]], st=wst_dp[i])

        def cast_piece(i, dst, dst_dp, gofs):
            for c in range(NCH):
                op("pool", lambda g: g.tensor_scalar(out=dst[:, c, :], in0=wst[i][:, c, :],
                                                     scalar1=small[:, gofs + c:gofs + c + 1], scalar2=0.0,
                                                     op0=ALU.mult, op1=ALU.add),
                   R=[wst_dp[i], small_dp], W=[dst_dp], nowait=(c > 0))

        load_piece(0)
        load_piece(1)
        for p in range(10):
            i = p % 3
            wi = p % 2
            cast_piece(i, wb[wi], wb_dp[wi], 0)
            load_piece(p + 2)
            for tg in range(NG):
                tsl = slice(tg * 512, (tg + 1) * 512)
                hdeps = [d for blk_ in hT_dp[tg * 4:(tg + 1) * 4] for d in blk_]
                if p < 9:
                    pi = rr["ps"] % 4; rr["ps"] += 1
                    for c in range(NCH):
                        op("pe", lambda g: g.matmul(out=PS[pi][:], lhsT=wb[wi][:, c, :], rhs=hT[:, c, tsl],
                                                    start=(c == 0), stop=(c == NCH - 1)),
                           R=[wb_dp[wi]] + hdeps, W=[PD[pi]])
                    if p < 8:
                        oi = cnt["ofm"] % 3; cnt["ofm"] += 1
                        copy("act", ofm[oi][:], PS[pi][:], R=[PD[pi]], W=[ofm_dp[oi]])
                        dma([(QA[p, :, tsl], ofm[oi][:])], R=[ofm_dp[oi]], W=[qa_dp], st=ofm_dp[oi], q="act")
                    else:
                        copy(evac_engine(), sndt[3][:, tsl], PS[pi][:], R=[PD[pi]], W=[sn_dp[3]])
                else:
                    pis = []
                    for half in range(2):
                        pi = rr["ps"] % 4; rr["ps"] += 1
                        pis.append(pi)
                        for c in range(NCH):
                            op("pe", lambda g: g.matmul(out=PS[pi][0:64, :], lhsT=wb[wi][:, c, half * 64:(half + 1) * 64],
                                                        rhs=hT[:, c, tsl], start=(c == 0), stop=(c == NCH - 1)),
                               R=[wb_dp[wi]] + hdeps, W=[PD[pi]])
                    op("dve", lambda g: g.tensor_tensor(out=rp[0][:], in0=PS[pis[0]][0:64, :], in1=cs[:, 0, tsl], op=ALU.mult),
                       R=[PD[pis[0]], cs_dp], W=[rp_dp[0]])
                    op("dve", lambda g: g.tensor_tensor(out=rp[1][:], in0=PS[pis[1]][0:64, :], in1=cs[:, 1, tsl], op=ALU.mult),
                       R=[PD[pis[1]], cs_dp], W=[rp_dp[1]])
                    op("dve", lambda g: g.tensor_tensor(out=sndt[2][0:64, tsl], in0=rp[0][:], in1=rp[1][:], op=ALU.add),
                       R=[rp_dp[0], rp_dp[1]], W=[sn_dp[2]])
            if p == 8:
                send(3)
            if p == 9:
                send(2)

        groups = [("cqva", [10, 11, 12, 13]), ("ckv", [14, 15]), ("ga0", [16, 17, 18, 19]), ("ga1", [20, 21, 22, 23])]
        for gi, (gname, pieces) in enumerate(groups):
            wi = gi % 2
            ncol = 128 * len(pieces)
            for k, p in enumerate(pieces):
                i = p % 3
                cast_piece(i, wt[wi][:, :, k * 128:(k + 1) * 128], wt_dp[wi], 0)
                load_piece(p + 2)
            deferred = []
            for s in range(NS):
                pi = rr["ps"] % 4; rr["ps"] += 1
                for c in range(NCH):
                    op("pe", lambda g: g.matmul(out=PS[pi][:, 0:ncol], lhsT=hT[:, c, s * 128:(s + 1) * 128],
                                                rhs=wt[wi][:, c, 0:ncol], start=(c == 0), stop=(c == NCH - 1)),
                       R=[wt_dp[wi]] + hT_dp[s], W=[PD[pi]])
                for f in deferred:
                    f()
                deferred = []
                if gname in ("cqva", "ckv"):
                    nr = QL if gname == "cqva" else KVL
                    nck = nr // 128
                    op("act", lambda g: g.activation(out=cn[:, 0:nr], in_=PS[pi][:, 0:nr], func=AF.Square,
                                                     accum_out=st2[:, 0:1]),
                       R=[PD[pi]], W=[cn_dp, st2_dp])
                    op("dve", lambda g: g.tensor_scalar(out=st2[:, 1:2], in0=st2[:, 0:1], scalar1=1.0 / nr,
                                                        scalar2=EPS, op0=ALU.mult, op1=ALU.add), W=[st2_dp])
                    op("act", lambda g: g.activation(out=st2[:, 2:3], in_=st2[:, 1:2], func=AF.Sqrt), W=[st2_dp])
                    op("dve", lambda g: g.reciprocal(out=st2[:, 3:4], in_=st2[:, 2:3]), W=[st2_dp])
                    cb = s % 2
                    op("dve", lambda g: g.tensor_scalar(out=cnb[cb][:, 0:nr], in0=PS[pi][:, 0:nr], scalar1=st2[:, 3:4],
                                                        scalar2=None, op0=ALU.mult),
                       R=[PD[pi], st2_dp], W=[cnb_dp[cb]])
                    if gname == "cqva":
                        copy("act", sndt[4][:, s * 128:(s + 1) * 128], PS[pi][:, 384:512],
                             R=[PD[pi]], W=[sn_dp[4]])
                    def tr(s=s, cb=cb, nck=nck, gname=gname):
                        pj = 4 + (s % 2)
                        pv = bf(PS[pj][:])[:, 0:128 * nck].rearrange("p (a b) -> p a b", a=nck)
                        for a in range(nck):
                            op("pe", lambda g: g.transpose(out=pv[:, a, :], in_=cnb[cb][:, a * 128:(a + 1) * 128], identity=ident[:]),
                               R=[cnb_dp[cb], ident_dp], W=[PD[pj]])
                        if gname == "cqva":
                            copy("dve", cqnT[:, :, s * 128:(s + 1) * 128], pv, R=[PD[pj]], W=[cqnT_dp[s]])
                        else:
                            for a in range(2):
                                copy("dve", sndt[a][:, s * 128:(s + 1) * 128], pv[:, a, :], R=[PD[pj]], W=[sn_dp[a]])
                    deferred.append(tr)
                else:
                    oi = cnt["otm"] % 3; cnt["otm"] += 1
                    op("act", lambda g: g.activation(out=otm[oi][:], in_=PS[pi][:], func=AF.Silu),
                       R=[PD[pi]], W=[otm_dp[oi]])
                    col = (gi - 2) * 512
                    dma([(GT[s * 128:(s + 1) * 128, col:col + 512], otm[oi][:])], R=[otm_dp[oi]], W=[gt_dp[s]],
                        st=otm_dp[oi], q="act")
            for f in deferred:
                f()
            deferred = []
            if gname == "cqva":
                send(4)
            if gname == "ckv":
                send(0)
                send(1)
        for p in range(24, 32):
            i = p % 3
            wi = p % 2
            cast_piece(i, wb[wi], wb_dp[wi], 0)
            load_piece(p + 2)
            for tg in range(NG):
                tsl = slice(tg * 512, (tg + 1) * 512)
                hdeps = [d for blk_ in hT_dp[tg * 4:(tg + 1) * 4] for d in blk_]
                pi = rr["ps"] % 4; rr["ps"] += 1
                for c in range(NCH):
                    op("pe", lambda g: g.matmul(out=PS[pi][:], lhsT=wb[wi][:, c, :], rhs=hT[:, c, tsl],
                                                start=(c == 0), stop=(c == NCH - 1)),
                       R=[wb_dp[wi]] + hdeps, W=[PD[pi]])
                oi = cnt["ofm"] % 3; cnt["ofm"] += 1
                op("act", lambda g: g.activation(out=ofm[oi][:], in_=PS[pi][:], func=AF.Silu),
                   R=[PD[pi]], W=[ofm_dp[oi]])
                dma([(GTB[p - 24, :, tsl], ofm[oi][:])], R=[ofm_dp[oi]], W=[gtb_dp], st=ofm_dp[oi], q="act")
        p2.close()
        pa.close()

        pY = Ph(kb)
        yta = pY.sb("yta", [128, NCH, T], BF16); yta_dp = [pY.dep() for _ in range(NCH)]
        pS = Ph(kb)
        qall = pS.sb("qall", [128, 8, T], BF16); qall_dp = pS.dep()
        kaT = pS.sb("kaT", [128, 2, T], BF16); kaT_dp = pS.dep()
        vraw = pS.sb("vraw", [128, 2, T], BF16); vraw_dp = pS.dep()
        vaug = pS.sb("vaug", [128, 2 * NS * 2, 66], BF16); vaug_dp = pS.dep()
        etab = pS.sb("etab", [128, 3, 16, 128], BF16); etab_dp = pS.dep()
        gat = [pS.sb("gat%d" % i, [128, 1024], BF16) for i in range(2)]; gat_dp = [pS.dep() for _ in range(2)]
        pex = [pS.sb("pex%d" % i, [128, 512], BF16) for i in range(3)]; pex_dp = [pS.dep() for _ in range(3)]
        ptS = [pS.sb("ptS%d" % i, [128, 512], BF16) for i in range(6)]; ptS_dp = [pS.dep() for _ in range(6)]
        ysw = [pS.sb("ysw%d" % i, [128, 1024], BF16) for i in range(2)]; ysw_dp = [pS.dep() for _ in range(2)]
        lS = pS.sb("lS", [128, 8], F32); lS_dp = pS.dep()
        dma([(etab[:], etab_d)], W=[etab_dp], st=etab_dp)
        dma([(qall[:, j, :], QA[j]) for j in range(8)], R=[qa_dp], W=[qall_dp], st=qall_dp)
        dma([(kaT[:, r, :], RCV[3][r * 128:(r + 1) * 128, :]) for r in range(2)], R=[rcv_dp[3]], W=[kaT_dp], st=kaT_dp)
        dma([(vraw[:, r, :], RCV[4][r * 128:(r + 1) * 128, :]) for r in range(2)], R=[rcv_dp[4]], W=[vraw_dp], st=vraw_dp)
        op("pool", lambda g: g.memset(vaug[:], 1.0), W=[vaug_dp])
        for r in range(2):
            dst = vaug[:, r * NS * 2:(r + 1) * NS * 2, 0:64]
            src = vraw[:, r, :].rearrange("p (a d) -> p a d", d=64)
            op("pool", lambda g: g.tensor_copy(out=dst, in_=src), R=[vraw_dp], W=[vaug_dp])
        cS = {"pex": 0, "pt": 0}
        units = [(s, gk, quad) for s in range(NS) for gk in range(2) for quad in range(2)]
        s1out = {}

        def stage1(u):
            s, gk, quad = units[u]
            b = s % 2
            if gk == 0 and quad == 0:
                dma([(gat[b][:], GT[s * 128:(s + 1) * 128, 0:1024])], R=[gt_dp[s]], W=[gat_dp[b]], st=gat_dp[b])
            cands = [(0, 1, s - 1), (1, 0, s), (2, 1, s)]
            if s == 0:
                cands = cands[1:]
            prt = slice(gk * 64, (gk + 1) * 64)
            h0 = gk * 8 + quad * 4
            pts = []
            for (ci, r, ks) in cands:
                pi = rr["ps"] % 2; rr["ps"] += 1
                op("pe", lambda g: g.matmul(out=PS[pi][:], lhsT=kaT[prt, r, ks * 128:(ks + 1) * 128],
                                            rhs=qall[prt, quad * 4:(quad + 1) * 4, s * 128:(s + 1) * 128],
                                            start=True, stop=True),
                   R=[kaT_dp, qall_dp], W=[PD[pi]])
                xi = cS["pex"] % 3; cS["pex"] += 1
                op("act", lambda g: g.activation(out=pex[xi][:], in_=PS[pi][:], func=AF.Exp, scale=0.125),
                   R=[PD[pi]], W=[pex_dp[xi]])
                ti = cS["pt"] % 6; cS["pt"] += 1
                op("dve" if ti % 2 == 0 else "pool", lambda g: g.tensor_tensor(out=ptS[ti][:].rearrange("p (a b) -> p a b", a=4),
                                                    in0=pex[xi][:].rearrange("p (a b) -> p a b", a=4),
                                                    in1=etab[:, ci, h0:h0 + 4, :], op=ALU.mult),
                   R=[pex_dp[xi], etab_dp], W=[ptS_dp[ti]])
                pts.append((ti, r, ks))
            s1out[u] = pts

        def stage2(u):
            s, gk, quad = units[u]
            b = s % 2
            pts = s1out.pop(u)
            oi = 2 + gk * 2 + quad
            h0 = gk * 8 + quad * 4
            ov = PS[oi][:, 0:4 * 65].rearrange("p (a b) -> p a b", a=4)
            for hq in range(4):
                for n, (ti, r, ks) in enumerate(pts):
                    op("pe", lambda g: g.matmul(out=ov[:, hq, :], lhsT=ptS[ti][:, hq * 128:(hq + 1) * 128],
                                                rhs=vaug[:, (r * NS + ks) * 2 + gk, 0:65],
                                                start=(n == 0), stop=(n == len(pts) - 1)),
                       R=[ptS_dp[ti], vaug_dp], W=[PD[oi]])
            op("dve", lambda g: g.tensor_tensor(out=lS[:, 0:4], in0=ov[:, :, 64], in1=small[:, 40 + h0:44 + h0], op=ALU.add),
               R=[PD[oi], esink_dp], W=[lS_dp])
            op("dve", lambda g: g.reciprocal(out=lS[:, 4:8], in_=lS[:, 0:4]), W=[lS_dp])
            for hq in range(4):
                h = h0 + hq
                op("dve", lambda g: g.scalar_tensor_tensor(out=ysw[b][:, h * 64:(h + 1) * 64], in0=ov[:, hq, 0:64],
                                                            scalar=lS[:, 4 + hq:5 + hq], in1=gat[b][:, h * 64:(h + 1) * 64],
                                                            op0=ALU.mult, op1=ALU.mult),
                   R=[PD[oi], lS_dp, gat_dp[b]], W=[ysw_dp[b]], nowait=(hq > 0))
            if gk == 1 and quad == 1:
                pj = 6 + (s % 2)
                pv = bf(PS[pj][:]).rearrange("p (a b) -> p a b", a=8)
                for a in range(8):
                    op("pe", lambda g: g.transpose(out=pv[:, a, :], in_=ysw[b][:, a * 128:(a + 1) * 128], identity=ident[:]),
                       R=[ysw_dp[b], ident_dp], W=[PD[pj]])
                copy("act", yta[:, 0:8, s * 128:(s + 1) * 128], pv, R=[PD[pj]], W=yta_dp[0:8])

        stage1(0)
        for u in range(len(units)):
            if u + 1 < len(units):
                stage1(u + 1)
            stage2(u)
        pS.close()

        pC = Ph(kb)
        ck = pC.sb("ck", [128, 2, 2, T], BF16); ck_dp = pC.dep()
        kr = pC.sb("kr", [64, 2, T], BF16); kr_dp = pC.dep()
        wq = pC.sb("wq", [128, 3, 2048], BF16); wq_dp = pC.dep()
        wkv = pC.sb("wkv", [128, 2, 2048], BF16); wkv_dp = pC.dep()
        wst2 = [pC.sb("wsc%d" % i, [128, 1024], F32) for i in range(2)]; wst2_dp = [pC.dep() for _ in range(2)]
        khT = [pC.sb("khT%d" % i, [128, 2 * T], BF16) for i in range(2)]; khT_dp = [pC.dep() for _ in range(2)]
        vh = [pC.sb("vh%d" % i, [128, 2 * NS, 130], BF16) for i in range(2)]; vh_dp = [pC.dep() for _ in range(2)]
        gbt = [pC.sb("gbt%d" % i, [128, 512], BF16) for i in range(2)]; gbt_dp = [pC.dep() for _ in range(2)]
        onesb = pC.sb("onesb", [128, 128], BF16); onesb_dp = pC.dep()
        rLt = [pC.sb("rLt%d" % i, [128, 512], F32) for i in range(2)]; rLt_dp = [pC.dep() for _ in range(2)]
        op("pool", lambda g: g.memset(onesb[:], 1.0), W=[onesb_dp])
        qn = [pC.sb("qn%d" % i, [128, 512], BF16) for i in range(2)]; qn_dp = [pC.dep() for _ in range(2)]
        qr = [pC.sb("qr%d" % i, [64, 512], BF16) for i in range(2)]; qr_dp = [pC.dep() for _ in range(2)]
        rq = [pC.sb("rq%d" % i, [64, 512], F32) for i in range(2)]; rq_dp = [pC.dep() for _ in range(2)]
        ptC = [pC.sb("ptC%d" % i, [128, 512], BF16) for i in range(4)]; ptC_dp = [pC.dep() for _ in range(4)]
        for r in range(2):
            dma([(ck[:, c, r, :], RCV[c][r * 128:(r + 1) * 128, :]) for c in range(2)]
                + [(kr[:, r, :], RCV[2][r * 128:r * 128 + 64, :])],
                R=rcv_dp[0:3], W=[ck_dp, kr_dp], st=ck_dp if r == 0 else kr_dp)
        nst = 0
        for (wsrc, wdst, wdp, nchk, gofs) in ((wkv_d, wkv, wkv_dp, 2, 19), (wq_d, wq, wq_dp, 3, 16)):
            for c in range(nchk):
                for hf in range(2):
                    i = nst % 2; nst += 1
                    csl = slice(hf * 1024, (hf + 1) * 1024)
                    dma([(wst2[i][:], wsrc[l, :, c, csl])], W=[wst2_dp[i]], st=wst2_dp[i])
                    op("pool", lambda g: g.tensor_scalar(out=wdst[:, c, csl], in0=wst2[i][:], scalar1=small[:, gofs + c:gofs + c + 1],
                                                         scalar2=0.0, op0=ALU.mult, op1=ALU.add),
                       R=[wst2_dp[i], small_dp], W=[wdp])
        for i in range(2):
            op("pool", lambda g: g.memset(vh[i][:, :, 128:130], 1.0), W=[vh_dp[i]])
        cC = {"pt": 0, "g": 0, "s": 0}
        scale = float((128 + 64) ** -0.5)
        NKS = 2 * NS

        def misc_bank():
            pi = 5 + rr["ps"] % 3; rr["ps"] += 1
            return pi

        def prep_chunks(h):
            hb_ = h % 2
            out = []
            for kg in range(2 * T // 512):
                def f(kg=kg):
                    r, t0 = divmod(kg * 512, T)
                    pi = misc_bank()
                    for c in range(2):
                        op("pe", lambda g: g.matmul(out=PS[pi][:], lhsT=wkv[:, c, h * 256:h * 256 + 128],
                                                    rhs=ck[:, c, r, t0:t0 + 512], start=(c == 0), stop=(c == 1)),
                           R=[wkv_dp, ck_dp], W=[PD[pi]])
                    copy("dve", khT[hb_][:, kg * 512:(kg + 1) * 512], PS[pi][:], R=[PD[pi]], W=[khT_dp[hb_]])
                out.append(f)
            for k4 in range(NKS // 4):
                def f(k4=k4):
                    pi = misc_bank()
                    for a in range(4):
                        ksl = k4 * 4 + a
                        r, s_ = divmod(ksl, NS)
                        for c in range(2):
                            op("pe", lambda g: g.matmul(out=PS[pi][:, a * 128:(a + 1) * 128],
                                                        lhsT=ck[:, c, r, s_ * 128:(s_ + 1) * 128],
                                                        rhs=wkv[:, c, h * 256 + 128:h * 256 + 256],
                                                        start=(c == 0), stop=(c == 1)),
                               R=[wkv_dp, ck_dp], W=[PD[pi]])
                    copy("dve", vh[hb_][:, k4 * 4:(k4 + 1) * 4, 0:128],
                         PS[pi][:].rearrange("p (a b) -> p a b", a=4), R=[PD[pi]], W=[vh_dp[hb_]])
                out.append(f)
            return out

        def qproj_chunks(h, G):
            gb_ = (h * NG + G) % 2
            tsl = slice(G * 512, (G + 1) * 512)
            cdeps = cqnT_dp[G * 4:(G + 1) * 4]

            def f0():
                dma([(gbt[gb_][:], GTB[h, :, tsl])], R=[gtb_dp], W=[gbt_dp[gb_]], st=gbt_dp[gb_])
                pi = misc_bank()
                for c in range(3):
                    op("pe", lambda g: g.matmul(out=PS[pi][:], lhsT=wq[:, c, h * 256:h * 256 + 128], rhs=cqnT[:, c, tsl],
                                                start=(c == 0), stop=(c == 2)),
                       R=[wq_dp] + cdeps, W=[PD[pi]])
                copy("dve", qn[gb_][:], PS[pi][:], R=[PD[pi]], W=[qn_dp[gb_]])

            def fr(half):
                pi = misc_bank()
                c0 = h * 256 + 128 + half * 64
                for c in range(3):
                    op("pe", lambda g: g.matmul(out=PS[pi][0:64, :], lhsT=wq[:, c, c0:c0 + 64], rhs=cqnT[:, c, tsl],
                                                start=(c == 0), stop=(c == 2)),
                       R=[wq_dp] + cdeps, W=[PD[pi]])
                op("dve", lambda g: g.tensor_tensor(out=rq[half][:], in0=PS[pi][0:64, :], in1=cs[:, half, tsl], op=ALU.mult),
                   R=[PD[pi], cs_dp], W=[rq_dp[half]])
                if half == 1:
                    op("dve", lambda g: g.tensor_tensor(out=qr[gb_][:], in0=rq[0][:], in1=rq[1][:], op=ALU.add),
                       R=[rq_dp[0], rq_dp[1]], W=[qr_dp[gb_]])
            return [f0, lambda: fr(0), lambda: fr(1)]

        qq = []
        pq = []
        for f in prep_chunks(0):
            f()
        for f in qproj_chunks(0, 0):
            f()
        for h in range(8):
            hb_ = h % 2
            if h + 1 < 8:
                pq = prep_chunks(h + 1)
            for G in range(NG):
                gb_ = (h * NG + G) % 2
                tsl = slice(G * 512, (G + 1) * 512)
                if G + 1 < NG:
                    qq = qproj_chunks(h, G + 1)
                elif h + 1 < 8:
                    qq = qproj_chunks(h + 1, 0)
                ents = [(r, s_, 0, False) for s_ in range(4 * G) for r in range(2)]
                ents += [(r, 4 * G + j, j, True) for j in range(4) for r in range(2)]
                NE = len(ents)
                st_ = {}

                def emit_S(n):
                    r, s_, jmin, msk = ents[n]
                    ncol = (4 - jmin) * 128
                    pi = cC["s"] % 3; cC["s"] += 1
                    kcol = r * T + s_ * 128
                    op("pe", lambda g: g.matmul(out=PS[pi][:, 0:ncol], lhsT=khT[hb_][:, kcol:kcol + 128],
                                                rhs=qn[gb_][:, jmin * 128:512], start=True, stop=False),
                       R=[khT_dp[hb_], qn_dp[gb_]], W=[PD[pi]])
                    op("pe", lambda g: g.matmul(out=PS[pi][:, 0:ncol], lhsT=kr[:, r, s_ * 128:(s_ + 1) * 128],
                                                rhs=qr[gb_][:, jmin * 128:512], start=False, stop=True),
                       R=[kr_dp, qr_dp[gb_]], W=[PD[pi]])
                    ti = cC["pt"] % 4; cC["pt"] += 1
                    op("act", lambda g: g.activation(out=ptC[ti][:, 0:ncol], in_=PS[pi][:, 0:ncol], func=AF.Exp, scale=scale),
                       R=[PD[pi]], W=[ptC_dp[ti]])
                    if msk:
                        op("dve", lambda g: g.tensor_tensor(out=ptC[ti][:, 0:128], in0=ptC[ti][:, 0:128], in1=mask[:, r, :], op=ALU.mult),
                           R=[mask_dp], W=[ptC_dp[ti]])
                    st_[n] = ti

                bO = 3

                def emit_PV(n):
                    r, s_, jmin, msk = ents[n]
                    ti = st_[n]
                    ncol = (4 - jmin) * 128
                    op("pe", lambda g: g.matmul(out=PS[bO][:, jmin * 128:512], lhsT=vh[hb_][:, r * NS + s_, 0:128],
                                                rhs=ptC[ti][:, 0:ncol], start=(n == 0), stop=(n == NE - 1)),
                       R=[ptC_dp[ti], vh_dp[hb_]], W=[PD[bO]])
                    op("pe", lambda g: g.matmul(out=PS[bO + 1][:, jmin * 128:512], lhsT=onesb[:],
                                                rhs=ptC[ti][:, 0:ncol], start=(n == 0), stop=(n == NE - 1)),
                       R=[ptC_dp[ti], onesb_dp], W=[PD[bO + 1]])

                emit_S(0)
                emit_S(1)
                for n in range(NE):
                    if n + 2 < NE:
                        emit_S(n + 2)
                    emit_PV(n)
                    if qq:
                        qq.pop(0)()
                    elif pq:
                        pq.pop(0)()
                while qq:
                    qq.pop(0)()
                op("dve", lambda g: g.reciprocal(out=rLt[gb_][:], in_=PS[bO + 1][:]), R=[PD[bO + 1]], W=[rLt_dp[gb_]])
                op("dve", lambda g: g.tensor_tensor(out=rLt[gb_][:], in0=PS[bO][:], in1=rLt[gb_][:], op=ALU.mult),
                   R=[PD[bO]], W=[rLt_dp[gb_]])
                op("dve", lambda g: g.tensor_tensor(out=yta[:, 8 + h, tsl], in0=rLt[gb_][:], in1=gbt[gb_][:], op=ALU.mult),
                   R=[rLt_dp[gb_], gbt_dp[gb_]], W=[yta_dp[8 + h]])
            while pq:
                pq.pop(0)()
        pC.close()

        pD = Ph(kb)
        wsd = [pD.sb("wsd%d" % i, [128, NCH, 128], F32) for i in range(2)]; wsd_dp = [pD.dep() for _ in range(2)]
        wo = [pD.sb("wo%d" % i, [128, NCH, 512], BF16) for i in range(2)]; wo_dp = [pD.dep() for _ in range(2)]
        xr = [pD.sb("xr%d" % i, [128, 512], F32) for i in range(3)]; xr_dp = [pD.dep() for _ in range(3)]
        xo = [pD.sb("xo%d" % i, [128, 512], F32) for i in range(3)]; xo_dp = [pD.dep() for _ in range(3)]
        cD = {"w": 0, "x": 0}
        def load_wo(cg):
            wi = cg % 2
            for k in range(4):
                i = cD["w"] % 2; cD["w"] += 1
                dma([(wsd[i][:], wo_d[l, cg * 4 + k])], W=[wsd_dp[i]], st=wsd_dp[i])
                op("pool", lambda g: g.tensor_copy(out=wo[wi][:, :, k * 128:(k + 1) * 128], in_=wsd[i][:]),
                   R=[wsd_dp[i]], W=[wo_dp[wi]])

        load_wo(0)
        for cg in range(4):
            wi = cg % 2
            if cg + 1 < 4:
                load_wo(cg + 1)
            for s in range(NS):
                xi = cD["x"] % 3; cD["x"] += 1
                dma([(xr[xi][:], x_src[s * 128:(s + 1) * 128, cg * 512:(cg + 1) * 512])], R=[x_src_dp[s]], W=[xr_dp[xi]], st=xr_dp[xi])
                pi = rr["ps"] % 4; rr["ps"] += 1
                for c in range(NCH):
                    op("pe", lambda g: g.matmul(out=PS[pi][:], lhsT=yta[:, c, s * 128:(s + 1) * 128], rhs=wo[wi][:, c, :],
                                                start=(c == 0), stop=(c == NCH - 1)),
                       R=yta_dp + [wo_dp[wi]], W=[PD[pi]])
                op("dve", lambda g: g.tensor_tensor(out=xo[xi][:], in0=PS[pi][:], in1=xr[xi][:], op=ALU.add),
                   R=[PD[pi], xr_dp[xi]], W=[xo_dp[xi]])
                dma([(x_dst[s * 128:(s + 1) * 128, cg * 512:(cg + 1) * 512], xo[xi][:])], R=[xo_dp[xi]], W=[x_dst_dp[s]], st=xo_dp[xi], q="act")
        pD.close()
        pY.close()

    pF = Ph(kb)
    x_src = XS[(L - 1) % 2]
    x_src_dp = x_dp[1 + (L - 1) % 2]
    gfin = pF.sb("gfin", [128, D], F32); gfin_dp = pF.dep()
    xs = [pF.sb("fx%d" % i, [128, D], F32) for i in range(2)]; xs_dp = [pF.dep() for _ in range(2)]
    fo = [pF.sb("fo%d" % i, [128, D], F32) for i in range(2)]; fo_dp = [pF.dep() for _ in range(2)]
    junk = pF.sb("fjunk", [128, D], BF16); junk_dp = pF.dep()
    st1 = pF.sb("fst", [128, 4 * NS], F32); st1_dp = [pF.dep() for _ in range(NS)]
    out_dp = Dep()
    dma([(gfin[:], gfin_d)], W=[gfin_dp], st=gfin_dp)
    for s in range(NS):
        b = s % 2
        dma([(xs[b][:], x_src[s * 128:(s + 1) * 128, :])], R=[x_src_dp[s]], W=[xs_dp[b]], st=xs_dp[b])
        c0 = 4 * s
        op("act", lambda g: g.activation(out=junk[:], in_=xs[b][:], func=AF.Square, accum_out=st1[:, c0:c0 + 1]),
           R=[xs_dp[b]], W=[junk_dp, st1_dp[s]])
        op("dve", lambda g: g.tensor_scalar(out=st1[:, c0 + 1:c0 + 2], in0=st1[:, c0:c0 + 1], scalar1=1.0 / D, scalar2=EPS,
                                            op0=ALU.mult, op1=ALU.add), W=[st1_dp[s]])
        op("act", lambda g: g.activation(out=st1[:, c0 + 2:c0 + 3], in_=st1[:, c0 + 1:c0 + 2], func=AF.Sqrt), W=[st1_dp[s]])
        op("dve", lambda g: g.reciprocal(out=st1[:, c0 + 3:c0 + 4], in_=st1[:, c0 + 2:c0 + 3]), W=[st1_dp[s]])
        op("dve", lambda g: g.scalar_tensor_tensor(out=fo[b][:], in0=xs[b][:], scalar=st1[:, c0 + 3:c0 + 4], in1=gfin[:],
                                                    op0=ALU.mult, op1=ALU.mult),
           R=[xs_dp[b], st1_dp[s], gfin_dp], W=[fo_dp[b]])
        dma([(out_d[s * 128:(s + 1) * 128, :], fo[b][:])], R=[fo_dp[b]], W=[out_dp], st=fo_dp[b], q="act")
    pF.close()
    gp.es.close()
    es.close()
    return nc


def _host_inputs(x, attn_norm_g, w_in, swa_sinks, q_a_norm_g, kv_a_norm_g, w_q_b, w_kv_b, w_out, final_norm_g, NS):
    L = w_in.shape[0]
    B = x.shape[0]
    T = NS * 128
    f32 = np.float32
    A_Q, A_KV, A_G = 1024, 128, 1024
    o_qa, o_ka, o_va, o_ga = 0, A_Q, A_Q + A_KV, A_Q + 2 * A_KV
    o_cq = o_ga + A_G
    o_ckv = o_cq + QL
    o_kr = o_ckv + KVL
    o_gb = o_kr + 64
    qa_cols = []
    for j in range(8):
        for half in range(2):
            hd = j + 8 * half
            qa_cols += list(range(o_qa + hd * 64, o_qa + (hd + 1) * 64))
    kr_cols = list(range(o_kr, o_kr + 64))
    kr_sw = list(range(o_kr + 32, o_kr + 64)) + list(range(o_kr, o_kr + 32))
    cols = (qa_cols + list(range(o_ka, o_ka + 128)) + kr_cols + kr_sw + list(range(o_cq, o_cq + QL))
            + list(range(o_va, o_va + 128)) + list(range(o_ckv, o_ckv + KVL)) + list(range(o_ga, o_ga + A_G))
            + list(range(o_gb, o_gb + 1024)))
    cols = np.asarray(cols)
    assert cols.size == NPIECE * 128
    wp = w_in[:, :, cols]
    win = np.ascontiguousarray(wp.reshape(L, NCH, 128, NPIECE, 128).transpose(0, 3, 2, 1, 4))
    qcols = []
    for h in range(8):
        b0 = h * 192
        qcols += list(range(b0, b0 + 192)) + list(range(b0 + 160, b0 + 192)) + list(range(b0 + 128, b0 + 160))
    wqp = w_q_b[:, :, np.asarray(qcols)]
    wq = np.ascontiguousarray(wqp.reshape(L, 3, 128, 2048).transpose(0, 2, 1, 3))
    wkv = np.ascontiguousarray(w_kv_b.reshape(L, 2, 128, 2048).transpose(0, 2, 1, 3))
    wo = np.ascontiguousarray(w_out.reshape(L, NCH, 128, 16, 128).transpose(0, 3, 2, 1, 4))
    gin = np.ascontiguousarray(attn_norm_g.reshape(L, NCH, 128).transpose(0, 2, 1))
    gq = np.ascontiguousarray(q_a_norm_g.reshape(L, 3, 128).transpose(0, 2, 1))
    gkv = np.ascontiguousarray(kv_a_norm_g.reshape(L, 2, 128).transpose(0, 2, 1))
    sinks = np.ascontiguousarray(np.broadcast_to(swa_sinks[:, None, :], (L, 128, 16)))
    gfin = np.ascontiguousarray(np.broadcast_to(final_norm_g[None, :], (128, D)))
    ident = np.eye(128, dtype=f32).astype(ml_dtypes.bfloat16)
    S_full = 2 * T
    pos = np.arange(S_full, dtype=f32)
    inv_freq = (10000.0 ** (-np.arange(0, 64, 2, dtype=f32) / 64)).astype(f32)
    ang = pos[:, None] * inv_freq[None, :]
    cos, sin = np.cos(ang).astype(f32), np.sin(ang).astype(f32)
    cos2 = np.concatenate([cos, cos], 1).T
    sin2 = np.concatenate([-sin, sin], 1).T
    slopes = np.exp2(-8.0 * np.arange(1, 17, dtype=f32) / 16).astype(f32)
    kk = np.arange(128)[:, None]
    qq = np.arange(128)[None, :]
    d_prev = (128 + qq - kk).astype(f32)
    d_cur = (qq - kk).astype(f32)
    E_prev = np.where((kk > qq)[:, None, :], np.exp(-slopes[None, :, None] * np.clip(d_prev, 0, 128)[:, None, :]), 0.0).astype(f32)
    E_cur = np.where((kk <= qq)[:, None, :], np.exp(-slopes[None, :, None] * np.clip(d_cur, 0, 128)[:, None, :]), 0.0).astype(f32)
    Z = np.zeros_like(E_prev)
    tri = (kk <= qq).astype(f32)
    ones = np.ones_like(tri)
    zer = np.zeros_like(tri)
    in_maps = []
    for c in range(2 * B):
        b, r = divmod(c, 2)
        xb = x[b].reshape(2 * NS, 128, D)[r::2].reshape(T, D)
        tok = (np.arange(NS)[:, None] * 2 + r) * 128 + np.arange(128)[None, :]
        tok = tok.reshape(-1)
        cs = np.ascontiguousarray(np.stack([cos2[:, tok], sin2[:, tok]], 0))
        if r == 0:
            et = np.stack([E_prev, E_cur, Z], 1)
            mk = np.stack([tri, zer], 1)
        else:
            et = np.stack([Z, E_prev, E_cur], 1)
            mk = np.stack([ones, tri], 1)
        in_maps.append({
            "x": np.ascontiguousarray(xb), "win": win, "wq": wq, "wkv": wkv, "wo": wo, "gin": gin, "gq": gq,
            "gkv": gkv, "sinks": sinks, "gfin": gfin, "cs": cs.astype(f32),
            "etab": np.ascontiguousarray(et).astype(ml_dtypes.bfloat16),
            "mask": np.ascontiguousarray(mk).astype(ml_dtypes.bfloat16), "ident": ident,
        })
    return in_maps


def _assemble(results, B, NS):
    T = NS * 128
    out = np.empty((B, 2 * NS, 128, D), np.float32)
    for c in range(2 * B):
        b, r = divmod(c, 2)
        out[b, r::2] = np.asarray(results[c]["out"]).reshape(NS, 128, D)
    return out.reshape(B, 2 * T, D)


_NC_CACHE = {}


def kernel(x, attn_norm_g, w_in, swa_sinks, q_a_norm_g, kv_a_norm_g, w_q_b, w_kv_b, w_out, final_norm_g):
    args = [np.asarray(a, dtype=np.float32) for a in (x, attn_norm_g, w_in, swa_sinks, q_a_norm_g, kv_a_norm_g,
                                                      w_q_b, w_kv_b, w_out, final_norm_g)]
    B, S = args[0].shape[0], args[0].shape[1]
    NS = S // 256
    L = args[2].shape[0]
    in_maps = _host_inputs(*args, NS=NS)
    key = (NS, L)
    if key not in _NC_CACHE:
        _NC_CACHE[key] = build(NS, L)
    res = run_bass_kernel_spmd(_NC_CACHE[key], in_maps, core_ids=list(range(2 * B)))
    return _assemble(res.results, B, NS)
```

```python
import contextlib
import numpy as np
import ml_dtypes
import concourse.bass as bass
import concourse.mybir as mybir
from concourse.bass_utils import run_bass_kernel_spmd

F32 = mybir.dt.float32
BF16 = mybir.dt.bfloat16
AF = mybir.ActivationFunctionType
ALU = mybir.AluOpType

D = 2048
NCH = 16
EPS = 1e-6
QL = 384
KVL = 256
NPIECE = 32
PAIRS = [[0, 1], [2, 3], [4, 5], [6, 7]]
EMBED_WAIT = True


class Dep:
    __slots__ = ("w", "r", "ds")

    def __init__(self):
        self.w = None
        self.r = {}
        self.ds = None


class KB:
    def __init__(self, nc, es, n_dsem=70):
        self.nc = nc
        self.eng = {"pe": nc.tensor, "act": nc.scalar, "dve": nc.vector,
                    "pool": nc.gpsimd, "sp": nc.sync}
        self.sem = {e: es.enter_context(nc.semaphore("s_" + e)) for e in ("pe", "act", "dve", "pool")}
        self.cnt = {e: 0 for e in self.sem}
        self.known = {e: {} for e in self.eng}
        self.dsems = [[es.enter_context(nc.semaphore("d%d" % i)), 0] for i in range(n_dsem)]
        self.dfree = list(range(n_dsem))
        self.ccsem = es.enter_context(nc.semaphore("ccs"))
        self.cccnt = 0

    def _wait(self, e, ev):
        if ev is None:
            return
        s, v = ev
        k = self.known[e]
        if k.get(id(s), 0) >= v:
            return
        if e == "pe" and s is self.sem["pe"]:
            return
        k[id(s)] = v
        p = self.pend
        if id(s) in p:
            p[id(s)] = (s, max(v, p[id(s)][1]))
        else:
            p[id(s)] = (s, v)

    def _deps(self, e, R, W):
        self.pend = {}
        for d in R:
            self._wait(e, d.w)
        for d in W:
            self._wait(e, d.w)
            for ev in d.r.values():
                self._wait(e, ev)
        return list(self.pend.values())

    def _emit_waits(self, e, waits, keep_last):
        n = len(waits) - (1 if keep_last and waits else 0)
        for (s, v) in waits[:n]:
            self.eng[e].wait_ge(s, v)
        return waits[n:]

    def op(self, e, fn, R=(), W=(), nowait=False):
        rest = []
        if not nowait:
            rest = self._emit_waits(e, self._deps(e, R, W), EMBED_WAIT)
        ins = fn(self.eng[e])
        for (s_, v_) in rest:
            ins._wait_ge(s_, v_)
        self.cnt[e] += 1
        ins.then_inc(self.sem[e], 1)
        ev = (self.sem[e], self.cnt[e])
        for d in R:
            d.r[e] = ev
        for d in W:
            d.w = ev
            d.r = {}
        return ev

    def dsem_alloc(self, dep):
        dep.ds = self.dfree.pop()
        return dep

    def dsem_free(self, dep):
        if dep.ds is not None:
            self.dfree.append(dep.ds)
            dep.ds = None

    def dma(self, pairs, R=(), W=(), st=None, q="sp"):
        if st.ds is None:
            self.dsem_alloc(st)
        self._emit_waits(q, self._deps(q, R, W), False)
        slot = self.dsems[st.ds]
        for (o, i) in pairs:
            ins = self.eng[q].dma_start(out=o, in_=i)
            slot[1] += 16
            ins.then_inc(slot[0], 16)
        ev = (slot[0], slot[1])
        for d in R:
            d.r[("d", st.ds)] = ev
        for d in W:
            d.w = ev
            d.r = {}
        return ev

    def allgather(self, src, dst, R=(), W=()):
        self._emit_waits("pool", self._deps("pool", R, W), False)
        ins = self.nc.gpsimd.collective_compute("AllGather", ALU.bypass, replica_groups=PAIRS,
                                                ins=[src], outs=[dst])
        self.cccnt += 1
        ins.then_inc(self.ccsem, 1)
        ev = (self.ccsem, self.cccnt)
        for d in R:
            d.r["cc"] = ev
        for d in W:
            d.w = ev
            d.r = {}

    def barrier(self):
        evs = [(self.sem[e], self.cnt[e]) for e in self.sem if self.cnt[e] > 0]
        evs += [(s, c) for (s, c) in self.dsems if c > 0]
        if self.cccnt:
            evs.append((self.ccsem, self.cccnt))
        for e in self.eng:
            self.pend = {}
            for ev in evs:
                self._wait(e, ev)
            self._emit_waits(e, list(self.pend.values()), False)


class Ph:
    uid = 0

    def __init__(self, kb):
        self.kb = kb
        self.es = contextlib.ExitStack()
        self.deps = []

    def sb(self, name, shape, dt):
        Ph.uid += 1
        t = self.es.enter_context(self.kb.nc.sbuf_tensor("sb%d_%s" % (Ph.uid, name), list(shape), dt))
        return t

    def dep(self):
        d = Dep()
        self.deps.append(d)
        return d

    def close(self):
        self.kb.barrier()
        for d in self.deps:
            self.kb.dsem_free(d)
        self.es.close()


def build(NS=16, L=4):
    T = NS * 128
    NG = NS // 4
    nc = bass.Bass("TRN2", target_bir_lowering=False)
    dt = nc.dram_tensor
    x_in = dt("x", [T, D], F32, kind="ExternalInput").ap()
    win = dt("win", [L, NPIECE, 128, NCH, 128], F32, kind="ExternalInput").ap()
    wq_d = dt("wq", [L, 128, 3, 2048], F32, kind="ExternalInput").ap()
    wkv_d = dt("wkv", [L, 128, 2, 2048], F32, kind="ExternalInput").ap()
    wo_d = dt("wo", [L, 16, 128, NCH, 128], F32, kind="ExternalInput").ap()
    gin_d = dt("gin", [L, 128, NCH], F32, kind="ExternalInput").ap()
    gq_d = dt("gq", [L, 128, 3], F32, kind="ExternalInput").ap()
    gkv_d = dt("gkv", [L, 128, 2], F32, kind="ExternalInput").ap()
    sink_d = dt("sinks", [L, 128, 16], F32, kind="ExternalInput").ap()
    gfin_d = dt("gfin", [128, D], F32, kind="ExternalInput").ap()
    cs_d = dt("cs", [2, 64, T], F32, kind="ExternalInput").ap()
    etab_d = dt("etab", [128, 3, 16, 128], BF16, kind="ExternalInput").ap()
    mask_d = dt("mask", [128, 2, 128], BF16, kind="ExternalInput").ap()
    ident_d = dt("ident", [128, 128], BF16, kind="ExternalInput").ap()
    out_d = dt("out", [T, D], F32, kind="ExternalOutput").ap()
    QA = dt("QA", [8, 128, T], BF16, kind="Internal").ap()
    GT = dt("GT", [T, 2048], BF16, kind="Internal").ap()
    GTB = dt("GTB", [8, 128, T], BF16, kind="Internal").ap()
    XS = [dt("X%d" % i, [T, D], F32, kind="Internal").ap() for i in range(2)]
    SND = [dt("SND%d" % k, [128, T], BF16).ap() for k in range(5)]
    RCV = [dt("RCV%d" % k, [256, T], BF16).ap() for k in range(5)]
    O_CK, O_KR, O_KA, O_VA = 0, 2 * T, 3 * T, 4 * T

    es = contextlib.ExitStack()
    kb = KB(nc, es)
    op, dma = kb.op, kb.dma

    PS = [es.enter_context(nc.psum_tensor("ps%d" % i, [128, 512], F32)) for i in range(8)]
    PD = [Dep() for _ in range(8)]
    gp = Ph(kb)
    ident = gp.sb("ident", [128, 128], BF16); ident_dp = gp.dep()
    cs = gp.sb("cs", [64, 2, T], F32); cs_dp = gp.dep()
    mask = gp.sb("mask", [128, 2, 128], BF16); mask_dp = gp.dep()
    cqnT = gp.sb("cqnT", [128, 3, T], BF16); cqnT_dp = [gp.dep() for _ in range(NS)]
    small = gp.sb("small", [128, 64], F32)
    small_dp = gp.dep()
    esink_dp = gp.dep()
    dma([(ident[:], ident_d)], W=[ident_dp], st=ident_dp)
    dma([(cs[:, 0, :], cs_d[0]), (cs[:, 1, :], cs_d[1])], W=[cs_dp], st=cs_dp)
    dma([(mask[:], mask_d)], W=[mask_dp], st=mask_dp)

    x_dp = [[Dep() for _ in range(NS)] for _ in range(3)]
    qa_dp = Dep(); gt_dp = [Dep() for _ in range(NS)]; yt_dp = Dep()
    snd_dp = [Dep() for _ in range(5)]; rcv_dp = [Dep() for _ in range(5)]
    gtb_dp = Dep()

    def bf(ps_ap):
        return ps_ap.bitcast(BF16)

    rr = {"ps": 0, "ev": 0}

    def evac_engine():
        rr["ev"] += 1
        return "act" if rr["ev"] % 2 else "dve"

    def copy(e, out, in_, R, W):
        if e == "act":
            return op("act", lambda g: g.copy(out=out, in_=in_), R=R, W=W)
        return op(e, lambda g: g.tensor_copy(out=out, in_=in_), R=R, W=W)

    for l in range(L):
        x_src = x_in if l == 0 else XS[(l - 1) % 2]
        x_src_dp = x_dp[0] if l == 0 else x_dp[1 + (l - 1) % 2]
        x_dst = XS[l % 2]
        x_dst_dp = x_dp[1 + l % 2]

        dma([(small[:, 0:16], gin_d[l]), (small[:, 16:19], gq_d[l]), (small[:, 19:21], gkv_d[l]),
             (small[:, 24:40], sink_d[l])], W=[small_dp], st=small_dp)
        op("act", lambda g: g.activation(out=small[:, 40:56], in_=small[:, 24:40], func=AF.Exp),
           R=[small_dp], W=[esink_dp])

        pa = Ph(kb)
        hT = pa.sb("hT", [128, NCH, T], BF16)
        hT_dp = [[pa.dep() for _ in range(4)] for _ in range(NS)]
        p1 = Ph(kb)
        xs = [p1.sb("xs%d" % i, [128, D], F32) for i in range(3)]; xs_dp = [p1.dep() for _ in range(3)]
        hb = [p1.sb("hb%d" % i, [128, D], BF16) for i in range(2)]; hb_dp = [p1.dep() for _ in range(2)]
        junk = p1.sb("junk", [128, D], BF16); junk_dp = p1.dep()
        st1 = p1.sb("st1", [128, 4 * NS], F32); st1_dp = [p1.dep() for _ in range(NS)]
        for s in range(NS):
            b = s % 2
            xb = s % 3
            dma([(xs[xb][:], x_src[s * 128:(s + 1) * 128, :])], R=[x_src_dp[s]], W=[xs_dp[xb]], st=xs_dp[xb])
            c0 = 4 * s
            op("act", lambda g: g.activation(out=junk[:], in_=xs[xb][:], func=AF.Square,
                                             accum_out=st1[:, c0:c0 + 1]),
               R=[xs_dp[xb]], W=[junk_dp, st1_dp[s]])
            op("dve", lambda g: g.tensor_scalar(out=st1[:, c0 + 1:c0 + 2], in0=st1[:, c0:c0 + 1],
                                                scalar1=1.0 / D, scalar2=EPS, op0=ALU.mult, op1=ALU.add),
               R=[], W=[st1_dp[s]])
            op("act", lambda g: g.activation(out=st1[:, c0 + 2:c0 + 3], in_=st1[:, c0 + 1:c0 + 2], func=AF.Sqrt),
               W=[st1_dp[s]])
            op("dve", lambda g: g.reciprocal(out=st1[:, c0 + 3:c0 + 4], in_=st1[:, c0 + 2:c0 + 3]),
               W=[st1_dp[s]])
            op("dve", lambda g: g.tensor_scalar(out=hb[b][:], in0=xs[xb][:], scalar1=st1[:, c0 + 3:c0 + 4],
                                                scalar2=None, op0=ALU.mult),
               R=[xs_dp[xb], st1_dp[s]], W=[hb_dp[b]])
            for cg in range(4):
                pi = rr["ps"] % 4; rr["ps"] += 1
                pv = bf(PS[pi][:])[:, 0:512].rearrange("p (a b) -> p a b", a=4)
                for a in range(4):
                    c = cg * 4 + a
                    op("pe", lambda g: g.transpose(out=pv[:, a, :], in_=hb[b][:, c * 128:(c + 1) * 128],
                                                   identity=ident[:]),
                       R=[hb_dp[b], ident_dp], W=[PD[pi]])
                copy(evac_engine(), hT[:, cg * 4:(cg + 1) * 4, s * 128:(s + 1) * 128], pv,
                     R=[PD[pi]], W=[hT_dp[s][cg]])
        p1.close()

        p2 = Ph(kb)
        wst = [p2.sb("wst%d" % i, [128, NCH, 128], F32) for i in range(3)]; wst_dp = [p2.dep() for _ in range(3)]
        wb = [p2.sb("wb%d" % i, [128, NCH, 128], BF16) for i in range(2)]; wb_dp = [p2.dep() for _ in range(2)]
        wt = [p2.sb("wt%d" % i, [128, NCH, 512], BF16) for i in range(2)]; wt_dp = [p2.dep() for _ in range(2)]
        ofm = [p2.sb("ofm%d" % i, [128, 512], BF16) for i in range(3)]; ofm_dp = [p2.dep() for _ in range(3)]
        otm = [p2.sb("otm%d" % i, [128, 512], BF16) for i in range(3)]; otm_dp = [p2.dep() for _ in range(3)]
        sndt = [p2.sb("sndt%d" % k, [128, T], BF16) for k in range(5)]
        sn_dp = [p2.dep() for _ in range(5)]

        def send(k):
            dma([(SND[k], sndt[k][:])], R=[sn_dp[k]], W=[snd_dp[k]], st=sn_dp[k])
            kb.allgather(SND[k], RCV[k], R=[snd_dp[k]], W=[rcv_dp[k]])
        rp = [p2.sb("rp%d" % i, [64, 512], F32) for i in range(2)]; rp_dp = [p2.dep() for _ in range(2)]
        st2 = p2.sb("st2", [128, 8], F32); st2_dp = p2.dep()
        cn = p2.sb("cn", [128, 384], BF16); cn_dp = p2.dep()
        cnb = [p2.sb("cnb%d" % i, [128, 384], BF16) for i in range(2)]; cnb_dp = [p2.dep() for _ in range(2)]
        op("pool", lambda g: g.memset(sndt[2][64:128, :], 0.0), W=[sn_dp[2]])
        cnt = {"w": 0, "ofm": 0, "otm": 0}

        def load_piece(p):
            if p >= NPIECE:
                return
            i = p % 3
            dma([(wst[i][:], win[l, p])], W=[wst_dp[i]], st=wst_dp[i])

        def cast_piece(i, dst, dst_dp, gofs):
            for c in range(NCH):
                op("pool", lambda g: g.tensor_scalar(out=dst[:, c, :], in0=wst[i][:, c, :],
                                                     scalar1=small[:, gofs + c:gofs + c + 1], scalar2=0.0,
                                                     op0=ALU.mult, op1=ALU.add),
                   R=[wst_dp[i], small_dp], W=[dst_dp], nowait=(c > 0))

        load_piece(0)
        load_piece(1)
        for p in range(10):
            i = p % 3
            wi = p % 2
            cast_piece(i, wb[wi], wb_dp[wi], 0)
            load_piece(p + 2)
            for tg in range(NG):
                tsl = slice(tg * 512, (tg + 1) * 512)
                hdeps = [d for blk_ in hT_dp[tg * 4:(tg + 1) * 4] for d in blk_]
                if p < 9:
                    pi = rr["ps"] % 4; rr["ps"] += 1
                    for c in range(NCH):
                        op("pe", lambda g: g.matmul(out=PS[pi][:], lhsT=wb[wi][:, c, :], rhs=hT[:, c, tsl],
                                                    start=(c == 0), stop=(c == NCH - 1)),
                           R=[wb_dp[wi]] + hdeps, W=[PD[pi]])
                    if p < 8:
                        oi = cnt["ofm"] % 3; cnt["ofm"] += 1
                        copy("act", ofm[oi][:], PS[pi][:], R=[PD[pi]], W=[ofm_dp[oi]])
                        dma([(QA[p, :, tsl], ofm[oi][:])], R=[ofm_dp[oi]], W=[qa_dp], st=ofm_dp[oi], q="act")
                    else:
                        copy(evac_engine(), sndt[3][:, tsl], PS[pi][:], R=[PD[pi]], W=[sn_dp[3]])
                else:
                    pis = []
                    for half in range(2):
                        pi = rr["ps"] % 4; rr["ps"] += 1
                        pis.append(pi)
                        for c in range(NCH):
                            op("pe", lambda g: g.matmul(out=PS[pi][0:64, :], lhsT=wb[wi][:, c, half * 64:(half + 1) * 64],
                                                        rhs=hT[:, c, tsl], start=(c == 0), stop=(c == NCH - 1)),
                               R=[wb_dp[wi]] + hdeps, W=[PD[pi]])
                    op("dve", lambda g: g.tensor_tensor(out=rp[0][:], in0=PS[pis[0]][0:64, :], in1=cs[:, 0, tsl], op=ALU.mult),
                       R=[PD[pis[0]], cs_dp], W=[rp_dp[0]])
                    op("dve", lambda g: g.tensor_tensor(out=rp[1][:], in0=PS[pis[1]][0:64, :], in1=cs[:, 1, tsl], op=ALU.mult),
                       R=[PD[pis[1]], cs_dp], W=[rp_dp[1]])
                    op("dve", lambda g: g.tensor_tensor(out=sndt[2][0:64, tsl], in0=rp[0][:], in1=rp[1][:], op=ALU.add),
                       R=[rp_dp[0], rp_dp[1]], W=[sn_dp[2]])
            if p == 8:
                send(3)
            if p == 9:
                send(2)

        groups = [("cqva", [10, 11, 12, 13]), ("ckv", [14, 15]), ("ga0", [16, 17, 18, 19]), ("ga1", [20, 21, 22, 23])]
        for gi, (gname, pieces) in enumerate(groups):
            wi = gi % 2
            ncol = 128 * len(pieces)
            for k, p in enumerate(pieces):
                i = p % 3
                cast_piece(i, wt[wi][:, :, k * 128:(k + 1) * 128], wt_dp[wi], 0)
                load_piece(p + 2)
            deferred = []
            for s in range(NS):
                pi = rr["ps"] % 4; rr["ps"] += 1
                for c in range(NCH):
                    op("pe", lambda g: g.matmul(out=PS[pi][:, 0:ncol], lhsT=hT[:, c, s * 128:(s + 1) * 128],
                                                rhs=wt[wi][:, c, 0:ncol], start=(c == 0), stop=(c == NCH - 1)),
                       R=[wt_dp[wi]] + hT_dp[s], W=[PD[pi]])
                for f in deferred:
                    f()
                deferred = []
                if gname in ("cqva", "ckv"):
                    nr = QL if gname == "cqva" else KVL
                    nck = nr // 128
                    op("act", lambda g: g.activation(out=cn[:, 0:nr], in_=PS[pi][:, 0:nr], func=AF.Square,
                                                     accum_out=st2[:, 0:1]),
                       R=[PD[pi]], W=[cn_dp, st2_dp])
                    op("dve", lambda g: g.tensor_scalar(out=st2[:, 1:2], in0=st2[:, 0:1], scalar1=1.0 / nr,
                                                        scalar2=EPS, op0=ALU.mult, op1=ALU.add), W=[st2_dp])
                    op("act", lambda g: g.activation(out=st2[:, 2:3], in_=st2[:, 1:2], func=AF.Sqrt), W=[st2_dp])
                    op("dve", lambda g: g.reciprocal(out=st2[:, 3:4], in_=st2[:, 2:3]), W=[st2_dp])
                    cb = s % 2
                    op("dve", lambda g: g.tensor_scalar(out=cnb[cb][:, 0:nr], in0=PS[pi][:, 0:nr], scalar1=st2[:, 3:4],
                                                        scalar2=None, op0=ALU.mult),
                       R=[PD[pi], st2_dp], W=[cnb_dp[cb]])
                    if gname == "cqva":
                        copy("act", sndt[4][:, s * 128:(s + 1) * 128], PS[pi][:, 384:512],
                             R=[PD[pi]], W=[sn_dp[4]])
                    def tr(s=s, cb=cb, nck=nck, gname=gname):
                        pj = 4 + (s % 2)
                        pv = bf(PS[pj][:])[:, 0:128 * nck].rearrange("p (a b) -> p a b", a=nck)
                        for a in range(nck):
                            op("pe", lambda g: g.transpose(out=pv[:, a, :], in_=cnb[cb][:, a * 128:(a + 1) * 128], identity=ident[:]),
                               R=[cnb_dp[cb], ident_dp], W=[PD[pj]])
                        if gname == "cqva":
                            copy("dve", cqnT[:, :, s * 128:(s + 1) * 128], pv, R=[PD[pj]], W=[cqnT_dp[s]])
                        else:
                            for a in range(2):
                                copy("dve", sndt[a][:, s * 128:(s + 1) * 128], pv[:, a, :], R=[PD[pj]], W=[sn_dp[a]])
                    deferred.append(tr)
                else:
                    oi = cnt["otm"] % 3; cnt["otm"] += 1
                    op("act", lambda g: g.activation(out=otm[oi][:], in_=PS[pi][:], func=AF.Silu),
                       R=[PD[pi]], W=[otm_dp[oi]])
                    col = (gi - 2) * 512
                    dma([(GT[s * 128:(s + 1) * 128, col:col + 512], otm[oi][:])], R=[otm_dp[oi]], W=[gt_dp[s]],
                        st=otm_dp[oi], q="act")
            for f in deferred:
                f()
            deferred = []
            if gname == "cqva":
                send(4)
            if gname == "ckv":
                send(0)
                send(1)
        for p in range(24, 32):
            i = p % 3
            wi = p % 2
            cast_piece(i, wb[wi], wb_dp[wi], 0)
            load_piece(p + 2)
            for tg in range(NG):
                tsl = slice(tg * 512, (tg + 1) * 512)
                hdeps = [d for blk_ in hT_dp[tg * 4:(tg + 1) * 4] for d in blk_]
                pi = rr["ps"] % 4; rr["ps"] += 1
                for c in range(NCH):
                    op("pe", lambda g: g.matmul(out=PS[pi][:], lhsT=wb[wi][:, c, :], rhs=hT[:, c, tsl],
                                                start=(c == 0), stop=(c == NCH - 1)),
                       R=[wb_dp[wi]] + hdeps, W=[PD[pi]])
                oi = cnt["ofm"] % 3; cnt["ofm"] += 1
                op("act", lambda g: g.activation(out=ofm[oi][:], in_=PS[pi][:], func=AF.Silu),
                   R=[PD[pi]], W=[ofm_dp[oi]])
                dma([(GTB[p - 24, :, tsl], ofm[oi][:])], R=[ofm_dp[oi]], W=[gtb_dp], st=ofm_dp[oi], q="act")
        p2.close()
        pa.close()

        pY = Ph(kb)
        yta = pY.sb("yta", [128, NCH, T], BF16); yta_dp = [pY.dep() for _ in range(NCH)]
        pS = Ph(kb)
        qall = pS.sb("qall", [128, 8, T], BF16); qall_dp = pS.dep()
        kaT = pS.sb("kaT", [128, 2, T], BF16); kaT_dp = pS.dep()
        vraw = pS.sb("vraw", [128, 2, T], BF16); vraw_dp = pS.dep()
        vaug = pS.sb("vaug", [128, 2 * NS * 2, 66], BF16); vaug_dp = pS.dep()
        etab = pS.sb("etab", [128, 3, 16, 128], BF16); etab_dp = pS.dep()
        gat = [pS.sb("gat%d" % i, [128, 1024], BF16) for i in range(2)]; gat_dp = [pS.dep() for _ in range(2)]
        pex = [pS.sb("pex%d" % i, [128, 512], BF16) for i in range(3)]; pex_dp = [pS.dep() for _ in range(3)]
        ptS = [pS.sb("ptS%d" % i, [128, 512], BF16) for i in range(6)]; ptS_dp = [pS.dep() for _ in range(6)]
        ysw = [pS.sb("ysw%d" % i, [128, 1024], BF16) for i in range(2)]; ysw_dp = [pS.dep() for _ in range(2)]
        lS = pS.sb("lS", [128, 8], F32); lS_dp = pS.dep()
        dma([(etab[:], etab_d)], W=[etab_dp], st=etab_dp)
        dma([(qall[:, j, :], QA[j]) for j in range(8)], R=[qa_dp], W=[qall_dp], st=qall_dp)
        dma([(kaT[:, r, :], RCV[3][r * 128:(r + 1) * 128, :]) for r in range(2)], R=[rcv_dp[3]], W=[kaT_dp], st=kaT_dp)
        dma([(vraw[:, r, :], RCV[4][r * 128:(r + 1) * 128, :]) for r in range(2)], R=[rcv_dp[4]], W=[vraw_dp], st=vraw_dp)
        op("pool", lambda g: g.memset(vaug[:], 1.0), W=[vaug_dp])
        for r in range(2):
            dst = vaug[:, r * NS * 2:(r + 1) * NS * 2, 0:64]
            src = vraw[:, r, :].rearrange("p (a d) -> p a d", d=64)
            op("pool", lambda g: g.tensor_copy(out=dst, in_=src), R=[vraw_dp], W=[vaug_dp])
        cS = {"pex": 0, "pt": 0}
        units = [(s, gk, quad) for s in range(NS) for gk in range(2) for quad in range(2)]
        s1out = {}

        def stage1(u):
            s, gk, quad = units[u]
            b = s % 2
            if gk == 0 and quad == 0:
                dma([(gat[b][:], GT[s * 128:(s + 1) * 128, 0:1024])], R=[gt_dp[s]], W=[gat_dp[b]], st=gat_dp[b])
            cands = [(0, 1, s - 1), (1, 0, s), (2, 1, s)]
            if s == 0:
                cands = cands[1:]
            prt = slice(gk * 64, (gk + 1) * 64)
            h0 = gk * 8 + quad * 4
            pts = []
            for (ci, r, ks) in cands:
                pi = rr["ps"] % 2; rr["ps"] += 1
                op("pe", lambda g: g.matmul(out=PS[pi][:], lhsT=kaT[prt, r, ks * 128:(ks + 1) * 128],
                                            rhs=qall[prt, quad * 4:(quad + 1) * 4, s * 128:(s + 1) * 128],
                                            start=True, stop=True),
                   R=[kaT_dp, qall_dp], W=[PD[pi]])
                xi = cS["pex"] % 3; cS["pex"] += 1
                op("act", lambda g: g.activation(out=pex[xi][:], in_=PS[pi][:], func=AF.Exp, scale=0.125),
                   R=[PD[pi]], W=[pex_dp[xi]])
                ti = cS["pt"] % 6; cS["pt"] += 1
                op("dve" if ti % 2 == 0 else "pool", lambda g: g.tensor_tensor(out=ptS[ti][:].rearrange("p (a b) -> p a b", a=4),
                                                    in0=pex[xi][:].rearrange("p (a b) -> p a b", a=4),
                                                    in1=etab[:, ci, h0:h0 + 4, :], op=ALU.mult),
                   R=[pex_dp[xi], etab_dp], W=[ptS_dp[ti]])
                pts.append((ti, r, ks))
            s1out[u] = pts

        def stage2(u):
            s, gk, quad = units[u]
            b = s % 2
            pts = s1out.pop(u)
            oi = 2 + gk * 2 + quad
            h0 = gk * 8 + quad * 4
            ov = PS[oi][:, 0:4 * 65].rearrange("p (a b) -> p a b", a=4)
            for hq in range(4):
                for n, (ti, r, ks) in enumerate(pts):
                    op("pe", lambda g: g.matmul(out=ov[:, hq, :], lhsT=ptS[ti][:, hq * 128:(hq + 1) * 128],
                                                rhs=vaug[:, (r * NS + ks) * 2 + gk, 0:65],
                                                start=(n == 0), stop=(n == len(pts) - 1)),
                       R=[ptS_dp[ti], vaug_dp], W=[PD[oi]])
            op("dve", lambda g: g.tensor_tensor(out=lS[:, 0:4], in0=ov[:, :, 64], in1=small[:, 40 + h0:44 + h0], op=ALU.add),
               R=[PD[oi], esink_dp], W=[lS_dp])
            op("dve", lambda g: g.reciprocal(out=lS[:, 4:8], in_=lS[:, 0:4]), W=[lS_dp])
            for hq in range(4):
                h = h0 + hq
                op("dve", lambda g: g.scalar_tensor_tensor(out=ysw[b][:, h * 64:(h + 1) * 64], in0=ov[:, hq, 0:64],
                                                            scalar=lS[:, 4 + hq:5 + hq], in1=gat[b][:, h * 64:(h + 1) * 64],
                                                            op0=ALU.mult, op1=ALU.mult),
                   R=[PD[oi], lS_dp, gat_dp[b]], W=[ysw_dp[b]], nowait=(hq > 0))
            if gk == 1 and quad == 1:
                pj = 6 + (s % 2)
                pv = bf(PS[pj][:]).rearrange("p (a b) -> p a b", a=8)
                for a in range(8):
                    op("pe", lambda g: g.transpose(out=pv[:, a, :], in_=ysw[b][:, a * 128:(a + 1) * 128], identity=ident[:]),
                       R=[ysw_dp[b], ident_dp], W=[PD[pj]])
                copy("act", yta[:, 0:8, s * 128:(s + 1) * 128], pv, R=[PD[pj]], W=yta_dp[0:8])

        stage1(0)
        for u in range(len(units)):
            if u + 1 < len(units):
                stage1(u + 1)
            stage2(u)
        pS.close()

        pC = Ph(kb)
        ck = pC.sb("ck", [128, 2, 2, T], BF16); ck_dp = pC.dep()
        kr = pC.sb("kr", [64, 2, T], BF16); kr_dp = pC.dep()
        wq = pC.sb("wq", [128, 3, 2048], BF16); wq_dp = pC.dep()
        wkv = pC.sb("wkv", [128, 2, 2048], BF16); wkv_dp = pC.dep()
        wst2 = [pC.sb("wsc%d" % i, [128, 1024], F32) for i in range(2)]; wst2_dp = [pC.dep() for _ in range(2)]
        khT = [pC.sb("khT%d" % i, [128, 2 * T], BF16) for i in range(2)]; khT_dp = [pC.dep() for _ in range(2)]
        vh = [pC.sb("vh%d" % i, [128, 2 * NS, 130], BF16) for i in range(2)]; vh_dp = [pC.dep() for _ in range(2)]
        gbt = [pC.sb("gbt%d" % i, [128, 512], BF16) for i in range(2)]; gbt_dp = [pC.dep() for _ in range(2)]
        onesb = pC.sb("onesb", [128, 128], BF16); onesb_dp = pC.dep()
        rLt = [pC.sb("rLt%d" % i, [128, 512], F32) for i in range(2)]; rLt_dp = [pC.dep() for _ in range(2)]
        op("pool", lambda g: g.memset(onesb[:], 1.0), W=[onesb_dp])
        qn = [pC.sb("qn%d" % i, [128, 512], BF16) for i in range(2)]; qn_dp = [pC.dep() for _ in range(2)]
        qr = [pC.sb("qr%d" % i, [64, 512], BF16) for i in range(2)]; qr_dp = [pC.dep() for _ in range(2)]
        rq = [pC.sb("rq%d" % i, [64, 512], F32) for i in range(2)]; rq_dp = [pC.dep() for _ in range(2)]
        ptC = [pC.sb("ptC%d" % i, [128, 512], BF16) for i in range(4)]; ptC_dp = [pC.dep() for _ in range(4)]
        for r in range(2):
            dma([(ck[:, c, r, :], RCV[c][r * 128:(r + 1) * 128, :]) for c in range(2)]
                + [(kr[:, r, :], RCV[2][r * 128:r * 128 + 64, :])],
                R=rcv_dp[0:3], W=[ck_dp, kr_dp], st=ck_dp if r == 0 else kr_dp)
        nst = 0
        for (wsrc, wdst, wdp, nchk, gofs) in ((wkv_d, wkv, wkv_dp, 2, 19), (wq_d, wq, wq_dp, 3, 16)):
            for c in range(nchk):
                for hf in range(2):
                    i = nst % 2; nst += 1
                    csl = slice(hf * 1024, (hf + 1) * 1024)
                    dma([(wst2[i][:], wsrc[l, :, c, csl])], W=[wst2_dp[i]], st=wst2_dp[i])
                    op("pool", lambda g: g.tensor_scalar(out=wdst[:, c, csl], in0=wst2[i][:], scalar1=small[:, gofs + c:gofs + c + 1],
                                                         scalar2=0.0, op0=ALU.mult, op1=ALU.add),
                       R=[wst2_dp[i], small_dp], W=[wdp])
        for i in range(2):
            op("pool", lambda g: g.memset(vh[i][:, :, 128:130], 1.0), W=[vh_dp[i]])
        accv = [wst2[0][:, 0:512], wst2[0][:, 512:1024]]
        accb_all = wst2[1][:].bitcast(BF16)
        accb = [accb_all[:, 0:512], accb_all[:, 512:1024]]
        acc_dp = [pC.dep() for _ in range(2)]; accb_dp = [pC.dep() for _ in range(2)]
        acc_first = [True, True]; accb_first = [True, True]
        acc_eng = ["dve", "pool"]
        pend_fin = []
        cC = {"pt": 0, "g": 0, "s": 0}
        scale = float((128 + 64) ** -0.5)
        NKS = 2 * NS

        def misc_bank():
            pi = 5 + rr["ps"] % 3; rr["ps"] += 1
            return pi

        def prep_chunks(h):
            hb_ = h % 2
            out = []
            for kg in range(2 * T // 512):
                def f(kg=kg):
                    r, t0 = divmod(kg * 512, T)
                    pi = misc_bank()
                    for c in range(2):
                        op("pe", lambda g: g.matmul(out=PS[pi][:], lhsT=wkv[:, c, h * 256:h * 256 + 128],
                                                    rhs=ck[:, c, r, t0:t0 + 512], start=(c == 0), stop=(c == 1)),
                           R=[wkv_dp, ck_dp], W=[PD[pi]])
                    copy("dve", khT[hb_][:, kg * 512:(kg + 1) * 512], PS[pi][:], R=[PD[pi]], W=[khT_dp[hb_]])
                out.append(f)
            for k4 in range(NKS // 4):
                def f(k4=k4):
                    pi = misc_bank()
                    for a in range(4):
                        ksl = k4 * 4 + a
                        r, s_ = divmod(ksl, NS)
                        for c in range(2):
                            op("pe", lambda g: g.matmul(out=PS[pi][:, a * 128:(a + 1) * 128],
                                                        lhsT=ck[:, c, r, s_ * 128:(s_ + 1) * 128],
                                                        rhs=wkv[:, c, h * 256 + 128:h * 256 + 256],
                                                        start=(c == 0), stop=(c == 1)),
                               R=[wkv_dp, ck_dp], W=[PD[pi]])
                    copy("dve", vh[hb_][:, k4 * 4:(k4 + 1) * 4, 0:128],
                         PS[pi][:].rearrange("p (a b) -> p a b", a=4), R=[PD[pi]], W=[vh_dp[hb_]])
                out.append(f)
            return out

        def qproj_chunks(h, G):
            gb_ = (h * NG + G) % 2
            tsl = slice(G * 512, (G + 1) * 512)
            cdeps = cqnT_dp[G * 4:(G + 1) * 4]

            def f0():
                dma([(gbt[gb_][:], GTB[h, :, tsl])], R=[gtb_dp], W=[gbt_dp[gb_]], st=gbt_dp[gb_])
                pi = misc_bank()
                for c in range(3):
                    op("pe", lambda g: g.matmul(out=PS[pi][:], lhsT=wq[:, c, h * 256:h * 256 + 128], rhs=cqnT[:, c, tsl],
                                                start=(c == 0), stop=(c == 2)),
                       R=[wq_dp] + cdeps, W=[PD[pi]])
                copy("dve", qn[gb_][:], PS[pi][:], R=[PD[pi]], W=[qn_dp[gb_]])

            def fr(half):
                pi = misc_bank()
                c0 = h * 256 + 128 + half * 64
                for c in range(3):
                    op("pe", lambda g: g.matmul(out=PS[pi][0:64, :], lhsT=wq[:, c, c0:c0 + 64], rhs=cqnT[:, c, tsl],
                                                start=(c == 0), stop=(c == 2)),
                       R=[wq_dp] + cdeps, W=[PD[pi]])
                op("dve", lambda g: g.tensor_tensor(out=rq[half][:], in0=PS[pi][0:64, :], in1=cs[:, half, tsl], op=ALU.mult),
                   R=[PD[pi], cs_dp], W=[rq_dp[half]])
                if half == 1:
                    op("dve", lambda g: g.tensor_tensor(out=qr[gb_][:], in0=rq[0][:], in1=rq[1][:], op=ALU.add),
                       R=[rq_dp[0], rq_dp[1]], W=[qr_dp[gb_]])
            return [f0, lambda: fr(0), lambda: fr(1)]

        qq = []
        pq = []
        for f in prep_chunks(0):
            f()
        for f in qproj_chunks(0, 0):
            f()
        for h in range(8):
            hb_ = h % 2
            if h + 1 < 8:
                pq = prep_chunks(h + 1)
            for G in range(NG):
                gb_ = (h * NG + G) % 2
                tsl = slice(G * 512, (G + 1) * 512)
                if G + 1 < NG:
                    qq = qproj_chunks(h, G + 1)
                elif h + 1 < 8:
                    qq = qproj_chunks(h + 1, 0)
                ents = [(r, s_, 0, False) for s_ in range(4 * G) for r in range(2)]
                ents += [(r, 4 * G + j, j, True) for j in range(4) for r in range(2)]
                NE = len(ents)
                st_ = {}

                def emit_S(n):
                    r, s_, jmin, msk = ents[n]
                    ncol = (4 - jmin) * 128
                    pi = cC["s"] % 3; cC["s"] += 1
                    kcol = r * T + s_ * 128
                    op("pe", lambda g: g.matmul(out=PS[pi][:, 0:ncol], lhsT=khT[hb_][:, kcol:kcol + 128],
                                                rhs=qn[gb_][:, jmin * 128:512], start=True, stop=False),
                       R=[khT_dp[hb_], qn_dp[gb_]], W=[PD[pi]])
                    op("pe", lambda g: g.matmul(out=PS[pi][:, 0:ncol], lhsT=kr[:, r, s_ * 128:(s_ + 1) * 128],
                                                rhs=qr[gb_][:, jmin * 128:512], start=False, stop=True),
                       R=[kr_dp, qr_dp[gb_]], W=[PD[pi]])
                    ti = cC["pt"] % 4; cC["pt"] += 1
                    op("act", lambda g: g.activation(out=ptC[ti][:, 0:ncol], in_=PS[pi][:, 0:ncol], func=AF.Exp, scale=scale),
                       R=[PD[pi]], W=[ptC_dp[ti]])
                    if msk:
                        op("dve", lambda g: g.tensor_tensor(out=ptC[ti][:, 0:128], in0=ptC[ti][:, 0:128], in1=mask[:, r, :], op=ALU.mult),
                           R=[mask_dp], W=[ptC_dp[ti]])
                    st_[n] = ti

                bO = 3 + gb_

                def emit_PV(n):
                    r, s_, jmin, msk = ents[n]
                    ti = st_[n]
                    ncol = (4 - jmin) * 128
                    op("pe", lambda g: g.matmul(out=PS[bO][:, jmin * 128:512], lhsT=vh[hb_][:, r * NS + s_, 0:128],
                                                rhs=ptC[ti][:, 0:ncol], start=(n == 0), stop=(n == NE - 1)),
                       R=[ptC_dp[ti], vh_dp[hb_]], W=[PD[bO]])
                    a_ = n % 2
                    if n < 2:
                        extra = [wst2_dp[0]] if acc_first[a_] else []
                        acc_first[a_] = False
                        op(acc_eng[a_], lambda g: g.tensor_copy(out=accv[a_], in_=ptC[ti][:, 0:512]),
                           R=[ptC_dp[ti]], W=[acc_dp[a_]] + extra)
                    else:
                        op(acc_eng[a_], lambda g: g.tensor_tensor(out=accv[a_][:, jmin * 128:512], in0=accv[a_][:, jmin * 128:512],
                                                                  in1=ptC[ti][:, 0:ncol], op=ALU.add),
                           R=[ptC_dp[ti]], W=[acc_dp[a_]])

                emit_S(0)
                emit_S(1)
                while pend_fin:
                    pend_fin.pop(0)()
                for n in range(NE):
                    if n + 2 < NE:
                        emit_S(n + 2)
                    emit_PV(n)
                    if qq:
                        qq.pop(0)()
                    elif pq:
                        pq.pop(0)()
                while qq:
                    qq.pop(0)()
                for a_ in range(2):
                    extra = [wst2_dp[1]] if accb_first[a_] else []
                    accb_first[a_] = False
                    op(acc_eng[a_], lambda g: g.tensor_copy(out=accb[a_], in_=accv[a_]),
                       R=[acc_dp[a_]], W=[accb_dp[a_]] + extra)

                def fin(h=h, gb_=gb_, tsl=tsl, bO=bO):
                    pi = misc_bank()
                    for a_ in range(2):
                        op("pe", lambda g: g.matmul(out=PS[pi][:], lhsT=onesb[:], rhs=accb[a_],
                                                    start=(a_ == 0), stop=(a_ == 1)),
                           R=[accb_dp[a_], onesb_dp], W=[PD[pi]])
                    op("dve", lambda g: g.reciprocal(out=rLt[gb_][:], in_=PS[pi][:]), R=[PD[pi]], W=[rLt_dp[gb_]])
                    op("dve", lambda g: g.tensor_tensor(out=rLt[gb_][:], in0=PS[bO][:], in1=rLt[gb_][:], op=ALU.mult),
                       R=[PD[bO]], W=[rLt_dp[gb_]])
                    op("dve", lambda g: g.tensor_tensor(out=yta[:, 8 + h, tsl], in0=rLt[gb_][:], in1=gbt[gb_][:], op=ALU.mult),
                       R=[rLt_dp[gb_], gbt_dp[gb_]], W=[yta_dp[8 + h]])
                pend_fin.append(fin)
            while pq:
                pq.pop(0)()
        while pend_fin:
            pend_fin.pop(0)()
        pC.close()

        pD = Ph(kb)
        wsd = [pD.sb("wsd%d" % i, [128, NCH, 128], F32) for i in range(2)]; wsd_dp = [pD.dep() for _ in range(2)]
        wo = [pD.sb("wo%d" % i, [128, NCH, 512], BF16) for i in range(2)]; wo_dp = [pD.dep() for _ in range(2)]
        xr = [pD.sb("xr%d" % i, [128, 512], F32) for i in range(3)]; xr_dp = [pD.dep() for _ in range(3)]
        xo = [pD.sb("xo%d" % i, [128, 512], F32) for i in range(3)]; xo_dp = [pD.dep() for _ in range(3)]
        cD = {"w": 0, "x": 0}
        def load_wo(cg):
            wi = cg % 2
            for k in range(4):
                i = cD["w"] % 2; cD["w"] += 1
                dma([(wsd[i][:], wo_d[l, cg * 4 + k])], W=[wsd_dp[i]], st=wsd_dp[i])
                op("pool", lambda g: g.tensor_copy(out=wo[wi][:, :, k * 128:(k + 1) * 128], in_=wsd[i][:]),
                   R=[wsd_dp[i]], W=[wo_dp[wi]])

        load_wo(0)
        for cg in range(4):
            wi = cg % 2
            if cg + 1 < 4:
                load_wo(cg + 1)
            for s in range(NS):
                xi = cD["x"] % 3; cD["x"] += 1
                dma([(xr[xi][:], x_src[s * 128:(s + 1) * 128, cg * 512:(cg + 1) * 512])], R=[x_src_dp[s]], W=[xr_dp[xi]], st=xr_dp[xi])
                pi = rr["ps"] % 4; rr["ps"] += 1
                for c in range(NCH):
                    op("pe", lambda g: g.matmul(out=PS[pi][:], lhsT=yta[:, c, s * 128:(s + 1) * 128], rhs=wo[wi][:, c, :],
                                                start=(c == 0), stop=(c == NCH - 1)),
                       R=yta_dp + [wo_dp[wi]], W=[PD[pi]])
                op("dve", lambda g: g.tensor_tensor(out=xo[xi][:], in0=PS[pi][:], in1=xr[xi][:], op=ALU.add),
                   R=[PD[pi], xr_dp[xi]], W=[xo_dp[xi]])
                dma([(x_dst[s * 128:(s + 1) * 128, cg * 512:(cg + 1) * 512], xo[xi][:])], R=[xo_dp[xi]], W=[x_dst_dp[s]], st=xo_dp[xi], q="act")
        pD.close()
        pY.close()

    pF = Ph(kb)
    x_src = XS[(L - 1) % 2]
    x_src_dp = x_dp[1 + (L - 1) % 2]
    gfin = pF.sb("gfin", [128, D], F32); gfin_dp = pF.dep()
    xs = [pF.sb("fx%d" % i, [128, D], F32) for i in range(2)]; xs_dp = [pF.dep() for _ in range(2)]
    fo = [pF.sb("fo%d" % i, [128, D], F32) for i in range(2)]; fo_dp = [pF.dep() for _ in range(2)]
    junk = pF.sb("fjunk", [128, D], BF16); junk_dp = pF.dep()
    st1 = pF.sb("fst", [128, 4 * NS], F32); st1_dp = [pF.dep() for _ in range(NS)]
    out_dp = Dep()
    dma([(gfin[:], gfin_d)], W=[gfin_dp], st=gfin_dp)
    for s in range(NS):
        b = s % 2
        dma([(xs[b][:], x_src[s * 128:(s + 1) * 128, :])], R=[x_src_dp[s]], W=[xs_dp[b]], st=xs_dp[b])
        c0 = 4 * s
        op("act", lambda g: g.activation(out=junk[:], in_=xs[b][:], func=AF.Square, accum_out=st1[:, c0:c0 + 1]),
           R=[xs_dp[b]], W=[junk_dp, st1_dp[s]])
        op("dve", lambda g: g.tensor_scalar(out=st1[:, c0 + 1:c0 + 2], in0=st1[:, c0:c0 + 1], scalar1=1.0 / D, scalar2=EPS,
                                            op0=ALU.mult, op1=ALU.add), W=[st1_dp[s]])
        op("act", lambda g: g.activation(out=st1[:, c0 + 2:c0 + 3], in_=st1[:, c0 + 1:c0 + 2], func=AF.Sqrt), W=[st1_dp[s]])
        op("dve", lambda g: g.reciprocal(out=st1[:, c0 + 3:c0 + 4], in_=st1[:, c0 + 2:c0 + 3]), W=[st1_dp[s]])
        op("dve", lambda g: g.scalar_tensor_tensor(out=fo[b][:], in0=xs[b][:], scalar=st1[:, c0 + 3:c0 + 4], in1=gfin[:],
                                                    op0=ALU.mult, op1=ALU.mult),
           R=[xs_dp[b], st1_dp[s], gfin_dp], W=[fo_dp[b]])
        dma([(out_d[s * 128:(s + 1) * 128, :], fo[b][:])], R=[fo_dp[b]], W=[out_dp], st=fo_dp[b], q="act")
    pF.close()
    gp.es.close()
    es.close()
    return nc


def _host_inputs(x, attn_norm_g, w_in, swa_sinks, q_a_norm_g, kv_a_norm_g, w_q_b, w_kv_b, w_out, final_norm_g, NS):
    L = w_in.shape[0]
    B = x.shape[0]
    T = NS * 128
    f32 = np.float32
    A_Q, A_KV, A_G = 1024, 128, 1024
    o_qa, o_ka, o_va, o_ga = 0, A_Q, A_Q + A_KV, A_Q + 2 * A_KV
    o_cq = o_ga + A_G
    o_ckv = o_cq + QL
    o_kr = o_ckv + KVL
    o_gb = o_kr + 64
    qa_cols = []
    for j in range(8):
        for half in range(2):
            hd = j + 8 * half
            qa_cols += list(range(o_qa + hd * 64, o_qa + (hd + 1) * 64))
    kr_cols = list(range(o_kr, o_kr + 64))
    kr_sw = list(range(o_kr + 32, o_kr + 64)) + list(range(o_kr, o_kr + 32))
    cols = (qa_cols + list(range(o_ka, o_ka + 128)) + kr_cols + kr_sw + list(range(o_cq, o_cq + QL))
            + list(range(o_va, o_va + 128)) + list(range(o_ckv, o_ckv + KVL)) + list(range(o_ga, o_ga + A_G))
            + list(range(o_gb, o_gb + 1024)))
    cols = np.asarray(cols)
    assert cols.size == NPIECE * 128
    wp = w_in[:, :, cols]
    win = np.ascontiguousarray(wp.reshape(L, NCH, 128, NPIECE, 128).transpose(0, 3, 2, 1, 4))
    qcols = []
    for h in range(8):
        b0 = h * 192
        qcols += list(range(b0, b0 + 192)) + list(range(b0 + 160, b0 + 192)) + list(range(b0 + 128, b0 + 160))
    wqp = w_q_b[:, :, np.asarray(qcols)]
    wq = np.ascontiguousarray(wqp.reshape(L, 3, 128, 2048).transpose(0, 2, 1, 3))
    wkv = np.ascontiguousarray(w_kv_b.reshape(L, 2, 128, 2048).transpose(0, 2, 1, 3))
    wo = np.ascontiguousarray(w_out.reshape(L, NCH, 128, 16, 128).transpose(0, 3, 2, 1, 4))
    gin = np.ascontiguousarray(attn_norm_g.reshape(L, NCH, 128).transpose(0, 2, 1))
    gq = np.ascontiguousarray(q_a_norm_g.reshape(L, 3, 128).transpose(0, 2, 1))
    gkv = np.ascontiguousarray(kv_a_norm_g.reshape(L, 2, 128).transpose(0, 2, 1))
    sinks = np.ascontiguousarray(np.broadcast_to(swa_sinks[:, None, :], (L, 128, 16)))
    gfin = np.ascontiguousarray(np.broadcast_to(final_norm_g[None, :], (128, D)))
    ident = np.eye(128, dtype=f32).astype(ml_dtypes.bfloat16)
    S_full = 2 * T
    pos = np.arange(S_full, dtype=f32)
    inv_freq = (10000.0 ** (-np.arange(0, 64, 2, dtype=f32) / 64)).astype(f32)
    ang = pos[:, None] * inv_freq[None, :]
    cos, sin = np.cos(ang).astype(f32), np.sin(ang).astype(f32)
    cos2 = np.concatenate([cos, cos], 1).T
    sin2 = np.concatenate([-sin, sin], 1).T
    slopes = np.exp2(-8.0 * np.arange(1, 17, dtype=f32) / 16).astype(f32)
    kk = np.arange(128)[:, None]
    qq = np.arange(128)[None, :]
    d_prev = (128 + qq - kk).astype(f32)
    d_cur = (qq - kk).astype(f32)
    E_prev = np.where((kk > qq)[:, None, :], np.exp(-slopes[None, :, None] * np.clip(d_prev, 0, 128)[:, None, :]), 0.0).astype(f32)
    E_cur = np.where((kk <= qq)[:, None, :], np.exp(-slopes[None, :, None] * np.clip(d_cur, 0, 128)[:, None, :]), 0.0).astype(f32)
    Z = np.zeros_like(E_prev)
    tri = (kk <= qq).astype(f32)
    ones = np.ones_like(tri)
    zer = np.zeros_like(tri)
    in_maps = []
    for c in range(2 * B):
        b, r = divmod(c, 2)
        xb = x[b].reshape(2 * NS, 128, D)[r::2].reshape(T, D)
        tok = (np.arange(NS)[:, None] * 2 + r) * 128 + np.arange(128)[None, :]
        tok = tok.reshape(-1)
        cs = np.ascontiguousarray(np.stack([cos2[:, tok], sin2[:, tok]], 0))
        if r == 0:
            et = np.stack([E_prev, E_cur, Z], 1)
            mk = np.stack([tri, zer], 1)
        else:
            et = np.stack([Z, E_prev, E_cur], 1)
            mk = np.stack([ones, tri], 1)
        in_maps.append({
            "x": np.ascontiguousarray(xb), "win": win, "wq": wq, "wkv": wkv, "wo": wo, "gin": gin, "gq": gq,
            "gkv": gkv, "sinks": sinks, "gfin": gfin, "cs": cs.astype(f32),
            "etab": np.ascontiguousarray(et).astype(ml_dtypes.bfloat16),
            "mask": np.ascontiguousarray(mk).astype(ml_dtypes.bfloat16), "ident": ident,
        })
    return in_maps


def _assemble(results, B, NS):
    T = NS * 128
    out = np.empty((B, 2 * NS, 128, D), np.float32)
    for c in range(2 * B):
        b, r = divmod(c, 2)
        out[b, r::2] = np.asarray(results[c]["out"]).reshape(NS, 128, D)
    return out.reshape(B, 2 * T, D)


_NC_CACHE = {}


def kernel(x, attn_norm_g, w_in, swa_sinks, q_a_norm_g, kv_a_norm_g, w_q_b, w_kv_b, w_out, final_norm_g):
    args = [np.asarray(a, dtype=np.float32) for a in (x, attn_norm_g, w_in, swa_sinks, q_a_norm_g, kv_a_norm_g,
                                                      w_q_b, w_kv_b, w_out, final_norm_g)]
    B, S = args[0].shape[0], args[0].shape[1]
    NS = S // 256
    L = args[2].shape[0]
    in_maps = _host_inputs(*args, NS=NS)
    key = (NS, L)
    if key not in _NC_CACHE:
        _NC_CACHE[key] = build(NS, L)
    res = run_bass_kernel_spmd(_NC_CACHE[key], in_maps, core_ids=list(range(2 * B)))
    return _assemble(res.results, B, NS)
```

```python
import contextlib
import numpy as np
import ml_dtypes
import concourse.bass as bass
import concourse.mybir as mybir
from concourse.bass_utils import run_bass_kernel_spmd

F32 = mybir.dt.float32
BF16 = mybir.dt.bfloat16
AF = mybir.ActivationFunctionType
ALU = mybir.AluOpType

D = 2048
NCH = 16
EPS = 1e-6
QL = 384
KVL = 256
NPIECE = 32
PAIRS = [[0, 1], [2, 3], [4, 5], [6, 7]]
EMBED_WAIT = True


class Dep:
    __slots__ = ("w", "r", "ds")

    def __init__(self):
        self.w = None
        self.r = {}
        self.ds = None


class KB:
    def __init__(self, nc, es, n_dsem=70):
        self.nc = nc
        self.eng = {"pe": nc.tensor, "act": nc.scalar, "dve": nc.vector,
                    "pool": nc.gpsimd, "sp": nc.sync}
        self.sem = {e: es.enter_context(nc.semaphore("s_" + e)) for e in ("pe", "act", "dve", "pool")}
        self.cnt = {e: 0 for e in self.sem}
        self.known = {e: {} for e in self.eng}
        self.dsems = [[es.enter_context(nc.semaphore("d%d" % i)), 0] for i in range(n_dsem)]
        self.dfree = list(range(n_dsem))
        self.ccsem = es.enter_context(nc.semaphore("ccs"))
        self.cccnt = 0

    def _wait(self, e, ev):
        if ev is None:
            return
        s, v = ev
        k = self.known[e]
        if k.get(id(s), 0) >= v:
            return
        if e == "pe" and s is self.sem["pe"]:
            return
        k[id(s)] = v
        p = self.pend
        if id(s) in p:
            p[id(s)] = (s, max(v, p[id(s)][1]))
        else:
            p[id(s)] = (s, v)

    def _deps(self, e, R, W):
        self.pend = {}
        for d in R:
            self._wait(e, d.w)
        for d in W:
            self._wait(e, d.w)
            for ev in d.r.values():
                self._wait(e, ev)
        return list(self.pend.values())

    def _emit_waits(self, e, waits, keep_last):
        n = len(waits) - (1 if keep_last and waits else 0)
        for (s, v) in waits[:n]:
            self.eng[e].wait_ge(s, v)
        return waits[n:]

    def op(self, e, fn, R=(), W=(), nowait=False):
        rest = []
        if not nowait:
            rest = self._emit_waits(e, self._deps(e, R, W), EMBED_WAIT)
        ins = fn(self.eng[e])
        for (s_, v_) in rest:
            ins._wait_ge(s_, v_)
        self.cnt[e] += 1
        ins.then_inc(self.sem[e], 1)
        ev = (self.sem[e], self.cnt[e])
        for d in R:
            d.r[e] = ev
        for d in W:
            d.w = ev
            d.r = {}
        return ev

    def dsem_alloc(self, dep):
        dep.ds = self.dfree.pop()
        return dep

    def dsem_free(self, dep):
        if dep.ds is not None:
            self.dfree.append(dep.ds)
            dep.ds = None

    def dma(self, pairs, R=(), W=(), st=None, q="sp"):
        if st.ds is None:
            self.dsem_alloc(st)
        self._emit_waits(q, self._deps(q, R, W), False)
        slot = self.dsems[st.ds]
        for (o, i) in pairs:
            ins = self.eng[q].dma_start(out=o, in_=i)
            slot[1] += 16
            ins.then_inc(slot[0], 16)
        ev = (slot[0], slot[1])
        for d in R:
            d.r[("d", st.ds)] = ev
        for d in W:
            d.w = ev
            d.r = {}
        return ev

    def allgather(self, src, dst, R=(), W=()):
        self._emit_waits("pool", self._deps("pool", R, W), False)
        ins = self.nc.gpsimd.collective_compute("AllGather", ALU.bypass, replica_groups=PAIRS,
                                                ins=[src], outs=[dst])
        self.cccnt += 1
        ins.then_inc(self.ccsem, 1)
        ev = (self.ccsem, self.cccnt)
        for d in R:
            d.r["cc"] = ev
        for d in W:
            d.w = ev
            d.r = {}

    def barrier(self):
        evs = [(self.sem[e], self.cnt[e]) for e in self.sem if self.cnt[e] > 0]
        evs += [(s, c) for (s, c) in self.dsems if c > 0]
        if self.cccnt:
            evs.append((self.ccsem, self.cccnt))
        for e in self.eng:
            self.pend = {}
            for ev in evs:
                self._wait(e, ev)
            self._emit_waits(e, list(self.pend.values()), False)


class Ph:
    uid = 0

    def __init__(self, kb):
        self.kb = kb
        self.es = contextlib.ExitStack()
        self.deps = []

    def sb(self, name, shape, dt):
        Ph.uid += 1
        t = self.es.enter_context(self.kb.nc.sbuf_tensor("sb%d_%s" % (Ph.uid, name), list(shape), dt))
        return t

    def dep(self):
        d = Dep()
        self.deps.append(d)
        return d

    def close(self):
        self.kb.barrier()
        for d in self.deps:
            self.kb.dsem_free(d)
        self.es.close()


def build(NS=16, L=4):
    T = NS * 128
    NG = NS // 4
    nc = bass.Bass("TRN2", target_bir_lowering=False)
    dt = nc.dram_tensor
    x_in = dt("x", [T, D], F32, kind="ExternalInput").ap()
    win = dt("win", [L, NPIECE, 128, NCH, 128], F32, kind="ExternalInput").ap()
    wq_d = dt("wq", [L, 128, 3, 2048], F32, kind="ExternalInput").ap()
    wkv_d = dt("wkv", [L, 128, 2, 2048], F32, kind="ExternalInput").ap()
    wo_d = dt("wo", [L, 16, 128, NCH, 128], F32, kind="ExternalInput").ap()
    gin_d = dt("gin", [L, 128, NCH], F32, kind="ExternalInput").ap()
    gq_d = dt("gq", [L, 128, 3], F32, kind="ExternalInput").ap()
    gkv_d = dt("gkv", [L, 128, 2], F32, kind="ExternalInput").ap()
    sink_d = dt("sinks", [L, 128, 16], F32, kind="ExternalInput").ap()
    gfin_d = dt("gfin", [128, D], F32, kind="ExternalInput").ap()
    cs_d = dt("cs", [2, 64, T], F32, kind="ExternalInput").ap()
    etab_d = dt("etab", [128, 3, 16, 128], BF16, kind="ExternalInput").ap()
    mask_d = dt("mask", [128, 2, 128], BF16, kind="ExternalInput").ap()
    ident_d = dt("ident", [128, 128], BF16, kind="ExternalInput").ap()
    out_d = dt("out", [T, D], F32, kind="ExternalOutput").ap()
    QA = dt("QA", [8, 128, T], BF16, kind="Internal").ap()
    GT = dt("GT", [T, 2048], BF16, kind="Internal").ap()
    GTB = dt("GTB", [8, 128, T], BF16, kind="Internal").ap()
    XS = [dt("X%d" % i, [T, D], F32, kind="Internal").ap() for i in range(2)]
    SND = [dt("SND%d" % k, [128, T], BF16).ap() for k in range(5)]
    RCV = [dt("RCV%d" % k, [256, T], BF16).ap() for k in range(5)]
    O_CK, O_KR, O_KA, O_VA = 0, 2 * T, 3 * T, 4 * T

    es = contextlib.ExitStack()
    kb = KB(nc, es)
    op, dma = kb.op, kb.dma

    PS = [es.enter_context(nc.psum_tensor("ps%d" % i, [128, 512], F32)) for i in range(8)]
    PD = [Dep() for _ in range(8)]
    gp = Ph(kb)
    ident = gp.sb("ident", [128, 128], BF16); ident_dp = gp.dep()
    cs = gp.sb("cs", [64, 2, T], F32); cs_dp = gp.dep()
    mask = gp.sb("mask", [128, 2, 128], BF16); mask_dp = gp.dep()
    cqnT = gp.sb("cqnT", [128, 3, T], BF16); cqnT_dp = [gp.dep() for _ in range(NS)]
    small = gp.sb("small", [128, 64], F32)
    small_dp = gp.dep()
    esink_dp = gp.dep()
    dma([(ident[:], ident_d)], W=[ident_dp], st=ident_dp)
    dma([(cs[:, 0, :], cs_d[0]), (cs[:, 1, :], cs_d[1])], W=[cs_dp], st=cs_dp)
    dma([(mask[:], mask_d)], W=[mask_dp], st=mask_dp)

    x_dp = [[Dep() for _ in range(NS)] for _ in range(3)]
    qa_dp = Dep(); gt_dp = [Dep() for _ in range(NS)]; yt_dp = Dep()
    snd_dp = [Dep() for _ in range(5)]; rcv_dp = [Dep() for _ in range(5)]
    gtb_dp = Dep()

    def bf(ps_ap):
        return ps_ap.bitcast(BF16)

    rr = {"ps": 0, "ev": 0}

    def evac_engine():
        rr["ev"] += 1
        return "act" if rr["ev"] % 2 else "dve"

    def copy(e, out, in_, R, W):
        if e == "act":
            return op("act", lambda g: g.copy(out=out, in_=in_), R=R, W=W)
        return op(e, lambda g: g.tensor_copy(out=out, in_=in_), R=R, W=W)

    for l in range(L):
        x_src = x_in if l == 0 else XS[(l - 1) % 2]
        x_src_dp = x_dp[0] if l == 0 else x_dp[1 + (l - 1) % 2]
        x_dst = XS[l % 2]
        x_dst_dp = x_dp[1 + l % 2]

        dma([(small[:, 0:16], gin_d[l]), (small[:, 16:19], gq_d[l]), (small[:, 19:21], gkv_d[l]),
             (small[:, 24:40], sink_d[l])], W=[small_dp], st=small_dp)
        op("act", lambda g: g.activation(out=small[:, 40:56], in_=small[:, 24:40], func=AF.Exp),
           R=[small_dp], W=[esink_dp])

        pa = Ph(kb)
        hT = pa.sb("hT", [128, NCH, T], BF16)
        hT_dp = [[pa.dep() for _ in range(4)] for _ in range(NS)]
        p1 = Ph(kb)
        xs = [p1.sb("xs%d" % i, [128, D], F32) for i in range(3)]; xs_dp = [p1.dep() for _ in range(3)]
        hb = [p1.sb("hb%d" % i, [128, D], BF16) for i in range(2)]; hb_dp = [p1.dep() for _ in range(2)]
        junk = p1.sb("junk", [128, D], BF16); junk_dp = p1.dep()
        st1 = p1.sb("st1", [128, 4 * NS], F32); st1_dp = [p1.dep() for _ in range(NS)]
        for s in range(NS):
            b = s % 2
            xb = s % 3
            dma([(xs[xb][:], x_src[s * 128:(s + 1) * 128, :])], R=[x_src_dp[s]], W=[xs_dp[xb]], st=xs_dp[xb])
            c0 = 4 * s
            op("act", lambda g: g.activation(out=junk[:], in_=xs[xb][:], func=AF.Square,
                                             accum_out=st1[:, c0:c0 + 1]),
               R=[xs_dp[xb]], W=[junk_dp, st1_dp[s]])
            op("dve", lambda g: g.tensor_scalar(out=st1[:, c0 + 1:c0 + 2], in0=st1[:, c0:c0 + 1],
                                                scalar1=1.0 / D, scalar2=EPS, op0=ALU.mult, op1=ALU.add),
               R=[], W=[st1_dp[s]])
            op("act", lambda g: g.activation(out=st1[:, c0 + 2:c0 + 3], in_=st1[:, c0 + 1:c0 + 2], func=AF.Sqrt),
               W=[st1_dp[s]])
            op("dve", lambda g: g.reciprocal(out=st1[:, c0 + 3:c0 + 4], in_=st1[:, c0 + 2:c0 + 3]),
               W=[st1_dp[s]])
            op("dve", lambda g: g.tensor_scalar(out=hb[b][:], in0=xs[xb][:], scalar1=st1[:, c0 + 3:c0 + 4],
                                                scalar2=None, op0=ALU.mult),
               R=[xs_dp[xb], st1_dp[s]], W=[hb_dp[b]])
            for cg in range(4):
                pi = rr["ps"] % 4; rr["ps"] += 1
                pv = bf(PS[pi][:])[:, 0:512].rearrange("p (a b) -> p a b", a=4)
                for a in range(4):
                    c = cg * 4 + a
                    op("pe", lambda g: g.transpose(out=pv[:, a, :], in_=hb[b][:, c * 128:(c + 1) * 128],
                                                   identity=ident[:]),
                       R=[hb_dp[b], ident_dp], W=[PD[pi]])
                copy(evac_engine(), hT[:, cg * 4:(cg + 1) * 4, s * 128:(s + 1) * 128], pv,
                     R=[PD[pi]], W=[hT_dp[s][cg]])
        p1.close()

        p2 = Ph(kb)
        wst = [p2.sb("wst%d" % i, [128, NCH, 128], F32) for i in range(3)]; wst_dp = [p2.dep() for _ in range(3)]
        wb = [p2.sb("wb%d" % i, [128, NCH, 128], BF16) for i in range(2)]; wb_dp = [p2.dep() for _ in range(2)]
        wt = [p2.sb("wt%d" % i, [128, NCH, 512], BF16) for i in range(2)]; wt_dp = [p2.dep() for _ in range(2)]
        ofm = [p2.sb("ofm%d" % i, [128, 512], BF16) for i in range(3)]; ofm_dp = [p2.dep() for _ in range(3)]
        otm = [p2.sb("otm%d" % i, [128, 512], BF16) for i in range(3)]; otm_dp = [p2.dep() for _ in range(3)]
        sndt = [p2.sb("sndt%d" % k, [128, T], BF16) for k in range(5)]
        sn_dp = [p2.dep() for _ in range(5)]

        def send(k):
            dma([(SND[k], sndt[k][:])], R=[sn_dp[k]], W=[snd_dp[k]], st=sn_dp[k])
            kb.allgather(SND[k], RCV[k], R=[snd_dp[k]], W=[rcv_dp[k]])
        rp = [p2.sb("rp%d" % i, [64, 512], F32) for i in range(2)]; rp_dp = [p2.dep() for _ in range(2)]
        st2 = p2.sb("st2", [128, 8], F32); st2_dp = p2.dep()
        cn = p2.sb("cn", [128, 384], BF16); cn_dp = p2.dep()
        cnb = [p2.sb("cnb%d" % i, [128, 384], BF16) for i in range(2)]; cnb_dp = [p2.dep() for _ in range(2)]
        op("pool", lambda g: g.memset(sndt[2][64:128, :], 0.0), W=[sn_dp[2]])
        cnt = {"w": 0, "ofm": 0, "otm": 0}

        def load_piece(p):
            if p >= NPIECE:
                return
            i = p % 3
            dma([(wst[i][:], win[l, p])], W=[wst_dp[i]], st=wst_dp[i])

        def cast_piece(i, dst, dst_dp, gofs):
            for c in range(NCH):
                op("pool", lambda g: g.tensor_scalar(out=dst[:, c, :], in0=wst[i][:, c, :],
                                                     scalar1=small[:, gofs + c:gofs + c + 1], scalar2=0.0,
                                                     op0=ALU.mult, op1=ALU.add),
                   R=[wst_dp[i], small_dp], W=[dst_dp], nowait=(c > 0))

        load_piece(0)
        load_piece(1)
        for p in range(10):
            i = p % 3
            wi = p % 2
            cast_piece(i, wb[wi], wb_dp[wi], 0)
            load_piece(p + 2)
            for tg in range(NG):
                tsl = slice(tg * 512, (tg + 1) * 512)
                hdeps = [d for blk_ in hT_dp[tg * 4:(tg + 1) * 4] for d in blk_]
                if p < 9:
                    pi = rr["ps"] % 4; rr["ps"] += 1
                    for c in range(NCH):
                        op("pe", lambda g: g.matmul(out=PS[pi][:], lhsT=wb[wi][:, c, :], rhs=hT[:, c, tsl],
                                                    start=(c == 0), stop=(c == NCH - 1)),
                           R=[wb_dp[wi]] + hdeps, W=[PD[pi]])
                    if p < 8:
                        oi = cnt["ofm"] % 3; cnt["ofm"] += 1
                        copy("act", ofm[oi][:], PS[pi][:], R=[PD[pi]], W=[ofm_dp[oi]])
                        dma([(QA[p, :, tsl], ofm[oi][:])], R=[ofm_dp[oi]], W=[qa_dp], st=ofm_dp[oi], q="act")
                    else:
                        copy(evac_engine(), sndt[3][:, tsl], PS[pi][:], R=[PD[pi]], W=[sn_dp[3]])
                else:
                    pis = []
                    for half in range(2):
                        pi = rr["ps"] % 4; rr["ps"] += 1
                        pis.append(pi)
                        for c in range(NCH):
                            op("pe", lambda g: g.matmul(out=PS[pi][0:64, :], lhsT=wb[wi][:, c, half * 64:(half + 1) * 64],
                                                        rhs=hT[:, c, tsl], start=(c == 0), stop=(c == NCH - 1)),
                               R=[wb_dp[wi]] + hdeps, W=[PD[pi]])
                    op("dve", lambda g: g.tensor_tensor(out=rp[0][:], in0=PS[pis[0]][0:64, :], in1=cs[:, 0, tsl], op=ALU.mult),
                       R=[PD[pis[0]], cs_dp], W=[rp_dp[0]])
                    op("dve", lambda g: g.tensor_tensor(out=rp[1][:], in0=PS[pis[1]][0:64, :], in1=cs[:, 1, tsl], op=ALU.mult),
                       R=[PD[pis[1]], cs_dp], W=[rp_dp[1]])
                    op("dve", lambda g: g.tensor_tensor(out=sndt[2][0:64, tsl], in0=rp[0][:], in1=rp[1][:], op=ALU.add),
                       R=[rp_dp[0], rp_dp[1]], W=[sn_dp[2]])
            if p == 8:
                send(3)
            if p == 9:
                send(2)

        groups = [("cqva", [10, 11, 12, 13]), ("ckv", [14, 15]), ("ga0", [16, 17, 18, 19]), ("ga1", [20, 21, 22, 23])]
        for gi, (gname, pieces) in enumerate(groups):
            wi = gi % 2
            ncol = 128 * len(pieces)
            for k, p in enumerate(pieces):
                i = p % 3
                cast_piece(i, wt[wi][:, :, k * 128:(k + 1) * 128], wt_dp[wi], 0)
                load_piece(p + 2)
            deferred = []
            for s in range(NS):
                pi = rr["ps"] % 4; rr["ps"] += 1
                for c in range(NCH):
                    op("pe", lambda g: g.matmul(out=PS[pi][:, 0:ncol], lhsT=hT[:, c, s * 128:(s + 1) * 128],
                                                rhs=wt[wi][:, c, 0:ncol], start=(c == 0), stop=(c == NCH - 1)),
                       R=[wt_dp[wi]] + hT_dp[s], W=[PD[pi]])
                for f in deferred:
                    f()
                deferred = []
                if gname in ("cqva", "ckv"):
                    nr = QL if gname == "cqva" else KVL
                    nck = nr // 128
                    op("act", lambda g: g.activation(out=cn[:, 0:nr], in_=PS[pi][:, 0:nr], func=AF.Square,
                                                     accum_out=st2[:, 0:1]),
                       R=[PD[pi]], W=[cn_dp, st2_dp])
                    op("dve", lambda g: g.tensor_scalar(out=st2[:, 1:2], in0=st2[:, 0:1], scalar1=1.0 / nr,
                                                        scalar2=EPS, op0=ALU.mult, op1=ALU.add), W=[st2_dp])
                    op("act", lambda g: g.activation(out=st2[:, 2:3], in_=st2[:, 1:2], func=AF.Sqrt), W=[st2_dp])
                    op("dve", lambda g: g.reciprocal(out=st2[:, 3:4], in_=st2[:, 2:3]), W=[st2_dp])
                    cb = s % 2
                    op("dve", lambda g: g.tensor_scalar(out=cnb[cb][:, 0:nr], in0=PS[pi][:, 0:nr], scalar1=st2[:, 3:4],
                                                        scalar2=None, op0=ALU.mult),
                       R=[PD[pi], st2_dp], W=[cnb_dp[cb]])
                    if gname == "cqva":
                        copy("act", sndt[4][:, s * 128:(s + 1) * 128], PS[pi][:, 384:512],
                             R=[PD[pi]], W=[sn_dp[4]])
                    def tr(s=s, cb=cb, nck=nck, gname=gname):
                        pj = 4 + (s % 2)
                        pv = bf(PS[pj][:])[:, 0:128 * nck].rearrange("p (a b) -> p a b", a=nck)
                        for a in range(nck):
                            op("pe", lambda g: g.transpose(out=pv[:, a, :], in_=cnb[cb][:, a * 128:(a + 1) * 128], identity=ident[:]),
                               R=[cnb_dp[cb], ident_dp], W=[PD[pj]])
                        if gname == "cqva":
                            copy("dve", cqnT[:, :, s * 128:(s + 1) * 128], pv, R=[PD[pj]], W=[cqnT_dp[s]])
                        else:
                            for a in range(2):
                                copy("dve", sndt[a][:, s * 128:(s + 1) * 128], pv[:, a, :], R=[PD[pj]], W=[sn_dp[a]])
                    deferred.append(tr)
                else:
                    oi = cnt["otm"] % 3; cnt["otm"] += 1
                    op("act", lambda g: g.activation(out=otm[oi][:], in_=PS[pi][:], func=AF.Silu),
                       R=[PD[pi]], W=[otm_dp[oi]])
                    col = (gi - 2) * 512
                    dma([(GT[s * 128:(s + 1) * 128, col:col + 512], otm[oi][:])], R=[otm_dp[oi]], W=[gt_dp[s]],
                        st=otm_dp[oi], q="act")
            for f in deferred:
                f()
            deferred = []
            if gname == "cqva":
                send(4)
            if gname == "ckv":
                send(0)
                send(1)
        for p in range(24, 32):
            i = p % 3
            wi = p % 2
            cast_piece(i, wb[wi], wb_dp[wi], 0)
            load_piece(p + 2)
            for tg in range(NG):
                tsl = slice(tg * 512, (tg + 1) * 512)
                hdeps = [d for blk_ in hT_dp[tg * 4:(tg + 1) * 4] for d in blk_]
                pi = rr["ps"] % 4; rr["ps"] += 1
                for c in range(NCH):
                    op("pe", lambda g: g.matmul(out=PS[pi][:], lhsT=wb[wi][:, c, :], rhs=hT[:, c, tsl],
                                                start=(c == 0), stop=(c == NCH - 1)),
                       R=[wb_dp[wi]] + hdeps, W=[PD[pi]])
                oi = cnt["ofm"] % 3; cnt["ofm"] += 1
                op("act", lambda g: g.activation(out=ofm[oi][:], in_=PS[pi][:], func=AF.Silu),
                   R=[PD[pi]], W=[ofm_dp[oi]])
                dma([(GTB[p - 24, :, tsl], ofm[oi][:])], R=[ofm_dp[oi]], W=[gtb_dp], st=ofm_dp[oi], q="act")
        p2.close()
        pa.close()

        pY = Ph(kb)
        yta = pY.sb("yta", [128, NCH, T], BF16); yta_dp = [pY.dep() for _ in range(NCH)]
        pS = Ph(kb)
        qall = pS.sb("qall", [128, 8, T], BF16); qall_dp = pS.dep()
        kaT = pS.sb("kaT", [128, 2, T], BF16); kaT_dp = pS.dep()
        vraw = pS.sb("vraw", [128, 2, T], BF16); vraw_dp = pS.dep()
        vaug = pS.sb("vaug", [128, 2 * NS * 2, 66], BF16); vaug_dp = pS.dep()
        etab = pS.sb("etab", [128, 3, 16, 128], BF16); etab_dp = pS.dep()
        gat = [pS.sb("gat%d" % i, [128, 1024], BF16) for i in range(2)]; gat_dp = [pS.dep() for _ in range(2)]
        pex = [pS.sb("pex%d" % i, [128, 512], BF16) for i in range(3)]; pex_dp = [pS.dep() for _ in range(3)]
        ptS = [pS.sb("ptS%d" % i, [128, 512], BF16) for i in range(6)]; ptS_dp = [pS.dep() for _ in range(6)]
        ysw = [pS.sb("ysw%d" % i, [128, 1024], BF16) for i in range(2)]; ysw_dp = [pS.dep() for _ in range(2)]
        lS = pS.sb("lS", [128, 8], F32); lS_dp = pS.dep()
        dma([(etab[:], etab_d)], W=[etab_dp], st=etab_dp, q="act")
        dma([(qall[:, j, :], QA[j]) for j in range(8)], R=[qa_dp], W=[qall_dp], st=qall_dp)
        dma([(kaT[:, r, :], RCV[3][r * 128:(r + 1) * 128, :]) for r in range(2)], R=[rcv_dp[3]], W=[kaT_dp], st=kaT_dp, q="act")
        dma([(vraw[:, r, :], RCV[4][r * 128:(r + 1) * 128, :]) for r in range(2)], R=[rcv_dp[4]], W=[vraw_dp], st=vraw_dp, q="act")
        op("pool", lambda g: g.memset(vaug[:], 1.0), W=[vaug_dp])
        for r in range(2):
            dst = vaug[:, r * NS * 2:(r + 1) * NS * 2, 0:64]
            src = vraw[:, r, :].rearrange("p (a d) -> p a d", d=64)
            op("pool", lambda g: g.tensor_copy(out=dst, in_=src), R=[vraw_dp], W=[vaug_dp])
        cS = {"pex": 0, "pt": 0}
        units = [(s, gk, quad) for s in range(NS) for gk in range(2) for quad in range(2)]
        s1out = {}

        def stage1(u):
            s, gk, quad = units[u]
            b = s % 2
            if gk == 0 and quad == 0:
                dma([(gat[b][:], GT[s * 128:(s + 1) * 128, 0:1024])], R=[gt_dp[s]], W=[gat_dp[b]], st=gat_dp[b])
            cands = [(0, 1, s - 1), (1, 0, s), (2, 1, s)]
            if s == 0:
                cands = cands[1:]
            prt = slice(gk * 64, (gk + 1) * 64)
            h0 = gk * 8 + quad * 4
            pts = []
            for (ci, r, ks) in cands:
                pi = rr["ps"] % 2; rr["ps"] += 1
                op("pe", lambda g: g.matmul(out=PS[pi][:], lhsT=kaT[prt, r, ks * 128:(ks + 1) * 128],
                                            rhs=qall[prt, quad * 4:(quad + 1) * 4, s * 128:(s + 1) * 128],
                                            start=True, stop=True),
                   R=[kaT_dp, qall_dp], W=[PD[pi]])
                xi = cS["pex"] % 3; cS["pex"] += 1
                op("act", lambda g: g.activation(out=pex[xi][:], in_=PS[pi][:], func=AF.Exp, scale=0.125),
                   R=[PD[pi]], W=[pex_dp[xi]])
                ti = cS["pt"] % 6; cS["pt"] += 1
                op("dve" if ti % 2 == 0 else "pool", lambda g: g.tensor_tensor(out=ptS[ti][:].rearrange("p (a b) -> p a b", a=4),
                                                    in0=pex[xi][:].rearrange("p (a b) -> p a b", a=4),
                                                    in1=etab[:, ci, h0:h0 + 4, :], op=ALU.mult),
                   R=[pex_dp[xi], etab_dp], W=[ptS_dp[ti]])
                pts.append((ti, r, ks))
            s1out[u] = pts

        def stage2(u):
            s, gk, quad = units[u]
            b = s % 2
            pts = s1out.pop(u)
            oi = 2 + gk * 2 + quad
            h0 = gk * 8 + quad * 4
            ov = PS[oi][:, 0:4 * 65].rearrange("p (a b) -> p a b", a=4)
            for hq in range(4):
                for n, (ti, r, ks) in enumerate(pts):
                    op("pe", lambda g: g.matmul(out=ov[:, hq, :], lhsT=ptS[ti][:, hq * 128:(hq + 1) * 128],
                                                rhs=vaug[:, (r * NS + ks) * 2 + gk, 0:65],
                                                start=(n == 0), stop=(n == len(pts) - 1)),
                       R=[ptS_dp[ti], vaug_dp], W=[PD[oi]])
            op("dve", lambda g: g.tensor_tensor(out=lS[:, 0:4], in0=ov[:, :, 64], in1=small[:, 40 + h0:44 + h0], op=ALU.add),
               R=[PD[oi], esink_dp], W=[lS_dp])
            op("dve", lambda g: g.reciprocal(out=lS[:, 4:8], in_=lS[:, 0:4]), W=[lS_dp])
            for hq in range(4):
                h = h0 + hq
                op("dve", lambda g: g.scalar_tensor_tensor(out=ysw[b][:, h * 64:(h + 1) * 64], in0=ov[:, hq, 0:64],
                                                            scalar=lS[:, 4 + hq:5 + hq], in1=gat[b][:, h * 64:(h + 1) * 64],
                                                            op0=ALU.mult, op1=ALU.mult),
                   R=[PD[oi], lS_dp, gat_dp[b]], W=[ysw_dp[b]], nowait=(hq > 0))
            if gk == 1 and quad == 1:
                pj = 6 + (s % 2)
                pv = bf(PS[pj][:]).rearrange("p (a b) -> p a b", a=8)
                for a in range(8):
                    op("pe", lambda g: g.transpose(out=pv[:, a, :], in_=ysw[b][:, a * 128:(a + 1) * 128], identity=ident[:]),
                       R=[ysw_dp[b], ident_dp], W=[PD[pj]])
                copy("act", yta[:, 0:8, s * 128:(s + 1) * 128], pv, R=[PD[pj]], W=yta_dp[0:8])

        stage1(0)
        for u in range(len(units)):
            if u + 1 < len(units):
                stage1(u + 1)
            stage2(u)
        pS.close()

        pC = Ph(kb)
        ck = pC.sb("ck", [128, 2, 2, T], BF16); ck_dp = pC.dep()
        kr = pC.sb("kr", [64, 2, T], BF16); kr_dp = pC.dep()
        wq = pC.sb("wq", [128, 3, 2048], BF16); wq_dp = pC.dep()
        wkv = pC.sb("wkv", [128, 2, 2048], BF16); wkv_dp = pC.dep()
        wst2 = [pC.sb("wsc%d" % i, [128, 1024], F32) for i in range(2)]; wst2_dp = [pC.dep() for _ in range(2)]
        khT = [pC.sb("khT%d" % i, [128, 2 * T], BF16) for i in range(2)]; khT_dp = [pC.dep() for _ in range(2)]
        vh = [pC.sb("vh%d" % i, [128, 2 * NS, 130], BF16) for i in range(2)]; vh_dp = [pC.dep() for _ in range(2)]
        gbt = [pC.sb("gbt%d" % i, [128, 512], BF16) for i in range(2)]; gbt_dp = [pC.dep() for _ in range(2)]
        onesb = pC.sb("onesb", [128, 128], BF16); onesb_dp = pC.dep()
        rLt = [pC.sb("rLt%d" % i, [128, 512], F32) for i in range(2)]; rLt_dp = [pC.dep() for _ in range(2)]
        op("pool", lambda g: g.memset(onesb[:], 1.0), W=[onesb_dp])
        qn = [pC.sb("qn%d" % i, [128, 512], BF16) for i in range(2)]; qn_dp = [pC.dep() for _ in range(2)]
        qr = [pC.sb("qr%d" % i, [64, 512], BF16) for i in range(2)]; qr_dp = [pC.dep() for _ in range(2)]
        rq = [pC.sb("rq%d" % i, [64, 512], F32) for i in range(2)]; rq_dp = [pC.dep() for _ in range(2)]
        ptC = [pC.sb("ptC%d" % i, [128, 512], BF16) for i in range(4)]; ptC_dp = [pC.dep() for _ in range(4)]
        for r in range(2):
            dma([(ck[:, c, r, :], RCV[c][r * 128:(r + 1) * 128, :]) for c in range(2)]
                + [(kr[:, r, :], RCV[2][r * 128:r * 128 + 64, :])],
                R=rcv_dp[0:3], W=[ck_dp, kr_dp], st=ck_dp if r == 0 else kr_dp, q="act")
        nst = 0
        for (wsrc, wdst, wdp, nchk, gofs) in ((wkv_d, wkv, wkv_dp, 2, 19), (wq_d, wq, wq_dp, 3, 16)):
            for c in range(nchk):
                for hf in range(2):
                    i = nst % 2; nst += 1
                    csl = slice(hf * 1024, (hf + 1) * 1024)
                    dma([(wst2[i][:], wsrc[l, :, c, csl])], W=[wst2_dp[i]], st=wst2_dp[i])
                    op("pool", lambda g: g.tensor_scalar(out=wdst[:, c, csl], in0=wst2[i][:], scalar1=small[:, gofs + c:gofs + c + 1],
                                                         scalar2=0.0, op0=ALU.mult, op1=ALU.add),
                       R=[wst2_dp[i], small_dp], W=[wdp])
        for i in range(2):
            op("pool", lambda g: g.memset(vh[i][:, :, 128:130], 1.0), W=[vh_dp[i]])
        accv = [wst2[0][:, 0:512], wst2[0][:, 512:1024]]
        accb_all = wst2[1][:].bitcast(BF16)
        accb = [accb_all[:, 0:512], accb_all[:, 512:1024]]
        acc_dp = [pC.dep() for _ in range(2)]; accb_dp = [pC.dep() for _ in range(2)]
        acc_first = [True, True]; accb_first = [True, True]
        acc_eng = ["dve", "pool"]
        pend_fin = []
        cC = {"pt": 0, "g": 0, "s": 0}
        scale = float((128 + 64) ** -0.5)
        NKS = 2 * NS

        def misc_bank():
            pi = 5 + rr["ps"] % 3; rr["ps"] += 1
            return pi

        def prep_chunks(h):
            hb_ = h % 2
            out = []
            for kg in range(2 * T // 512):
                def f(kg=kg):
                    r, t0 = divmod(kg * 512, T)
                    pi = misc_bank()
                    for c in range(2):
                        op("pe", lambda g: g.matmul(out=PS[pi][:], lhsT=wkv[:, c, h * 256:h * 256 + 128],
                                                    rhs=ck[:, c, r, t0:t0 + 512], start=(c == 0), stop=(c == 1)),
                           R=[wkv_dp, ck_dp], W=[PD[pi]])
                    copy("dve", khT[hb_][:, kg * 512:(kg + 1) * 512], PS[pi][:], R=[PD[pi]], W=[khT_dp[hb_]])
                out.append(f)
            for k4 in range(NKS // 4):
                def f(k4=k4):
                    pi = misc_bank()
                    for a in range(4):
                        ksl = k4 * 4 + a
                        r, s_ = divmod(ksl, NS)
                        for c in range(2):
                            op("pe", lambda g: g.matmul(out=PS[pi][:, a * 128:(a + 1) * 128],
                                                        lhsT=ck[:, c, r, s_ * 128:(s_ + 1) * 128],
                                                        rhs=wkv[:, c, h * 256 + 128:h * 256 + 256],
                                                        start=(c == 0), stop=(c == 1)),
                               R=[wkv_dp, ck_dp], W=[PD[pi]])
                    copy("dve", vh[hb_][:, k4 * 4:(k4 + 1) * 4, 0:128],
                         PS[pi][:].rearrange("p (a b) -> p a b", a=4), R=[PD[pi]], W=[vh_dp[hb_]])
                out.append(f)
            return out

        def qproj_chunks(h, G):
            gb_ = (h * NG + G) % 2
            tsl = slice(G * 512, (G + 1) * 512)
            cdeps = cqnT_dp[G * 4:(G + 1) * 4]

            def f0():
                dma([(gbt[gb_][:], GTB[h, :, tsl])], R=[gtb_dp], W=[gbt_dp[gb_]], st=gbt_dp[gb_])
                pi = misc_bank()
                for c in range(3):
                    op("pe", lambda g: g.matmul(out=PS[pi][:], lhsT=wq[:, c, h * 256:h * 256 + 128], rhs=cqnT[:, c, tsl],
                                                start=(c == 0), stop=(c == 2)),
                       R=[wq_dp] + cdeps, W=[PD[pi]])
                copy("dve", qn[gb_][:], PS[pi][:], R=[PD[pi]], W=[qn_dp[gb_]])

            def fr(half):
                pi = misc_bank()
                c0 = h * 256 + 128 + half * 64
                for c in range(3):
                    op("pe", lambda g: g.matmul(out=PS[pi][0:64, :], lhsT=wq[:, c, c0:c0 + 64], rhs=cqnT[:, c, tsl],
                                                start=(c == 0), stop=(c == 2)),
                       R=[wq_dp] + cdeps, W=[PD[pi]])
                op("dve", lambda g: g.tensor_tensor(out=rq[half][:], in0=PS[pi][0:64, :], in1=cs[:, half, tsl], op=ALU.mult),
                   R=[PD[pi], cs_dp], W=[rq_dp[half]])
                if half == 1:
                    op("dve", lambda g: g.tensor_tensor(out=qr[gb_][:], in0=rq[0][:], in1=rq[1][:], op=ALU.add),
                       R=[rq_dp[0], rq_dp[1]], W=[qr_dp[gb_]])
            return [f0, lambda: fr(0), lambda: fr(1)]

        qq = []
        pq = []
        for f in prep_chunks(0):
            f()
        for f in qproj_chunks(0, 0):
            f()
        for h in range(8):
            hb_ = h % 2
            if h + 1 < 8:
                pq = prep_chunks(h + 1)
            for G in range(NG):
                gb_ = (h * NG + G) % 2
                tsl = slice(G * 512, (G + 1) * 512)
                if G + 1 < NG:
                    qq = qproj_chunks(h, G + 1)
                elif h + 1 < 8:
                    qq = qproj_chunks(h + 1, 0)
                ents = [(r, s_, 0, False) for s_ in range(4 * G) for r in range(2)]
                ents += [(r, 4 * G + j, j, True) for j in range(4) for r in range(2)]
                NE = len(ents)
                st_ = {}

                def emit_S(n):
                    r, s_, jmin, msk = ents[n]
                    ncol = (4 - jmin) * 128
                    pi = cC["s"] % 3; cC["s"] += 1
                    kcol = r * T + s_ * 128
                    op("pe", lambda g: g.matmul(out=PS[pi][:, 0:ncol], lhsT=khT[hb_][:, kcol:kcol + 128],
                                                rhs=qn[gb_][:, jmin * 128:512], start=True, stop=False),
                       R=[khT_dp[hb_], qn_dp[gb_]], W=[PD[pi]])
                    op("pe", lambda g: g.matmul(out=PS[pi][:, 0:ncol], lhsT=kr[:, r, s_ * 128:(s_ + 1) * 128],
                                                rhs=qr[gb_][:, jmin * 128:512], start=False, stop=True),
                       R=[kr_dp, qr_dp[gb_]], W=[PD[pi]])
                    ti = cC["pt"] % 4; cC["pt"] += 1
                    op("act", lambda g: g.activation(out=ptC[ti][:, 0:ncol], in_=PS[pi][:, 0:ncol], func=AF.Exp, scale=scale),
                       R=[PD[pi]], W=[ptC_dp[ti]])
                    if msk:
                        op("dve", lambda g: g.tensor_tensor(out=ptC[ti][:, 0:128], in0=ptC[ti][:, 0:128], in1=mask[:, r, :], op=ALU.mult),
                           R=[mask_dp], W=[ptC_dp[ti]])
                    st_[n] = ti

                bO = 3 + gb_

                def emit_PV(n):
                    r, s_, jmin, msk = ents[n]
                    ti = st_[n]
                    ncol = (4 - jmin) * 128
                    op("pe", lambda g: g.matmul(out=PS[bO][:, jmin * 128:512], lhsT=vh[hb_][:, r * NS + s_, 0:128],
                                                rhs=ptC[ti][:, 0:ncol], start=(n == 0), stop=(n == NE - 1)),
                       R=[ptC_dp[ti], vh_dp[hb_]], W=[PD[bO]])
                    a_ = n % 2
                    if n < 2:
                        extra = [wst2_dp[0]] if acc_first[a_] else []
                        acc_first[a_] = False
                        op(acc_eng[a_], lambda g: g.tensor_copy(out=accv[a_], in_=ptC[ti][:, 0:512]),
                           R=[ptC_dp[ti]], W=[acc_dp[a_]] + extra)
                    else:
                        op(acc_eng[a_], lambda g: g.tensor_tensor(out=accv[a_][:, jmin * 128:512], in0=accv[a_][:, jmin * 128:512],
                                                                  in1=ptC[ti][:, 0:ncol], op=ALU.add),
                           R=[ptC_dp[ti]], W=[acc_dp[a_]])

                emit_S(0)
                emit_S(1)
                while pend_fin:
                    pend_fin.pop(0)()
                for n in range(NE):
                    if n + 2 < NE:
                        emit_S(n + 2)
                    emit_PV(n)
                    if qq:
                        qq.pop(0)()
                    elif pq:
                        pq.pop(0)()
                while qq:
                    qq.pop(0)()
                for a_ in range(2):
                    extra = [wst2_dp[1]] if accb_first[a_] else []
                    accb_first[a_] = False
                    op(acc_eng[a_], lambda g: g.tensor_copy(out=accb[a_], in_=accv[a_]),
                       R=[acc_dp[a_]], W=[accb_dp[a_]] + extra)

                def fin(h=h, gb_=gb_, tsl=tsl, bO=bO):
                    pi = misc_bank()
                    for a_ in range(2):
                        op("pe", lambda g: g.matmul(out=PS[pi][:], lhsT=onesb[:], rhs=accb[a_],
                                                    start=(a_ == 0), stop=(a_ == 1)),
                           R=[accb_dp[a_], onesb_dp], W=[PD[pi]])
                    op("dve", lambda g: g.reciprocal(out=rLt[gb_][:], in_=PS[pi][:]), R=[PD[pi]], W=[rLt_dp[gb_]])
                    op("dve", lambda g: g.tensor_tensor(out=rLt[gb_][:], in0=PS[bO][:], in1=rLt[gb_][:], op=ALU.mult),
                       R=[PD[bO]], W=[rLt_dp[gb_]])
                    op("dve", lambda g: g.tensor_tensor(out=yta[:, 8 + h, tsl], in0=rLt[gb_][:], in1=gbt[gb_][:], op=ALU.mult),
                       R=[rLt_dp[gb_], gbt_dp[gb_]], W=[yta_dp[8 + h]])
                pend_fin.append(fin)
            while pq:
                pq.pop(0)()
        while pend_fin:
            pend_fin.pop(0)()
        pC.close()

        pD = Ph(kb)
        wsd = [pD.sb("wsd%d" % i, [128, NCH, 128], F32) for i in range(2)]; wsd_dp = [pD.dep() for _ in range(2)]
        wo = [pD.sb("wo%d" % i, [128, NCH, 512], BF16) for i in range(2)]; wo_dp = [pD.dep() for _ in range(2)]
        xr = [pD.sb("xr%d" % i, [128, 512], F32) for i in range(3)]; xr_dp = [pD.dep() for _ in range(3)]
        xo = [pD.sb("xo%d" % i, [128, 512], F32) for i in range(3)]; xo_dp = [pD.dep() for _ in range(3)]
        cD = {"w": 0, "x": 0}
        def load_wo(cg):
            wi = cg % 2
            for k in range(4):
                i = cD["w"] % 2; cD["w"] += 1
                dma([(wsd[i][:], wo_d[l, cg * 4 + k])], W=[wsd_dp[i]], st=wsd_dp[i])
                op("pool", lambda g: g.tensor_copy(out=wo[wi][:, :, k * 128:(k + 1) * 128], in_=wsd[i][:]),
                   R=[wsd_dp[i]], W=[wo_dp[wi]])

        load_wo(0)
        for cg in range(4):
            wi = cg % 2
            if cg + 1 < 4:
                load_wo(cg + 1)
            for s in range(NS):
                xi = cD["x"] % 3; cD["x"] += 1
                dma([(xr[xi][:], x_src[s * 128:(s + 1) * 128, cg * 512:(cg + 1) * 512])], R=[x_src_dp[s]], W=[xr_dp[xi]], st=xr_dp[xi])
                pi = rr["ps"] % 4; rr["ps"] += 1
                for c in range(NCH):
                    op("pe", lambda g: g.matmul(out=PS[pi][:], lhsT=yta[:, c, s * 128:(s + 1) * 128], rhs=wo[wi][:, c, :],
                                                start=(c == 0), stop=(c == NCH - 1)),
                       R=yta_dp + [wo_dp[wi]], W=[PD[pi]])
                op("dve", lambda g: g.tensor_tensor(out=xo[xi][:], in0=PS[pi][:], in1=xr[xi][:], op=ALU.add),
                   R=[PD[pi], xr_dp[xi]], W=[xo_dp[xi]])
                dma([(x_dst[s * 128:(s + 1) * 128, cg * 512:(cg + 1) * 512], xo[xi][:])], R=[xo_dp[xi]], W=[x_dst_dp[s]], st=xo_dp[xi], q="act")
        pD.close()
        pY.close()

    pF = Ph(kb)
    x_src = XS[(L - 1) % 2]
    x_src_dp = x_dp[1 + (L - 1) % 2]
    gfin = pF.sb("gfin", [128, D], F32); gfin_dp = pF.dep()
    xs = [pF.sb("fx%d" % i, [128, D], F32) for i in range(2)]; xs_dp = [pF.dep() for _ in range(2)]
    fo = [pF.sb("fo%d" % i, [128, D], F32) for i in range(2)]; fo_dp = [pF.dep() for _ in range(2)]
    junk = pF.sb("fjunk", [128, D], BF16); junk_dp = pF.dep()
    st1 = pF.sb("fst", [128, 4 * NS], F32); st1_dp = [pF.dep() for _ in range(NS)]
    out_dp = Dep()
    dma([(gfin[:], gfin_d)], W=[gfin_dp], st=gfin_dp)
    for s in range(NS):
        b = s % 2
        dma([(xs[b][:], x_src[s * 128:(s + 1) * 128, :])], R=[x_src_dp[s]], W=[xs_dp[b]], st=xs_dp[b])
        c0 = 4 * s
        op("act", lambda g: g.activation(out=junk[:], in_=xs[b][:], func=AF.Square, accum_out=st1[:, c0:c0 + 1]),
           R=[xs_dp[b]], W=[junk_dp, st1_dp[s]])
        op("dve", lambda g: g.tensor_scalar(out=st1[:, c0 + 1:c0 + 2], in0=st1[:, c0:c0 + 1], scalar1=1.0 / D, scalar2=EPS,
                                            op0=ALU.mult, op1=ALU.add), W=[st1_dp[s]])
        op("act", lambda g: g.activation(out=st1[:, c0 + 2:c0 + 3], in_=st1[:, c0 + 1:c0 + 2], func=AF.Sqrt), W=[st1_dp[s]])
        op("dve", lambda g: g.reciprocal(out=st1[:, c0 + 3:c0 + 4], in_=st1[:, c0 + 2:c0 + 3]), W=[st1_dp[s]])
        op("dve", lambda g: g.scalar_tensor_tensor(out=fo[b][:], in0=xs[b][:], scalar=st1[:, c0 + 3:c0 + 4], in1=gfin[:],
                                                    op0=ALU.mult, op1=ALU.mult),
           R=[xs_dp[b], st1_dp[s], gfin_dp], W=[fo_dp[b]])
        dma([(out_d[s * 128:(s + 1) * 128, :], fo[b][:])], R=[fo_dp[b]], W=[out_dp], st=fo_dp[b], q="act")
    pF.close()
    gp.es.close()
    es.close()
    return nc


def _host_inputs(x, attn_norm_g, w_in, swa_sinks, q_a_norm_g, kv_a_norm_g, w_q_b, w_kv_b, w_out, final_norm_g, NS):
    L = w_in.shape[0]
    B = x.shape[0]
    T = NS * 128
    f32 = np.float32
    A_Q, A_KV, A_G = 1024, 128, 1024
    o_qa, o_ka, o_va, o_ga = 0, A_Q, A_Q + A_KV, A_Q + 2 * A_KV
    o_cq = o_ga + A_G
    o_ckv = o_cq + QL
    o_kr = o_ckv + KVL
    o_gb = o_kr + 64
    qa_cols = []
    for j in range(8):
        for half in range(2):
            hd = j + 8 * half
            qa_cols += list(range(o_qa + hd * 64, o_qa + (hd + 1) * 64))
    kr_cols = list(range(o_kr, o_kr + 64))
    kr_sw = list(range(o_kr + 32, o_kr + 64)) + list(range(o_kr, o_kr + 32))
    cols = (qa_cols + list(range(o_ka, o_ka + 128)) + kr_cols + kr_sw + list(range(o_cq, o_cq + QL))
            + list(range(o_va, o_va + 128)) + list(range(o_ckv, o_ckv + KVL)) + list(range(o_ga, o_ga + A_G))
            + list(range(o_gb, o_gb + 1024)))
    cols = np.asarray(cols)
    assert cols.size == NPIECE * 128
    wp = w_in[:, :, cols]
    win = np.ascontiguousarray(wp.reshape(L, NCH, 128, NPIECE, 128).transpose(0, 3, 2, 1, 4))
    qcols = []
    for h in range(8):
        b0 = h * 192
        qcols += list(range(b0, b0 + 192)) + list(range(b0 + 160, b0 + 192)) + list(range(b0 + 128, b0 + 160))
    wqp = w_q_b[:, :, np.asarray(qcols)]
    wq = np.ascontiguousarray(wqp.reshape(L, 3, 128, 2048).transpose(0, 2, 1, 3))
    wkv = np.ascontiguousarray(w_kv_b.reshape(L, 2, 128, 2048).transpose(0, 2, 1, 3))
    wo = np.ascontiguousarray(w_out.reshape(L, NCH, 128, 16, 128).transpose(0, 3, 2, 1, 4))
    gin = np.ascontiguousarray(attn_norm_g.reshape(L, NCH, 128).transpose(0, 2, 1))
    gq = np.ascontiguousarray(q_a_norm_g.reshape(L, 3, 128).transpose(0, 2, 1))
    gkv = np.ascontiguousarray(kv_a_norm_g.reshape(L, 2, 128).transpose(0, 2, 1))
    sinks = np.ascontiguousarray(np.broadcast_to(swa_sinks[:, None, :], (L, 128, 16)))
    gfin = np.ascontiguousarray(np.broadcast_to(final_norm_g[None, :], (128, D)))
    ident = np.eye(128, dtype=f32).astype(ml_dtypes.bfloat16)
    S_full = 2 * T
    pos = np.arange(S_full, dtype=f32)
    inv_freq = (10000.0 ** (-np.arange(0, 64, 2, dtype=f32) / 64)).astype(f32)
    ang = pos[:, None] * inv_freq[None, :]
    cos, sin = np.cos(ang).astype(f32), np.sin(ang).astype(f32)
    cos2 = np.concatenate([cos, cos], 1).T
    sin2 = np.concatenate([-sin, sin], 1).T
    slopes = np.exp2(-8.0 * np.arange(1, 17, dtype=f32) / 16).astype(f32)
    kk = np.arange(128)[:, None]
    qq = np.arange(128)[None, :]
    d_prev = (128 + qq - kk).astype(f32)
    d_cur = (qq - kk).astype(f32)
    E_prev = np.where((kk > qq)[:, None, :], np.exp(-slopes[None, :, None] * np.clip(d_prev, 0, 128)[:, None, :]), 0.0).astype(f32)
    E_cur = np.where((kk <= qq)[:, None, :], np.exp(-slopes[None, :, None] * np.clip(d_cur, 0, 128)[:, None, :]), 0.0).astype(f32)
    Z = np.zeros_like(E_prev)
    tri = (kk <= qq).astype(f32)
    ones = np.ones_like(tri)
    zer = np.zeros_like(tri)
    in_maps = []
    for c in range(2 * B):
        b, r = divmod(c, 2)
        xb = x[b].reshape(2 * NS, 128, D)[r::2].reshape(T, D)
        tok = (np.arange(NS)[:, None] * 2 + r) * 128 + np.arange(128)[None, :]
        tok = tok.reshape(-1)
        cs = np.ascontiguousarray(np.stack([cos2[:, tok], sin2[:, tok]], 0))
        if r == 0:
            et = np.stack([E_prev, E_cur, Z], 1)
            mk = np.stack([tri, zer], 1)
        else:
            et = np.stack([Z, E_prev, E_cur], 1)
            mk = np.stack([ones, tri], 1)
        in_maps.append({
            "x": np.ascontiguousarray(xb), "win": win, "wq": wq, "wkv": wkv, "wo": wo, "gin": gin, "gq": gq,
            "gkv": gkv, "sinks": sinks, "gfin": gfin, "cs": cs.astype(f32),
            "etab": np.ascontiguousarray(et).astype(ml_dtypes.bfloat16),
            "mask": np.ascontiguousarray(mk).astype(ml_dtypes.bfloat16), "ident": ident,
        })
    return in_maps


def _assemble(results, B, NS):
    T = NS * 128
    out = np.empty((B, 2 * NS, 128, D), np.float32)
    for c in range(2 * B):
        b, r = divmod(c, 2)
        out[b, r::2] = np.asarray(results[c]["out"]).reshape(NS, 128, D)
    return out.reshape(B, 2 * T, D)


_NC_CACHE = {}


def kernel(x, attn_norm_g, w_in, swa_sinks, q_a_norm_g, kv_a_norm_g, w_q_b, w_kv_b, w_out, final_norm_g):
    args = [np.asarray(a, dtype=np.float32) for a in (x, attn_norm_g, w_in, swa_sinks, q_a_norm_g, kv_a_norm_g,
                                                      w_q_b, w_kv_b, w_out, final_norm_g)]
    B, S = args[0].shape[0], args[0].shape[1]
    NS = S // 256
    L = args[2].shape[0]
    in_maps = _host_inputs(*args, NS=NS)
    key = (NS, L)
    if key not in _NC_CACHE:
        _NC_CACHE[key] = build(NS, L)
    res = run_bass_kernel_spmd(_NC_CACHE[key], in_maps, core_ids=list(range(2 * B)))
    return _assemble(res.results, B, NS)
```
